# Optimizing a Trainium2 kernel written in Bass

```python
import math
import jax, jax.numpy as jnp
from jax import lax
import numpy as np

D_MODEL = 1024
BATCH = 32
SEQ = 2048
DEPTH = 2

CTX_LEN = 256
GRID_W = 64
RMS_EPS = 1e-6

D_HY = 512
HY_ORDER = 2
HY_DIRS = 2
HY_SHORT = 3
HY_BANDS = 8
HY_EMB = 1 + 2 * HY_BANDS
HY_FH = 64
HY_DECAY_TARGET = 1e-2
HY_FAST_DECAY = 0.3
HY_SLOW_DECAY = 1.5
HY_FILTER_SCALE = 0.02

GLA_HEADS = 4
GLA_DK = 64
GLA_DV = 128
D_GLA_K = GLA_HEADS * GLA_DK
D_GLA_V = GLA_HEADS * GLA_DV
GLA_GATE_RANK = 16
GLA_GATE_NORM = 16.0
GLA_CHUNK = 64

POOL_WINDOWS = (2, 4, 8, 16)
POOL_GROUPS = 4
POOL_GROUP = 128
D_POOL = POOL_GROUPS * POOL_GROUP

D_FF = 4 * D_MODEL
N_BRANCH = 3

HY_COLS = 3 * D_HY
GLA_COLS = 2 * D_GLA_K + 2 * D_GLA_V + 2 * GLA_GATE_RANK
POOL_COLS = D_POOL
GATE_COLS = N_BRANCH * D_MODEL
GLA_OFF = HY_COLS
POOL_OFF = GLA_OFF + GLA_COLS
GATE_OFF = POOL_OFF + POOL_COLS
N_IN = GATE_OFF + GATE_COLS

kernel_name = "hybrid_hyena_gla_pool_dit_trunk"


def rms_norm(x, g):
    xf = x.astype(jnp.float32)
    y = xf * lax.rsqrt(jnp.mean(xf * xf, axis=-1, keepdims=True) + RMS_EPS)
    return (y * g.astype(jnp.float32)).astype(x.dtype)


def ada_norm(x, g, shift, scale):
    return rms_norm(x, g) * (1.0 + scale) + shift


def short_conv(u, w, b):
    k = w.shape[0]
    pad = k // 2
    y = lax.conv_general_dilated(u, w[:, None, :].astype(u.dtype), window_strides=(1,),
                                 padding=((pad, k - 1 - pad),),
                                 dimension_numbers=("NWC", "WIO", "NWC"),
                                 feature_group_count=u.shape[-1])
    return y + b


def hyena_filters(L, w1, b1, freq, w2, b2, w3):
    f32 = jnp.float32
    t = jnp.arange(L, dtype=f32)[:, None]
    bands = jnp.arange(1, HY_BANDS + 1, dtype=f32)[None, :]
    ang = (2.0 * math.pi / L) * t * bands
    z = jnp.concatenate([t / L, jnp.cos(ang), jnp.sin(ang)], axis=-1)
    a = jnp.sin(freq * (z @ w1 + b1))
    a = jnp.sin(freq * (a @ w2 + b2))
    h = (a @ w3).astype(f32).reshape(L, HY_DIRS, HY_ORDER, D_HY)
    deltas = jnp.abs(jnp.linspace(math.log(HY_DECAY_TARGET) / HY_SLOW_DECAY,
                                  math.log(HY_DECAY_TARGET) / HY_FAST_DECAY, D_HY, dtype=f32))
    window = jnp.exp(-(t / L) * deltas[None, :])
    return h * window[:, None, None, :]


def bidir_long_conv(u, h_fwd, h_bwd, skip):
    L = u.shape[1]
    n = 2 * L
    k = jnp.concatenate([h_fwd, jnp.zeros_like(h_fwd[:1]), h_bwd[:0:-1]], axis=0)
    kf = jnp.fft.rfft(k, n=n, axis=0)
    uf32 = u.astype(jnp.float32)
    uf = jnp.fft.rfft(uf32, n=n, axis=1)
    y = jnp.fft.irfft(uf * kf[None], n=n, axis=1)[:, :L]
    return (y + uf32 * skip.astype(jnp.float32)).astype(u.dtype)


def hyena_branch(u, p):
    L = u.shape[1]
    u = short_conv(u, p["hy_short_w"], p["hy_short_b"])
    x1, x2, v = jnp.split(u, 3, axis=-1)
    h = hyena_filters(L, p["hy_f_w1"], p["hy_f_b1"], p["hy_f_freq"], p["hy_f_w2"], p["hy_f_b2"], p["hy_f_w3"])
    z = x1 * bidir_long_conv(v, h[:, 0, 0], h[:, 1, 0], p["hy_skip"][0])
    z = x2 * bidir_long_conv(z, h[:, 0, 1], h[:, 1, 1], p["hy_skip"][1])
    return z


def gla_chunked(q, k, v, log_a, s0):
    B, L, H, DK = q.shape
    DV = v.shape[-1]
    C = GLA_CHUNK
    N = L // C
    q = q.reshape(B, N, C, H, DK)
    k = k.reshape(B, N, C, H, DK)
    v = v.reshape(B, N, C, H, DV)
    b = jnp.cumsum(log_a.reshape(B, N, C, H, DK), axis=2)
    b_last = b[:, :, -1:]
    q_dec = q * jnp.exp(b)
    k_inv = k * jnp.exp(-b)
    k_end = k * jnp.exp(b_last - b)
    tri = jnp.tril(jnp.ones((C, C), jnp.float32))
    scores = jnp.einsum("bnihd,bnjhd->bnhij", q_dec, k_inv) * tri
    o = jnp.einsum("bnhij,bnjhv->bnihv", scores, v)
    kv = jnp.einsum("bnjhd,bnjhv->bnhdv", k_end, v)
    decay = jnp.exp(b_last[:, :, 0])

    def step(s, inp):
        dec, kv_n = inp
        return dec[..., None] * s + kv_n, s

    s_fin, s_prev = lax.scan(step, s0, (jnp.moveaxis(decay, 1, 0), jnp.moveaxis(kv, 1, 0)))
    o = o + jnp.einsum("bnihd,nbhdv->bnihv", q_dec, s_prev)
    return o.reshape(B, L, H, DV), s_fin


def gla_branch(u, p, s_f0, s_b0):
    f32 = jnp.float32
    B, L, _ = u.shape
    idx = (D_GLA_K, 2 * D_GLA_K, 2 * D_GLA_K + D_GLA_V, 2 * D_GLA_K + 2 * D_GLA_V,
           2 * D_GLA_K + 2 * D_GLA_V + GLA_GATE_RANK)
    q, k, v, r, gf, gb = jnp.split(u, idx, axis=-1)
    q = q.reshape(B, L, GLA_HEADS, GLA_DK).astype(f32) * (GLA_DK ** -0.5)
    k = k.reshape(B, L, GLA_HEADS, GLA_DK).astype(f32)
    v = v.reshape(B, L, GLA_HEADS, GLA_DV).astype(f32)
    la_f = (jax.nn.log_sigmoid((gf @ p["gla_wa_f"] + p["gla_ba_f"]).astype(f32)) / GLA_GATE_NORM
            ).reshape(B, L, GLA_HEADS, GLA_DK)
    la_b = (jax.nn.log_sigmoid((gb @ p["gla_wa_b"] + p["gla_ba_b"]).astype(f32)) / GLA_GATE_NORM
            ).reshape(B, L, GLA_HEADS, GLA_DK)
    o_f, s_f = gla_chunked(q, k, v, la_f, s_f0)
    o_b, s_b = gla_chunked(q[:, ::-1], k[:, ::-1], v[:, ::-1], la_b[:, ::-1], s_b0)
    o = o_f + o_b[:, ::-1]
    o = o * lax.rsqrt(jnp.mean(o * o, axis=-1, keepdims=True) + RMS_EPS) * p["gla_norm_w"].astype(f32)
    y = o.reshape(B, L, D_GLA_V).astype(u.dtype) * jax.nn.silu(r)
    return y, s_f, s_b


def window_bounds(n, w):
    t = jnp.arange(n)
    return jnp.clip(t - w // 2, 0, n), jnp.clip(t - w // 2 + w, 0, n)


def mean_pool_1d(u, w):
    L = u.shape[1]
    cs = jnp.pad(jnp.cumsum(u, axis=1), ((0, 0), (1, 0), (0, 0)))
    lo, hi = window_bounds(L, w)
    s = jnp.take(cs, hi, axis=1) - jnp.take(cs, lo, axis=1)
    return s / (hi - lo).astype(u.dtype)[None, :, None]


def mean_pool_2d(u, w):
    R, W = u.shape[1], u.shape[2]
    sat = jnp.pad(jnp.cumsum(jnp.cumsum(u, axis=1), axis=2), ((0, 0), (1, 0), (1, 0), (0, 0)))
    rl, rh = window_bounds(R, w)
    cl, ch = window_bounds(W, w)
    top = jnp.take(sat, rl, axis=1)
    bot = jnp.take(sat, rh, axis=1)
    s = (jnp.take(bot, ch, axis=2) - jnp.take(bot, cl, axis=2)
         - jnp.take(top, ch, axis=2) + jnp.take(top, cl, axis=2))
    cnt = ((rh - rl)[:, None] * (ch - cl)[None, :]).astype(u.dtype)
    return s / cnt[None, :, :, None]


def pool_branch(u, p, grid):
    B, L, _ = u.shape
    uf = u.astype(jnp.float32)
    outs = []
    for g, w in enumerate(POOL_WINDOWS):
        ug = uf[..., g * POOL_GROUP:(g + 1) * POOL_GROUP]
        if grid:
            rows = L // GRID_W
            m = mean_pool_2d(ug.reshape(B, rows, GRID_W, POOL_GROUP), w).reshape(B, L, POOL_GROUP)
        else:
            m = mean_pool_1d(ug, w)
        outs.append(m - ug)
    y = jnp.stack(outs, axis=2).astype(u.dtype)
    y = jnp.einsum("blgc,gcd->blgd", y, p["pool_w"]).reshape(B, L, D_POOL)
    return y * p["pool_scale"]


def token_mixers(proj, p, grid, s_f0, s_b0):
    y_hy = hyena_branch(proj[..., :GLA_OFF], p)
    y_gla, s_f, s_b = gla_branch(proj[..., GLA_OFF:POOL_OFF], p, s_f0, s_b0)
    y_pool = pool_branch(proj[..., POOL_OFF:GATE_OFF], p, grid)
    g_hy, g_gla, g_pool = jnp.split(jax.nn.sigmoid(proj[..., GATE_OFF:]), N_BRANCH, axis=-1)
    merged = (g_hy * (y_hy @ p["w_br_hy"]) + g_gla * (y_gla @ p["w_br_gla"])
              + g_pool * (y_pool @ p["w_br_pool"]))
    return merged @ p["w_out"], s_f, s_b


def sqrelu_mlp(h, w_up, w_down):
    return jnp.square(jax.nn.relu(h @ w_up)) @ w_down


def setup_inputs(seed: int = 0) -> dict:
    key = jax.random.key(seed)
    ks = jax.random.split(key, 40)
    f32 = jnp.float32

    def nrm(k, shape, scale):
        return jax.random.normal(k, shape, f32) * scale

    return {
        "x": nrm(ks[0], (BATCH, SEQ, D_MODEL), 1.0),
        "c": nrm(ks[1], (BATCH, D_MODEL), 1.0),
        "ctx": nrm(ks[2], (BATCH, CTX_LEN, D_MODEL), 1.0),
        "c_ctx": nrm(ks[3], (D_MODEL,), 1.0),
        "w_mod": nrm(ks[4], (DEPTH, D_MODEL, 6 * D_MODEL), D_MODEL ** -0.5),
        "b_mod": nrm(ks[5], (DEPTH, 6 * D_MODEL), 0.02),
        "norm_mix": 1.0 + nrm(ks[6], (DEPTH, D_MODEL), 0.02),
        "norm_ffn": 1.0 + nrm(ks[7], (DEPTH, D_MODEL), 0.02),
        "w_in": nrm(ks[8], (DEPTH, D_MODEL, N_IN), D_MODEL ** -0.5),
        "b_in": nrm(ks[9], (DEPTH, N_IN), 0.02),
        "hy_short_w": nrm(ks[10], (DEPTH, HY_SHORT, HY_COLS), HY_SHORT ** -0.5),
        "hy_short_b": nrm(ks[11], (DEPTH, HY_COLS), 0.02),
        "hy_f_w1": nrm(ks[12], (DEPTH, HY_EMB, HY_FH), HY_EMB ** -0.5),
        "hy_f_b1": nrm(ks[13], (DEPTH, HY_FH), 0.02),
        "hy_f_freq": 1.0 + nrm(ks[14], (DEPTH, HY_FH), 0.02),
        "hy_f_w2": nrm(ks[15], (DEPTH, HY_FH, HY_FH), HY_FH ** -0.5),
        "hy_f_b2": nrm(ks[16], (DEPTH, HY_FH), 0.02),
        "hy_f_w3": nrm(ks[17], (DEPTH, HY_FH, HY_DIRS * HY_ORDER * D_HY), HY_FILTER_SCALE),
        "hy_skip": nrm(ks[18], (DEPTH, HY_ORDER, D_HY), 0.5),
        "gla_wa_f": nrm(ks[19], (DEPTH, GLA_GATE_RANK, D_GLA_K), GLA_GATE_RANK ** -0.5),
        "gla_ba_f": nrm(ks[20], (DEPTH, D_GLA_K), 0.02),
        "gla_wa_b": nrm(ks[21], (DEPTH, GLA_GATE_RANK, D_GLA_K), GLA_GATE_RANK ** -0.5),
        "gla_ba_b": nrm(ks[22], (DEPTH, D_GLA_K), 0.02),
        "gla_norm_w": 1.0 + nrm(ks[23], (DEPTH, GLA_DV), 0.02),
        "pool_w": nrm(ks[24], (DEPTH, POOL_GROUPS, POOL_GROUP, POOL_GROUP), POOL_GROUP ** -0.5),
        "pool_scale": 1.0 + nrm(ks[25], (DEPTH, D_POOL), 0.02),
        "w_br_hy": nrm(ks[26], (DEPTH, D_HY, D_MODEL), D_HY ** -0.5),
        "w_br_gla": nrm(ks[27], (DEPTH, D_GLA_V, D_MODEL), D_GLA_V ** -0.5),
        "w_br_pool": nrm(ks[28], (DEPTH, D_POOL, D_MODEL), D_POOL ** -0.5),
        "w_out": nrm(ks[29], (DEPTH, D_MODEL, D_MODEL), D_MODEL ** -0.5),
        "w_up": nrm(ks[30], (DEPTH, D_MODEL, D_FF), D_MODEL ** -0.5),
        "w_down": nrm(ks[31], (DEPTH, D_FF, D_MODEL), D_FF ** -0.5),
        "norm_final": 1.0 + nrm(ks[32], (D_MODEL,), 0.02),
    }


def reference(x, c, ctx, c_ctx, w_mod, b_mod, norm_mix, norm_ffn, w_in, b_in,
              hy_short_w, hy_short_b, hy_f_w1, hy_f_b1, hy_f_freq, hy_f_w2, hy_f_b2, hy_f_w3, hy_skip,
              gla_wa_f, gla_ba_f, gla_wa_b, gla_ba_b, gla_norm_w,
              pool_w, pool_scale, w_br_hy, w_br_gla, w_br_pool, w_out, w_up, w_down, norm_final):
    B = x.shape[0]
    s_zero = jnp.zeros((B, GLA_HEADS, GLA_DK, GLA_DV), jnp.float32)
    for l in range(DEPTH):
        p = {
            "hy_short_w": hy_short_w[l], "hy_short_b": hy_short_b[l],
            "hy_f_w1": hy_f_w1[l], "hy_f_b1": hy_f_b1[l], "hy_f_freq": hy_f_freq[l],
            "hy_f_w2": hy_f_w2[l], "hy_f_b2": hy_f_b2[l], "hy_f_w3": hy_f_w3[l], "hy_skip": hy_skip[l],
            "gla_wa_f": gla_wa_f[l], "gla_ba_f": gla_ba_f[l], "gla_wa_b": gla_wa_b[l], "gla_ba_b": gla_ba_b[l],
            "gla_norm_w": gla_norm_w[l], "pool_w": pool_w[l], "pool_scale": pool_scale[l],
            "w_br_hy": w_br_hy[l], "w_br_gla": w_br_gla[l], "w_br_pool": w_br_pool[l], "w_out": w_out[l],
        }
        mod_x = (jax.nn.silu(c) @ w_mod[l] + b_mod[l])[:, None, :]
        mod_c = (jax.nn.silu(c_ctx) @ w_mod[l] + b_mod[l])[None, None, :]
        sh1x, sc1x, g1x, sh2x, sc2x, g2x = jnp.split(mod_x, 6, axis=-1)
        sh1c, sc1c, g1c, sh2c, sc2c, g2c = jnp.split(mod_c, 6, axis=-1)

        hc = ada_norm(ctx, norm_mix[l], sh1c, sc1c)
        if l == DEPTH - 1:
            pc = hc @ w_in[l][:, GLA_OFF:POOL_OFF] + b_in[l][GLA_OFF:POOL_OFF]
            _, s_f, s_b = gla_branch(pc, p, s_zero, s_zero)
        else:
            pc = hc @ w_in[l] + b_in[l]
            mix_c, s_f, s_b = token_mixers(pc, p, False, s_zero, s_zero)
            ctx = ctx + g1c * mix_c
            ctx = ctx + g2c * sqrelu_mlp(ada_norm(ctx, norm_ffn[l], sh2c, sc2c), w_up[l], w_down[l])

        hx = ada_norm(x, norm_mix[l], sh1x, sc1x)
        px = hx @ w_in[l] + b_in[l]
        mix_x, _, _ = token_mixers(px, p, True, s_f, s_b)
        x = x + g1x * mix_x
        x = x + g2x * sqrelu_mlp(ada_norm(x, norm_ffn[l], sh2x, sc2x), w_up[l], w_down[l])
    return rms_norm(x, norm_final)
```

```python
import math
import numpy as np
import ml_dtypes
from contextlib import ExitStack
import concourse.bass as bass
import concourse.mybir as mybir
from concourse.bass_utils import run_bass_kernel_spmd

F32 = mybir.dt.float32
BF16 = mybir.dt.bfloat16
AF = mybir.ActivationFunctionType
ALU = mybir.AluOpType
AX = mybir.AxisListType

D = 1024
T = 2048
TC = 256
NB = 4
NCORES = 8
EPS = 1e-6
N_IN = 6688
GLA_OFF = 1536
POOL_OFF = 3104
GATE_OFF = 3616
EPOCH = 30000
NDS = 40
PI = math.pi


class Buf:
    __slots__ = ("w", "r")

    def __init__(self):
        self.w = None
        self.r = {}


class Tl:
    def __init__(self, t):
        self.t = t
        self.b = Buf()

    def __getitem__(self, k):
        return self.t[k]


class Sched:
    def __init__(self, nc, es):
        self.nc = nc
        self.es = es
        self.eng = {"pe": nc.tensor, "act": nc.scalar, "dve": nc.vector, "pool": nc.gpsimd, "sp": nc.sync}
        self.cnt = {k: 0 for k in self.eng}
        self.sems = {k: [] for k in self.eng}
        self.seen = {k: {} for k in self.eng}
        self.dsem = [es.enter_context(nc.semaphore(f"dq{i}")) for i in range(NDS)]
        self.dval = [0] * NDS
        self.dnext = 0
        self.nwait = 0
        self.ps_ring = []
        self.ps_i = 0
        self.psb_ring = []
        self.psb_i = 0
        self.ldq = 0
        self.dead = False

    def _sem(self, k, ep):
        while len(self.sems[k]) <= ep:
            self.sems[k].append(self.es.enter_context(self.nc.semaphore(f"s_{k}_{len(self.sems[k])}")))
        return self.sems[k][ep]

    def _wait(self, k, tok):
        if tok is None:
            return
        if tok[0] == "e":
            _, src, n = tok
            if n == 0:
                return
            if src == "pe" and k == "pe":
                return
            if self.seen[k].get(src, 0) >= n:
                return
            self.seen[k][src] = n
            self.eng[k].wait_ge(self._sem(src, (n - 1) // EPOCH), (n - 1) % EPOCH + 1)
        else:
            _, i, v = tok
            key = ("d", i)
            if self.seen[k].get(key, 0) >= v:
                return
            self.seen[k][key] = v
            self.eng[k].wait_ge(self.dsem[i], v)
        self.nwait += 1

    def _deps(self, k, R, W):
        for b in R:
            self._wait(k, b.w)
        for b in W:
            self._wait(k, b.w)
            for t in b.r.values():
                self._wait(k, t)

    def op(self, k, fn, R=(), W=()):
        if self.dead:
            return
        R = [x.b if isinstance(x, Tl) else x for x in R]
        W = [x.b if isinstance(x, Tl) else x for x in W]
        self._deps(k, R, W)
        ins = fn(self.eng[k])
        self.cnt[k] += 1
        n = self.cnt[k]
        ins.then_inc(self._sem(k, (n - 1) // EPOCH), 1)
        t = ("e", k, n)
        for b in R:
            b.r[k] = t
        for b in W:
            b.w = t
            b.r = {}

    def dma(self, q, out, in_, R=(), W=()):
        if self.dead:
            return
        R = [x.b if isinstance(x, Tl) else x for x in R]
        W = [x.b if isinstance(x, Tl) else x for x in W]
        i = self.dnext
        self.dnext = (i + 1) % NDS
        if self.dval[i] > 0:
            self._wait(q, ("d", i, self.dval[i]))
        self._deps(q, R, W)
        self.dval[i] += 16
        self.eng[q].dma_start(out=out, in_=in_).then_inc(self.dsem[i], 16)
        t = ("d", i, self.dval[i])
        for b in R:
            b.r[("d", i)] = t
        for b in W:
            b.w = t
            b.r = {}

    def load(self, out, in_, R=(), W=()):
        self.dma("sp", out, in_, R, W)

    def store(self, out, in_, R=(), W=()):
        self.dma("pool", out, in_, R, W)

    def barrier(self):
        if self.dead:
            return
        toks = [("e", k, self.cnt[k]) for k in self.eng] + [("d", i, self.dval[i]) for i in range(NDS) if self.dval[i] > 0]
        for k in self.eng:
            for t in toks:
                if t[0] == "e" and t[1] == k:
                    continue
                self._wait(k, t)

    def ps(self):
        p = self.ps_ring[self.ps_i]
        self.ps_i = (self.ps_i + 1) % len(self.ps_ring)
        return p

    def psb(self):
        p = self.psb_ring[self.psb_i]
        self.psb_i = (self.psb_i + 1) % len(self.psb_ring)
        return p


def _pool_tables():
    mats = []
    seen = {}
    nbrs = {}
    rcnt = {}
    for mode, L in (("g", T), ("c", TC)):
        t = np.arange(L)
        rc_all = np.zeros((4, L), np.float32)
        for g, w in enumerate((2, 4, 8, 16)):
            if mode == "g":
                r = t // 64
                c = t % 64
                rl = np.clip(r - w // 2, 0, 32)
                rh = np.clip(r - w // 2 + w, 0, 32)
                cl = np.clip(c - w // 2, 0, 64)
                ch = np.clip(c - w // 2 + w, 0, 64)
                P = ((r[:, None] >= rl[None, :]) & (r[:, None] < rh[None, :]) & (c[:, None] >= cl[None, :]) & (c[:, None] < ch[None, :])).astype(np.float32)
                cnt = ((rh - rl) * (ch - cl)).astype(np.float32)
            else:
                lo = np.clip(t - w // 2, 0, L)
                hi = np.clip(t - w // 2 + w, 0, L)
                P = ((t[:, None] >= lo[None, :]) & (t[:, None] < hi[None, :])).astype(np.float32)
                cnt = (hi - lo).astype(np.float32)
            P = P - np.diag(cnt)
            rc_all[g] = 1.0 / cnt
            for n in range(L // 128):
                lst = []
                for n2 in range(L // 128):
                    blk = np.ascontiguousarray(P[n2 * 128:(n2 + 1) * 128, n * 128:(n + 1) * 128])
                    if np.any(blk):
                        key = blk.tobytes()
                        if key not in seen:
                            seen[key] = len(mats)
                            mats.append(blk)
                        lst.append((n2, seen[key]))
                nbrs[(mode, g, n)] = lst
        rcnt[mode] = rc_all
    pm = np.stack(mats, 1)
    return pm, nbrs, rcnt


_POOL = _pool_tables()
NU = _POOL[0].shape[1]


def _z_table(L, NP):
    pos = np.abs(np.arange(NP) - (L - 1)).astype(np.float32)
    bands = np.arange(1, 9, dtype=np.float32)[None, :]
    ang = (np.float32(2.0 * math.pi / L) * pos[:, None]) * bands
    z = np.concatenate([pos[:, None] / np.float32(L), np.cos(ang), np.sin(ang)], -1).astype(np.float32)
    zT = np.zeros((32, NP), np.float32)
    zT[:17] = z.T
    posn = np.zeros((2, NP), np.float32)
    posn[0] = np.abs(np.arange(NP) - (L - 1)) / np.float32(L)
    posn[1] = np.abs(np.arange(NP) - L) / np.float32(L)
    return zT, np.ascontiguousarray(np.broadcast_to(posn[None], (128, 2, NP)))


def _fft_consts():
    N = 4096
    bf = ml_dtypes.bfloat16
    n1 = np.arange(32)
    n2 = np.arange(64)
    k1 = np.arange(64)
    th = 2 * np.pi * (((64 * n1[:, None, None] + n2[None, :, None]) * k1[None, None, :]) % N) / N
    F1x = np.zeros((64, 64, 128))
    F1x[0:32, :, 0:64] = np.cos(th)
    F1x[32:64, :, 0:64] = np.sin(th)
    F1x[0:32, :, 64:128] = -np.sin(th)
    F1x[32:64, :, 64:128] = np.cos(th)
    n1f = np.arange(64)
    thk = 2 * np.pi * (((64 * n1f[:, None, None] + n2[None, :, None]) * k1[None, None, :]) % N) / N
    F1k = np.zeros((64, 64, 128))
    F1k[:, :, 0:64] = np.cos(thk)
    F1k[:, :, 64:128] = -np.sin(thk)
    k2 = np.arange(64)
    ph = 2 * np.pi * ((n2[:, None] * k2[None, :]) % 64) / 64
    c2, s2 = np.cos(ph), np.sin(ph)
    F2 = np.zeros((64, 8, 128))
    for i, (a_, b_) in enumerate(((c2, -s2), (s2, c2), (-s2, c2), (c2, s2), (c2, c2), (s2, s2), (s2, -s2), (-c2, c2))):
        F2[:, i, 0:64] = a_
        F2[:, i, 64:128] = b_
    ph3 = ph.T
    F3 = np.zeros((128, 128))
    F3[0:64, 0:64] = np.cos(ph3)
    F3[64:, 0:64] = -np.sin(ph3)
    F3[0:64, 64:] = np.sin(ph3)
    F3[64:, 64:] = np.cos(ph3)
    F3 /= N
    th4 = 2 * np.pi * (((64 * n1[None, None, :] + n2[None, :, None]) * k1[:, None, None]) % N) / N
    F4 = np.zeros((64, 64, 2, 64))
    F4[:, :, 0, 0:32] = np.cos(th4)
    F4[:, :, 0, 32:] = np.sin(th4)
    F4[:, :, 1, 0:32] = -np.sin(th4)
    F4[:, :, 1, 32:] = np.cos(th4)
    n = 64 * n1f[:, None] + n2[None, :]
    tau = np.minimum(n, N - n).astype(np.float64)
    deltas = np.abs(np.linspace(math.log(1e-2) / 1.5, math.log(1e-2) / 0.3, 512, dtype=np.float32)).astype(np.float64)
    wtab = np.exp(-(tau[:, :, None] / 2048.0) * deltas[None, None, :])
    wtab[32, 0, :] = 0.0
    pos = np.minimum(np.arange(N), N - np.arange(N)).astype(np.float32)
    bands = np.arange(1, 9, dtype=np.float32)[None, :]
    ang = (np.float32(2.0 * math.pi / 2048) * pos[:, None]) * bands
    z = np.concatenate([pos[:, None] / np.float32(2048), np.cos(ang), np.sin(ang)], -1).astype(np.float32)
    zT = np.zeros((32, N), np.float32)
    zT[:17] = z.T
    return {"F1x": F1x.astype(bf), "F1k": F1k.astype(bf), "F2": F2.astype(bf), "F3": F3.astype(bf), "F4": F4.astype(bf),
            "wtab": wtab.astype(np.float32), "zT_f": zT}


def _consts():
    c = {}
    eye = np.eye(128, dtype=np.float32)
    c["ident_bf"] = eye.astype(ml_dtypes.bfloat16)
    c["identJ_bf"] = eye[::-1].copy().astype(ml_dtypes.bfloat16)
    c["ident_f"] = eye
    s = np.arange(128)[:, None]
    t = np.arange(128)[None, :]
    same = (s // 64) == (t // 64)
    tri = np.zeros((128, 4, 128), np.float32)
    tri[:, 0] = np.where(same & (s <= t), -1.0 / 16, 0)
    tri[:, 1] = np.where(same & (s > t), -1.0 / 16, 0)
    tri[:, 2] = np.where(same & (s >= t), -1.0 / 16, 0)
    tri[:, 3] = np.where(same & (s < t), -1.0 / 16, 0)
    c["tri"] = tri.astype(ml_dtypes.bfloat16)
    msk = np.zeros((128, 2, 128), np.float32)
    msk[:, 0] = np.where(same & (t >= s), 1.0, 0)
    msk[:, 1] = np.where(same & (t <= s), 1.0, 0)
    c["msk"] = msk
    c["pmat"] = _POOL[0].astype(ml_dtypes.bfloat16)
    c["rcnt_g"] = np.ascontiguousarray(np.broadcast_to(_POOL[2]["g"][None], (128, 4, T)))
    c["rcnt_c"] = np.ascontiguousarray(np.broadcast_to(_POOL[2]["c"][None], (128, 4, TC)))
    c["zT_x"], c["posn_x"] = _z_table(T, 2 * T)
    c["zT_c"], c["posn_c"] = _z_table(TC, 2 * TC)
    c.update(_fft_consts())
    c["ones_f"] = np.ones((1, 128), np.float32)
    c["ones_bf"] = np.ones((1, 128), ml_dtypes.bfloat16)
    return c


NPP = 112


def _pp_pack(inp, l):
    pp = np.zeros((128, NPP), np.float32)
    b_in = inp["b_in"][l]
    pp[:, 0:12] = b_in[0:1536].reshape(12, 128).T
    for k in range(3):
        pp[:, 12 + k * 12:24 + k * 12] = inp["hy_short_w"][l][k].reshape(12, 128).T
    pp[:, 48:60] = inp["hy_short_b"][l].reshape(12, 128).T
    pp[:, 60:64] = b_in[1536:2048].reshape(4, 128).T
    pp[:, 64:88] = b_in[GATE_OFF:GATE_OFF + 3072].reshape(24, 128).T
    pp[:, 88:92] = inp["pool_scale"][l].reshape(4, 128).T
    pp[0:16, 92] = b_in[3072:3088]
    pp[0:16, 93] = b_in[3088:3104]
    pp[0:64, 94] = inp["hy_f_freq"][l]
    pp[0:64, 95] = inp["hy_f_b1"][l]
    pp[0:64, 96] = inp["hy_f_b2"][l]
    deltas = np.abs(np.linspace(math.log(1e-2) / 1.5, math.log(1e-2) / 0.3, 512, dtype=np.float32))
    pp[:, 98:102] = (-deltas).reshape(4, 128).T
    pp[:, 102:106] = inp["hy_skip"][l][0].reshape(4, 128).T
    pp[:, 106:110] = inp["hy_skip"][l][1].reshape(4, 128).T
    return pp


def PC(nc, S, l, streams, E):
    tile = E["tile"]; pp = E["pp"]; Wb = E["Wb"]; C = E["C"]; MODV = E["MODV"]
    ident_bf = E["ident_bf"]; ones_bf = E["ones_bf"]; ones_f = E["ones_f"]
    XS = E["XS"]; CXS = E["CXS"]; YT = E["YT"]; YTC = E["YTC"]; out = E["out"]
    b_in = E["b_in"]; wa_f = E["wa_f"]; wa_b = E["wa_b"]; ba_f = E["ba_f"]; ba_b = E["ba_b"]
    gla_nw = E["gla_nw"]; norm_final = E["norm_final"]; bcast_row = E["bcast_row"]
    ppl = pp[l]
    chk = E["chk"]
    last = (l == 1)
    nbrs = _POOL[1]
    win = Wb[("in", l)]

    def mvap(s, j):
        return MODV[l, s][:, j * D:(j + 1) * D]

    with ExitStack() as es:
        hxT = tile(es, "c_hxT", [128, 8, T], BF16)
        yglaT = tile(es, "c_yglaT", [128, 4, T], BF16)
        Sst = tile(es, "c_Sst", [128, 2, 4, 128], F32)
        SstB = [Buf(), Buf()]
        tri = tile(es, "c_tri", [128, 4, 128], BF16)
        msk = tile(es, "c_msk", [128, 2, 128], F32)
        gnw = tile(es, "c_gnw", [128, 128], F32)
        brow = tile(es, "c_brow", [1, N_IN], BF16)
        waug = tile(es, "c_waug", [32, 2, 256], BF16)
        es_row = ExitStack()
        rowf = tile(es_row, "c_rowf", [1, N_IN], F32)
        waf = tile(es_row, "c_waf", [32, 2, 256], F32)
        S.load(tri[:], C["tri"], W=[tri])
        S.load(msk[:], C["msk"], W=[msk])
        bcast_row(es, gnw, gla_nw[l:l + 1, :], 128, rowf)
        S.load(rowf[:], b_in[l:l + 1, :], W=[rowf])
        S.op("dve", lambda e: e.tensor_copy(out=brow[:], in_=rowf[:]), R=[rowf], W=[brow])
        S.op("dve", lambda e: e.memset(waf[:], 0.0), W=[waf])
        S.load(waf[0:16, 0, :], wa_f[l], W=[waf])
        S.load(waf[0:16, 1, :], wa_b[l], W=[waf])
        S.load(waf[16:17, 0, :], ba_f[l:l + 1, :], W=[waf])
        S.load(waf[16:17, 1, :], ba_b[l:l + 1, :], W=[waf])
        S.op("dve", lambda e: e.tensor_copy(out=waug[:], in_=waf[:]), R=[waf], W=[waug])
        S.barrier()
        es_row.close()

        def tm_proj(p, ncols, i, wt, c0, bcol0):
            for k in range(8):
                S.op("pe", lambda e: e.matmul(p[:, 0:ncols], lhsT=hxT[:, k, i * 128:(i + 1) * 128], rhs=wt[:, k, c0:c0 + ncols], start=(k == 0), stop=False),
                     R=[hxT, wt], W=[p])
            S.op("pe", lambda e: e.matmul(p[:, 0:ncols], lhsT=ones_bf[0:1, :], rhs=brow[0:1, bcol0:bcol0 + ncols], start=False, stop=True),
                 R=[ones_bf, brow], W=[p])

        for (kind, b, Tn, xsrc, hdst, ms) in streams:
            NT = Tn // 128
            ctx_l1 = (kind == "c" and l == 1)
            for k in range(8):
                S.load(hxT[:, k, 0:Tn], hdst[:, k * Tn:(k + 1) * Tn], W=[hxT])
            with ExitStack() as e2:
                wg = tile(e2, "g_w", [128, 8, 1568], BF16)
                for k in range(8):
                    S.load(wg[:, k, :], win[k * 128:(k + 1) * 128, GLA_OFF:POOL_OFF], W=[wg])
                vtm = tile(e2, "g_v", [128, 16, 512], BF16)
                ktm = tile(e2, "g_k", [128, 16, 256], BF16)
                qkT = tile(e2, "g_qkT", [128, 4, T], BF16)
                gTa = tile(e2, "g_gT", [32, 2, T], BF16)
                oacc = tile(e2, "g_o", [128, 16, 512], F32)
                oab = [Buf() for _ in range(16)]
                S.op("pool", lambda e: e.memset(gTa[:], 1.0), W=[gTa])
                if not ctx_l1:
                    S.op("pool", lambda e: e.memset(oacc[:, 0:NT, :], 0.0), W=oab[0:NT])
                if kind == "c":
                    S.op("pool", lambda e: e.memset(Sst[:], 0.0), W=[SstB[0], SstB[1]])
                for i in range(NT):
                    p = S.ps()
                    tm_proj(p, 512, i, wg, 512, 2048)
                    S.op("act", lambda e: e.copy(out=vtm[:, i, :], in_=p[:]), R=[p], W=[vtm])
                    p = S.ps()
                    tm_proj(p, 256, i, wg, 256, 1792)
                    S.op("dve", lambda e: e.tensor_copy(out=ktm[:, i, :], in_=p[:, 0:256]), R=[p], W=[ktm])
                for j in range(4):
                    for t0 in range(0, Tn, 512):
                        tw = min(512, Tn - t0)
                        p = S.ps()
                        for k in range(8):
                            S.op("pe", lambda e: e.matmul(p[:, 0:tw], lhsT=wg[:, k, j * 128:(j + 1) * 128], rhs=hxT[:, k, t0:t0 + tw], start=(k == 0), stop=(k == 7)),
                                 R=[wg, hxT], W=[p])
                        S.op("act", lambda e: e.activation(out=qkT[:, j, t0:t0 + tw], in_=p[:, 0:tw], func=AF.Identity, bias=ppl[:, 60 + j:61 + j]), R=[p, ppl], W=[qkT])
                for dr in range(2):
                    for t0 in range(0, Tn, 512):
                        tw = min(512, Tn - t0)
                        p = S.ps()
                        for k in range(8):
                            S.op("pe", lambda e: e.matmul(p[0:16, 0:tw], lhsT=wg[:, k, 1536 + 16 * dr:1552 + 16 * dr], rhs=hxT[:, k, t0:t0 + tw], start=(k == 0), stop=(k == 7)),
                                 R=[wg, hxT], W=[p])
                        S.op("act", lambda e: e.activation(out=gTa[0:16, dr, t0:t0 + tw], in_=p[0:16, 0:tw], func=AF.Identity, bias=ppl[0:16, 92 + dr:93 + dr]), R=[p, ppl], W=[gTa])

                def chain(dr, P0, P1, P2):
                    spt = tile(e2, f"g_sp{dr}", [128, 256], BF16)
                    erb = tile(e2, f"g_erb{dr}", [128, 256], F32)
                    kend = tile(e2, f"g_kend{dr}", [128, 256], BF16)
                    eb = tile(e2, f"g_eb{dr}", [128, 2, 128], F32)
                    ebi = tile(e2, f"g_ebi{dr}", [128, 2, 128], F32)
                    qd = tile(e2, f"g_qd{dr}", [128, 2, 128], BF16)
                    ki = tile(e2, f"g_ki{dr}", [128, 2, 128], BF16)
                    qA = tile(e2, f"g_qA{dr}", [128, 2, 128], BF16)
                    qB = tile(e2, f"g_qB{dr}", [128, 2, 128], BF16)
                    scm = [tile(e2, f"g_scm{dr}{i}", [128, 128], BF16) for i in range(2)]
                    Smid = tile(e2, f"g_Smid{dr}", [128, 4, 128], F32)
                    Sb0 = tile(e2, f"g_Sb0{dr}", [128, 4, 128], BF16)
                    Sb1 = tile(e2, f"g_Sb1{dr}", [128, 4, 128], BF16)
                    Sd = SstB[dr]
                    S.op("pool", lambda e: e.memset(qA[:], 0.0), W=[qA])
                    S.op("pool", lambda e: e.memset(qB[:], 0.0), W=[qB])
                    yield
                    order = list(range(NT)) if dr == 0 else list(range(NT - 1, -1, -1))
                    for i in order:
                        tsl = slice(i * 128, (i + 1) * 128)
                        p = P0
                        S.op("pe", lambda e: e.matmul(p[:, 0:256], lhsT=gTa[0:32, dr, tsl], rhs=waug[0:32, dr, :], start=True, stop=True), R=[gTa, waug], W=[p])
                        yield
                        S.op("act", lambda e: e.activation(out=erb[:], in_=p[:, 0:256], func=AF.Exp, scale=-1.0), R=[p], W=[erb])
                        S.op("act", lambda e: e.activation(out=spt[:], in_=erb[:], func=AF.Ln, bias=1.0), R=[erb], W=[spt])
                        yield
                        S.op("pe", lambda e: e.matmul(P1[:, 0:256], lhsT=tri[:, 2 * dr + 1, :], rhs=spt[:], start=True, stop=True), R=[tri, spt], W=[P1])
                        for ch in range(2):
                            S.op("pe", lambda e: e.matmul(P2[:, ch * 128:(ch + 1) * 128], lhsT=spt[:, ch * 128:(ch + 1) * 128], rhs=tri[:, 2 * dr, :], start=True, stop=True),
                                 R=[tri, spt], W=[P2])
                        yield
                        S.op("act", lambda e: e.activation(out=erb[:], in_=P1[:, 0:256], func=AF.Exp), R=[P1], W=[erb])
                        S.op("act", lambda e: e.activation(out=eb[:].rearrange("p a b -> p (a b)"), in_=P2[:, 0:256], func=AF.Exp), R=[P2], W=[eb])
                        S.op("act", lambda e: e.activation(out=ebi[:].rearrange("p a b -> p (a b)"), in_=P2[:, 0:256], func=AF.Exp, scale=-1.0), R=[P2], W=[ebi])
                        yield
                        S.op("dve", lambda e: e.tensor_tensor(out=kend[:], in0=ktm[:, i, :], in1=erb[:], op=ALU.mult), R=[ktm, erb], W=[kend])
                        S.op("dve", lambda e: e.scalar_tensor_tensor(out=qd[:], in0=eb[:], scalar=0.125, in1=qkT[:, 0:2, tsl], op0=ALU.mult, op1=ALU.mult), R=[eb, qkT], W=[qd])
                        S.op("dve", lambda e: e.tensor_tensor(out=ki[:], in0=ebi[:], in1=qkT[:, 2:4, tsl], op=ALU.mult), R=[ebi, qkT], W=[ki])
                        yield
                        S.op("pool", lambda e: e.tensor_copy(out=qA[:, :, 0:64], in_=qd[:, :, 0:64]), R=[qd], W=[qA])
                        S.op("pool", lambda e: e.tensor_copy(out=qB[:, :, 64:128], in_=qd[:, :, 64:128]), R=[qd], W=[qB])
                        pk0, pk1 = P1, P2
                        for h in range(4):
                            cq = h // 2
                            S.op("pe", lambda e: e.matmul(pk0[:, h * 128:(h + 1) * 128], lhsT=kend[0:64, cq * 128:(cq + 1) * 128], rhs=vtm[0:64, i, h * 128:(h + 1) * 128], start=True, stop=True),
                                 R=[kend, vtm], W=[pk0])
                            S.op("pe", lambda e: e.matmul(pk1[:, h * 128:(h + 1) * 128], lhsT=kend[64:128, cq * 128:(cq + 1) * 128], rhs=vtm[64:128, i, h * 128:(h + 1) * 128], start=True, stop=True),
                                 R=[kend, vtm], W=[pk1])
                        if dr == 0:
                            pF, pS_, cF, cS, qF, qS = pk0, pk1, 63, 127, qA, qB
                        else:
                            pF, pS_, cF, cS, qF, qS = pk1, pk0, 64, 0, qB, qA
                        S.op("act", lambda e: e.copy(out=Sb0[:], in_=Sst[:, dr]), R=[Sd], W=[Sb0])
                        yield
                        for cq in range(2):
                            S.op("dve", lambda e: e.scalar_tensor_tensor(out=Smid[:, 2 * cq:2 * cq + 2, :], in0=Sst[:, dr, 2 * cq:2 * cq + 2, :], scalar=eb[:, cq, cF:cF + 1],
                                                                          in1=pF[:, cq * 256:(cq + 1) * 256].rearrange("p (a b) -> p a b", a=2), op0=ALU.mult, op1=ALU.add),
                                 R=[Sd, eb, pF], W=[Smid])
                        yield
                        S.op("act", lambda e: e.copy(out=Sb1[:], in_=Smid[:]), R=[Smid], W=[Sb1])
                        for cq in range(2):
                            S.op("dve", lambda e: e.scalar_tensor_tensor(out=Sst[:, dr, 2 * cq:2 * cq + 2, :], in0=Smid[:, 2 * cq:2 * cq + 2, :], scalar=eb[:, cq, cS:cS + 1],
                                                                          in1=pS_[:, cq * 256:(cq + 1) * 256].rearrange("p (a b) -> p a b", a=2), op0=ALU.mult, op1=ALU.add),
                                 R=[Smid, eb, pS_], W=[Sd])
                        yield
                        if ctx_l1:
                            continue
                        po = P0
                        for h in range(4):
                            cq = h // 2
                            r0 = 64 * (h % 2)
                            p_s = P1 if h % 2 == 0 else P2
                            S.op("pe", lambda e: e.matmul(p_s[:, 0:128], lhsT=ki[r0:r0 + 64, cq, :], rhs=qd[r0:r0 + 64, cq, :], start=True, stop=True), R=[ki, qd], W=[p_s])
                            yield
                            sc = scm[h % 2]
                            S.op("dve", lambda e: e.tensor_tensor(out=sc[:], in0=p_s[:, 0:128], in1=msk[:, dr, :], op=ALU.mult), R=[p_s, msk], W=[sc])
                            yield
                            S.op("pe", lambda e: e.matmul(po[:, h * 128:(h + 1) * 128], lhsT=sc[:], rhs=vtm[:, i, h * 128:(h + 1) * 128], start=True, stop=False), R=[sc, vtm], W=[po])
                            S.op("pe", lambda e: e.matmul(po[:, h * 128:(h + 1) * 128], lhsT=qF[r0:r0 + 64, cq, :], rhs=Sb0[r0:r0 + 64, h, :], start=False, stop=False), R=[qF, Sb0], W=[po])
                            S.op("pe", lambda e: e.matmul(po[:, h * 128:(h + 1) * 128], lhsT=qS[r0:r0 + 64, cq, :], rhs=Sb1[r0:r0 + 64, h, :], start=False, stop=True), R=[qS, Sb1], W=[po])
                        yield
                        S.op("dve", lambda e: e.tensor_tensor(out=oacc[:, i, :], in0=oacc[:, i, :], in1=po[:], op=ALU.add), R=[po, oab[i]], W=[oab[i]])
                        yield

                gens = [chain(0, S.ps_ring[0], S.ps_ring[1], S.ps_ring[2]), chain(1, S.ps_ring[3], S.ps_ring[4], S.ps_ring[5])]
                alive = list(gens)
                while alive:
                    for g_ in list(alive):
                        try:
                            next(g_)
                        except StopIteration:
                            alive.remove(g_)
                if not ctx_l1:
                    sr = tile(e2, "g_sr", [128, 512], F32)
                    tmpo = tile(e2, "g_tmpo", [128, 512], F32)
                    ybfs = [tile(e2, f"g_ybf{i}", [128, 512], BF16) for i in range(2)]
                    ss4 = tile(e2, "g_ss4", [128, 8], F32)

                    def nA(i):
                        ybf = ybfs[i % 2]
                        p = S.ps()
                        tm_proj(p, 512, i, wg, 1024, 2560)
                        S.op("act", lambda e: e.activation(out=sr[:], in_=p[:], func=AF.Silu), R=[p], W=[sr])
                        for h in range(4):
                            S.op("act", lambda e: e.activation(out=tmpo[:, h * 128:(h + 1) * 128], in_=oacc[:, i, h * 128:(h + 1) * 128], func=AF.Square, accum_out=ss4[:, h:h + 1]),
                                 R=[oab[i]], W=[tmpo, ss4])
                        S.op("act", lambda e: e.activation(out=ss4[:, 4:8], in_=ss4[:, 0:4], func=AF.Sqrt, scale=1.0 / 128, bias=EPS), R=[ss4], W=[ss4])
                        S.op("dve", lambda e: e.reciprocal(out=ss4[:, 4:8], in_=ss4[:, 4:8]), R=[ss4], W=[ss4])
                        for h in range(4):
                            S.op("dve", lambda e: e.scalar_tensor_tensor(out=tmpo[:, h * 128:(h + 1) * 128], in0=oacc[:, i, h * 128:(h + 1) * 128], scalar=ss4[:, 4 + h:5 + h],
                                                                          in1=gnw[:], op0=ALU.mult, op1=ALU.mult), R=[oab[i], ss4, gnw], W=[tmpo])
                        S.op("dve", lambda e: e.tensor_tensor(out=ybf[:], in0=tmpo[:], in1=sr[:], op=ALU.mult), R=[tmpo, sr], W=[ybf])

                    def nB(i):
                        ybf = ybfs[i % 2]
                        pb = S.psb()
                        for k in range(4):
                            S.op("pe", lambda e: e.transpose(out=pb[:, k * 128:(k + 1) * 128], in_=ybf[:, k * 128:(k + 1) * 128], identity=ident_bf[:]), R=[ybf, ident_bf], W=[pb])
                        S.op("act", lambda e: e.copy(out=yglaT[:, :, i * 128:(i + 1) * 128], in_=pb[:, 0:512].rearrange("p (k t) -> p k t", k=4)), R=[pb], W=[yglaT])

                    nA(0)
                    for i in range(NT):
                        if i + 1 < NT:
                            nA(i + 1)
                        nB(i)
                S.barrier()
            chk(f"PC{l}_gla_{kind}{b}")
            if ctx_l1:
                continue
            e3 = ExitStack()
            ypoolT = tile(e3, "c_ypoolT", [128, 4, T], BF16)
            with ExitStack() as e2:
                wp = tile(e2, "p_w", [128, 8, 512], BF16)
                for k in range(8):
                    S.load(wp[:, k, :], win[k * 128:(k + 1) * 128, POOL_OFF:GATE_OFF], W=[wp])
                up = tile(e2, "p_up", [128, 16, 512], BF16)
                rc = tile(e2, "p_rc", [128, 4, Tn], F32)
                pm = tile(e2, "p_pm", [128, NU, 128], BF16)
                pw = tile(e2, "p_pw", [128, 4, 128], BF16)
                mT = tile(e2, "p_mT", [128, 4, T], BF16)
                S.load(rc[:], C["rcnt_g"] if kind == "x" else C["rcnt_c"], W=[rc])
                S.load(pm[:], C["pmat"], W=[pm])
                for g in range(4):
                    S.load(pw[:, g, :], Wb[("pw", l)][g * 128:(g + 1) * 128, :], W=[pw])
                for i in range(NT):
                    p = S.ps()
                    tm_proj(p, 512, i, wp, 0, POOL_OFF)
                    S.op("act", lambda e: e.copy(out=up[:, i, :], in_=p[:]), R=[p], W=[up])
                mode = "g" if kind == "x" else "c"
                for g in range(4):
                    for n0 in range(0, NT, 4):
                        nn = min(4, NT - n0)
                        p = S.ps()
                        for n in range(n0, n0 + nn):
                            lst = nbrs[(mode, g, n)]
                            for q_, (n2, idx) in enumerate(lst):
                                S.op("pe", lambda e: e.matmul(p[:, (n - n0) * 128:(n - n0 + 1) * 128], lhsT=up[:, n2, g * 128:(g + 1) * 128], rhs=pm[:, idx, :],
                                                              start=(q_ == 0), stop=(q_ == len(lst) - 1)), R=[up, pm], W=[p])
                        S.op("dve", lambda e: e.tensor_tensor(out=mT[:, g, n0 * 128:(n0 + nn) * 128], in0=p[:, 0:nn * 128], in1=rc[:, g, n0 * 128:(n0 + nn) * 128], op=ALU.mult),
                             R=[p, rc], W=[mT])
                for g in range(4):
                    for t0 in range(0, Tn, 512):
                        tw = min(512, Tn - t0)
                        p = S.ps()
                        S.op("pe", lambda e: e.matmul(p[:, 0:tw], lhsT=pw[:, g, :], rhs=mT[:, g, t0:t0 + tw], start=True, stop=True), R=[pw, mT], W=[p])
                        S.op("act", lambda e: e.activation(out=ypoolT[:, g, t0:t0 + tw], in_=p[:, 0:tw], func=AF.Copy, scale=ppl[:, 88 + g:89 + g]), R=[p, ppl], W=[ypoolT])
                S.barrier()
            chk(f"PC{l}_pool_{kind}{b}")
            mergedT = tile(e3, "c_mergedT", [128, 8, T], BF16)
            with ExitStack() as e2:
                yhyT = tile(e2, "m_yhy", [128, 4, T], BF16)
                ysrc = YT[b] if kind == "x" else YTC[b]
                for cc in range(4):
                    S.load(yhyT[:, cc, 0:Tn], ysrc[cc * 128:(cc + 1) * 128, :], W=[yhyT])
                wbr = tile(e2, "m_wbr", [128, 3, 4, D], BF16)
                for br in range(3):
                    for k in range(4):
                        S.load(wbr[:, br, k, :], Wb[("br", br, l)][k * 128:(k + 1) * 128, :], W=[wbr])
                wgd = [tile(e2, f"m_wgd{i}", [128, 8, 3, 128], BF16) for i in range(2)]
                gts = [tile(e2, f"m_gt{i}", [128, 512], F32) for i in range(2)]
                acc = tile(e2, "m_acc", [128, 512], F32)
                tmpm = tile(e2, "m_tmp", [128, 512], F32)
                srcs = [yhyT, yglaT, ypoolT]
                for dc in range(8):
                    wg_ = wgd[dc % 2]
                    for br in range(3):
                        c0 = GATE_OFF + br * 1024 + dc * 128
                        S.load(wg_[:, :, br, :], win[:, c0:c0 + 128].rearrange("(k p) c -> p k c", p=128), W=[wg_])
                    for t0 in range(0, Tn, 512):
                        tw = min(512, Tn - t0)
                        for br in range(3):
                            pj = S.ps()
                            for k in range(4):
                                S.op("pe", lambda e: e.matmul(pj[:, 0:tw], lhsT=wbr[:, br, k, dc * 128:(dc + 1) * 128], rhs=srcs[br][:, k, t0:t0 + tw], start=(k == 0), stop=(k == 3)),
                                     R=[wbr, srcs[br]], W=[pj])
                            pg = S.ps()
                            for k in range(8):
                                S.op("pe", lambda e: e.matmul(pg[:, 0:tw], lhsT=wg_[:, k, br, :], rhs=hxT[:, k, t0:t0 + tw], start=(k == 0), stop=(k == 7)), R=[wg_, hxT], W=[pg])
                            gt_ = gts[br % 2]
                            S.op("act", lambda e: e.activation(out=gt_[:, 0:tw], in_=pg[:, 0:tw], func=AF.Sigmoid, bias=ppl[:, 64 + br * 8 + dc:65 + br * 8 + dc]), R=[pg, ppl], W=[gt_])
                            if br == 0:
                                S.op("dve", lambda e: e.tensor_tensor(out=acc[:, 0:tw], in0=gt_[:, 0:tw], in1=pj[:, 0:tw], op=ALU.mult), R=[gt_, pj], W=[acc])
                            else:
                                S.op("dve", lambda e: e.tensor_tensor(out=tmpm[:, 0:tw], in0=gt_[:, 0:tw], in1=pj[:, 0:tw], op=ALU.mult), R=[gt_, pj], W=[tmpm])
                                if br == 1:
                                    S.op("pool", lambda e: e.tensor_tensor(out=acc[:, 0:tw], in0=acc[:, 0:tw], in1=tmpm[:, 0:tw], op=ALU.add), R=[acc, tmpm], W=[acc])
                                else:
                                    S.op("pool", lambda e: e.tensor_tensor(out=mergedT[:, dc, t0:t0 + tw], in0=acc[:, 0:tw], in1=tmpm[:, 0:tw], op=ALU.add), R=[acc, tmpm], W=[mergedT])
                S.barrier()
            chk(f"PC{l}_merge_{kind}{b}")
            with ExitStack() as e2:
                wo = tile(e2, "o_w", [128, 8, D], BF16)
                for k in range(8):
                    S.load(wo[:, k, :], Wb[("out", l)][k * 128:(k + 1) * 128, :], W=[wo])
                g1 = tile(e2, "o_g1", [128, D], F32)
                S.load(g1[:], mvap(ms, 2), W=[g1])
                xt = [tile(e2, f"o_xt{i}", [128, D], F32) for i in range(2)]
                t2 = [tile(e2, f"o_t2{i}", [128, D], F32) for i in range(2)]
                xdst = XS[b] if kind == "x" else CXS[b]
                for i in range(NT):
                    x_ = xt[i % 2]
                    t_ = t2[i % 2]
                    S.load(x_[:], xsrc[i * 128:(i + 1) * 128, :], W=[x_])
                    for hf in range(2):
                        p = S.ps()
                        for k in range(8):
                            S.op("pe", lambda e: e.matmul(p[:], lhsT=mergedT[:, k, i * 128:(i + 1) * 128], rhs=wo[:, k, hf * 512:(hf + 1) * 512], start=(k == 0), stop=(k == 7)),
                                 R=[mergedT, wo], W=[p])
                        S.op("dve", lambda e: e.tensor_tensor(out=t_[:, hf * 512:(hf + 1) * 512], in0=p[:], in1=g1[:, hf * 512:(hf + 1) * 512], op=ALU.mult), R=[p, g1], W=[t_])
                    S.op("pool", lambda e: e.tensor_tensor(out=t_[:], in0=t_[:], in1=x_[:], op=ALU.add), R=[t_, x_], W=[t_])
                    S.store(xdst[i * 128:(i + 1) * 128, :], t_[:], R=[t_])
                S.barrier()
            e3.close()
            chk(f"PC{l}_wout_{kind}{b}")
        S.barrier()

    chk(f"PC{l}_mix")
    with ExitStack() as es:
        wu = tile(es, "f_wu", [128, 8, 4096], BF16)
        wd = tile(es, "f_wd", [128, 32, D], BF16)
        for k in range(8):
            S.load(wu[:, k, :], Wb[("up", l)][k * 128:(k + 1) * 128, :], W=[wu])
        for k in range(32):
            S.load(wd[:, k, :], Wb[("down", l)][k * 128:(k + 1) * 128, :], W=[wd])
        ab = tile(es, "f_ab", [128, 3, D], F32)
        gfin = tile(es, "f_gfin", [128, D], F32)
        junk = tile(es, "f_junk", [128, D], F32)
        if last:
            bcast_row(es, gfin, norm_final.rearrange("(o d) -> o d", o=1), D, junk)
        xt = [tile(es, f"f_xt{i}", [128, D], F32) for i in range(4)]
        junk2 = tile(es, "f_junk2", [128, D], BF16)
        h2 = [tile(es, f"f_h2{i}", [128, D], BF16) for i in range(2)]
        h2T = tile(es, "f_h2T", [128, 8, 256], BF16)
        aT = tile(es, "f_aT", [128, 32, 256], BF16)
        rl = [tile(es, f"f_rl{i}", [128, 512], F32) for i in range(2)]
        t2s = [tile(es, f"f_t2{i}", [128, D], F32) for i in range(2)]
        ss = tile(es, "f_ss", [128, 2], F32)
        ssf = [tile(es, f"f_ssf{i}", [128, 2], F32) for i in range(2)]
        for (kind, b, Tn, xsrc, hdst, ms) in streams:
            if kind == "c" and l == 1:
                continue
            src = XS[b] if kind == "x" else CXS[b]
            S.load(ab[:, 0, :], mvap(ms, 4), W=[ab])
            S.load(ab[:, 1, :], mvap(ms, 3), W=[ab])
            S.load(ab[:, 2, :], mvap(ms, 5), W=[ab])
            blocks = list(range(0, Tn, 256))

            def ln1(bi):
                t0 = blocks[bi]
                for ii in range(2):
                    i = t0 // 128 + ii
                    x_ = xt[(bi % 2) * 2 + ii]
                    h_ = h2[ii]
                    S.load(x_[:], src[i * 128:(i + 1) * 128, :], W=[x_])
                    S.op("act", lambda e: e.activation(out=junk[:], in_=x_[:], func=AF.Square, accum_out=ss[:, 0:1]), R=[x_], W=[junk, ss])
                    S.op("act", lambda e: e.activation(out=ss[:, 1:2], in_=ss[:, 0:1], func=AF.Sqrt, scale=1.0 / D, bias=EPS), R=[ss], W=[ss])
                    S.op("dve", lambda e: e.reciprocal(out=ss[:, 1:2], in_=ss[:, 1:2]), R=[ss], W=[ss])
                    S.op("dve", lambda e: e.scalar_tensor_tensor(out=junk[:], in0=x_[:], scalar=ss[:, 1:2], in1=ab[:, 0, :], op0=ALU.mult, op1=ALU.mult), R=[x_, ss, ab], W=[junk])
                    S.op("dve", lambda e: e.tensor_tensor(out=h_[:], in0=junk[:], in1=ab[:, 1, :], op=ALU.add), R=[junk, ab], W=[h_])

            def ln2(bi):
                for ii in range(2):
                    h_ = h2[ii]
                    pb = S.psb()
                    for k in range(8):
                        S.op("pe", lambda e: e.transpose(out=pb[:, k * 128:(k + 1) * 128], in_=h_[:, k * 128:(k + 1) * 128], identity=ident_bf[:]), R=[h_, ident_bf], W=[pb])
                    S.op("act", lambda e: e.copy(out=h2T[:, :, ii * 128:(ii + 1) * 128], in_=pb[:].rearrange("p (k t) -> p k t", k=8)), R=[pb], W=[h2T])

            def up(bi):
                for f0 in range(0, 32, 2):
                    p = S.ps()
                    for ff in range(2):
                        f = f0 + ff
                        for k in range(8):
                            S.op("pe", lambda e: e.matmul(p[:, ff * 256:(ff + 1) * 256], lhsT=wu[:, k, f * 128:(f + 1) * 128], rhs=h2T[:, k, :], start=(k == 0), stop=(k == 7)),
                                 R=[wu, h2T], W=[p])
                    r_ = rl[(f0 // 2) % 2]
                    S.op("act", lambda e: e.activation(out=r_[:], in_=p[:], func=AF.Relu), R=[p], W=[r_])
                    S.op("pool" if (f0 // 2) % 2 else "dve", lambda e: e.tensor_tensor(out=aT[:, f0:f0 + 2, :].rearrange("p a b -> p (a b)"), in0=r_[:], in1=r_[:], op=ALU.mult), R=[r_], W=[aT])

            def down(bi):
                t0 = blocks[bi]
                for ii in range(2):
                    i = t0 // 128 + ii
                    x_ = xt[(bi % 2) * 2 + ii]
                    t2 = t2s[ii]
                    sf = ssf[ii]
                    for hf in range(2):
                        p = S.ps()
                        for f in range(32):
                            S.op("pe", lambda e: e.matmul(p[:], lhsT=aT[:, f, ii * 128:(ii + 1) * 128], rhs=wd[:, f, hf * 512:(hf + 1) * 512], start=(f == 0), stop=(f == 31)), R=[aT, wd], W=[p])
                        S.op("dve", lambda e: e.tensor_tensor(out=t2[:, hf * 512:(hf + 1) * 512], in0=p[:], in1=ab[:, 2, hf * 512:(hf + 1) * 512], op=ALU.mult), R=[p, ab], W=[t2])
                    S.op("pool", lambda e: e.tensor_tensor(out=t2[:], in0=t2[:], in1=x_[:], op=ALU.add), R=[t2, x_], W=[t2])
                    if last:
                        S.op("act", lambda e: e.activation(out=junk2[:], in_=t2[:], func=AF.Square, accum_out=sf[:, 0:1]), R=[t2], W=[junk2, sf])
                        S.op("act", lambda e: e.activation(out=sf[:, 1:2], in_=sf[:, 0:1], func=AF.Sqrt, scale=1.0 / D, bias=EPS), R=[sf], W=[sf])
                        S.op("dve", lambda e: e.reciprocal(out=sf[:, 1:2], in_=sf[:, 1:2]), R=[sf], W=[sf])
                        S.op("dve", lambda e: e.scalar_tensor_tensor(out=t2[:], in0=t2[:], scalar=sf[:, 1:2], in1=gfin[:], op0=ALU.mult, op1=ALU.mult), R=[t2, sf, gfin], W=[t2])
                        S.store(out[b, i * 128:(i + 1) * 128, :], t2[:], R=[t2])
                    else:
                        S.store(src[i * 128:(i + 1) * 128, :], t2[:], R=[t2])

            ln1(0)
            ln2(0)
            for bi in range(len(blocks)):
                up(bi)
                if bi + 1 < len(blocks):
                    ln1(bi + 1)
                down(bi)
                if bi + 1 < len(blocks):
                    ln2(bi + 1)
        S.barrier()


class StopBuild(Exception):
    pass


def build(stop_after=None, dbg=()):
    nc = bass.Bass("TRN2", target_bir_lowering=False)
    es0 = ExitStack()

    def din(name, shape, dt=F32):
        return nc.dram_tensor(name, list(shape), dt, kind="ExternalInput").ap()

    def dscr(name, shape, dt):
        kind = "ExternalOutput" if name in dbg else "Internal"
        return nc.dram_tensor(name, list(shape), dt, kind=kind).ap()

    x_in = din("x", [NB, T, D])
    ctx_in = din("ctx", [NB, TC, D])
    cT_in = din("cT", [128, 40])
    w_mod = din("w_mod", [2, D, 6144])
    b_mod = din("b_mod", [2, 6144])
    norm_mix = din("norm_mix", [2, D])
    norm_ffn = din("norm_ffn", [2, D])
    w_in = din("w_in", [2, D, N_IN])
    b_in = din("b_in", [2, N_IN])
    hy_w1 = din("hy_f_w1", [2, 17, 64])
    hy_w2 = din("hy_f_w2", [2, 64, 64])
    hy_w3 = din("hy_f_w3", [2, 64, 2048])
    wa_f = din("gla_wa_f", [2, 16, 256])
    ba_f = din("gla_ba_f", [2, 256])
    wa_b = din("gla_wa_b", [2, 16, 256])
    ba_b = din("gla_ba_b", [2, 256])
    gla_nw = din("gla_norm_w", [2, 128])
    pool_w = din("pool_w", [2, 512, 128])
    w_brs = [din("w_br_hy", [2, 512, D]), din("w_br_gla", [2, 512, D]), din("w_br_pool", [2, 512, D])]
    w_out = din("w_out", [2, D, D])
    w_up = din("w_up", [2, D, 4096])
    w_down = din("w_down", [2, 4096, D])
    norm_final = din("norm_final", [D])
    pp_in = din("pp", [2, 128, NPP])
    hy_skip = din("hy_skip", [2, 2, 512])
    C = {}
    for name, shape, dt in (("ident_bf", [128, 128], BF16), ("identJ_bf", [128, 128], BF16), ("ident_f", [128, 128], F32),
                            ("tri", [128, 4, 128], BF16), ("msk", [128, 2, 128], F32), ("pmat", [128, NU, 128], BF16),
                            ("rcnt_g", [128, 4, T], F32), ("rcnt_c", [128, 4, TC], F32),
                            ("zT_x", [32, 2 * T], F32), ("posn_x", [128, 2, 2 * T], F32),
                            ("zT_c", [32, 2 * TC], F32), ("posn_c", [128, 2, 2 * TC], F32),
                            ("F1x", [64, 64, 128], BF16), ("F1k", [64, 64, 128], BF16), ("F2", [64, 8, 128], BF16),
                            ("F3", [128, 128], BF16), ("F4", [64, 64, 2, 64], BF16), ("wtab", [64, 64, 512], F32), ("zT_f", [32, 4096], F32),
                            ("ones_f", [1, 128], F32), ("ones_bf", [1, 128], BF16)):
        C[name] = din("c_" + name, shape, dt)
    out = nc.dram_tensor("out", [NB, T, D], F32, kind="ExternalOutput").ap()

    Wb = {}
    for l in range(2):
        Wb[("mod", l)] = dscr(f"wb_mod{l}", [D, 6144], BF16)
        Wb[("in", l)] = dscr(f"wb_in{l}", [D, N_IN], BF16)
        for i in range(3):
            Wb[("br", i, l)] = dscr(f"wb_br{i}_{l}", [512, D], BF16)
        Wb[("out", l)] = dscr(f"wb_out{l}", [D, D], BF16)
        Wb[("up", l)] = dscr(f"wb_up{l}", [D, 4096], BF16)
        Wb[("down", l)] = dscr(f"wb_down{l}", [4096, D], BF16)
        Wb[("pw", l)] = dscr(f"wb_pw{l}", [512, 128], BF16)
    MODV = dscr("modv", [2, 5, 128, 6144], F32)
    XS = dscr("xs", [NB, T, D], F32)
    CXS = dscr("cxs", [NB, TC, D], F32)
    HXT = dscr("hxt", [NB, 128, 8 * T], BF16)
    HXTC = dscr("hxtc", [NB, 128, 8 * TC], BF16)
    U = dscr("u", [12, NB, T, 128], BF16)
    UC = dscr("uc", [12, NB, TC, 128], BF16)
    YT = dscr("yt", [NB, 512, T], BF16)
    YTC = dscr("ytc", [NB, 512, TC], BF16)
    KB = dscr("kb", [2, 512, 2 * T], BF16)
    KBC = dscr("kbc", [2, 512, 2 * TC], BF16)
    KF = dscr("kf", [2, 4, 2, 128, 8192], BF16)

    S = Sched(nc, es0)
    with es0:
        for i in range(6):
            S.ps_ring.append(Tl(es0.enter_context(nc.psum_tensor(f"ps{i}", [128, 512], F32))))
        for i in range(2):
            S.psb_ring.append(Tl(es0.enter_context(nc.psum_tensor(f"psb{i}", [128, 1024], BF16))))

        tcount = [0]

        def tile(es, name, shape, dt):
            tcount[0] += 1
            return Tl(es.enter_context(nc.sbuf_tensor(f"t{tcount[0]}_{name}", list(shape), dt)))

        ident_bf = tile(es0, "ident_bf", [128, 128], BF16)
        identJ_bf = tile(es0, "identJ_bf", [128, 128], BF16)
        ident_f = tile(es0, "ident_f", [128, 128], F32)
        ones_f = tile(es0, "ones_f", [1, 128], F32)
        ones_bf = tile(es0, "ones_bf", [1, 128], BF16)
        pp = [tile(es0, f"pp{l}", [128, NPP], F32) for l in range(2)]
        for tl_, src in ((ident_bf, C["ident_bf"]), (identJ_bf, C["identJ_bf"]), (ident_f, C["ident_f"]),
                         (ones_f, C["ones_f"]), (ones_bf, C["ones_bf"]), (pp[0], pp_in[0]), (pp[1], pp_in[1])):
            S.load(tl_[:], src, W=[tl_])

        engs3 = ["dve", "act", "pool"]

        def cast(k, out_ap, in_ap, R, W):
            if k == "act":
                S.op("act", lambda e: e.copy(out=out_ap, in_=in_ap), R=R, W=W)
            else:
                S.op(k, lambda e: e.tensor_copy(out=out_ap, in_=in_ap), R=R, W=W)

        def bcast_row(es, dst, src_row_ap, n, scratch_row):
            S.load(scratch_row[0:1, 0:n], src_row_ap, W=[scratch_row])
            for c0 in range(0, n, 512):
                cw = min(512, n - c0)
                p = S.ps()
                S.op("pe", lambda e: e.matmul(p[:, 0:cw], lhsT=ones_f[0:1, :], rhs=scratch_row[0:1, c0:c0 + cw], start=True, stop=True),
                     R=[ones_f, scratch_row], W=[p])
                S.op("act", lambda e: e.copy(out=dst[:, c0:c0 + cw], in_=p[:, 0:cw]), R=[p], W=[dst])

        with ExitStack() as es:
            wf = [tile(es, f"pw_f{i}", [128, N_IN], F32) for i in range(2)]
            wb = [tile(es, f"pw_b{i}", [128, N_IN], BF16) for i in range(2)]
            it = 0
            for l in range(2):
                jobs = [(w_mod[l], Wb[("mod", l)], D, 6144), (w_in[l], Wb[("in", l)], D, N_IN),
                        (w_brs[0][l], Wb[("br", 0, l)], 512, D), (w_brs[1][l], Wb[("br", 1, l)], 512, D),
                        (w_brs[2][l], Wb[("br", 2, l)], 512, D), (w_out[l], Wb[("out", l)], D, D),
                        (w_up[l], Wb[("up", l)], D, 4096), (w_down[l], Wb[("down", l)], 4096, D),
                        (pool_w[l], Wb[("pw", l)], 512, 128)]
                for src, dst, rows, cols in jobs:
                    for r0 in range(0, rows, 128):
                        a, b_ = wf[it % 2], wb[it % 2]
                        S.load(a[:, 0:cols], src[r0:r0 + 128, :], W=[a])
                        cast(engs3[it % 3], b_[:, 0:cols], a[:, 0:cols], [a], [b_])
                        S.store(dst[r0:r0 + 128, :], b_[:, 0:cols], R=[b_])
                        it += 1
            S.barrier()
        if stop_after == "PW":
            return nc

        with ExitStack() as es:
            cT = tile(es, "cT", [128, 40], F32)
            sT = tile(es, "sT", [128, 40], BF16)
            srep = tile(es, "srep", [128, 40, 128], BF16)
            wm = tile(es, "wm", [128, 8, 6144], BF16)
            brow_f = tile(es, "brow_f", [1, 6144], F32)
            brow = tile(es, "brow", [1, 6144], BF16)
            gmix = tile(es, "gmix", [128, D], F32)
            gffn = tile(es, "gffn", [128, D], F32)
            mv = [tile(es, f"mv{i}", [128, 6144], F32) for i in range(2)]
            S.load(cT[:], cT_in, W=[cT])
            S.op("act", lambda e: e.activation(out=sT[:], in_=cT[:], func=AF.Silu), R=[cT], W=[sT])
            S.op("dve", lambda e: e.tensor_copy(out=srep[:], in_=sT[:].unsqueeze(2).to_broadcast([128, 40, 128])), R=[sT], W=[srep])
            for l in range(2):
                for k in range(8):
                    S.load(wm[:, k, :], Wb[("mod", l)][k * 128:(k + 1) * 128, :], W=[wm])
                S.load(brow_f[:], b_mod[l:l + 1, :], W=[brow_f])
                S.op("dve", lambda e: e.tensor_copy(out=brow[:], in_=brow_f[:]), R=[brow_f], W=[brow])
                bcast_row(es, gmix, norm_mix[l:l + 1, :], D, brow_f)
                bcast_row(es, gffn, norm_ffn[l:l + 1, :], D, brow_f)
                for s in range(5):
                    m = mv[s % 2]
                    for cb in range(12):
                        p = S.ps()
                        for k in range(8):
                            S.op("pe", lambda e: e.matmul(p[:], lhsT=srep[:, k * 5 + s, :], rhs=wm[:, k, cb * 512:(cb + 1) * 512], start=(k == 0), stop=False),
                                 R=[srep, wm], W=[p])
                        S.op("pe", lambda e: e.matmul(p[:], lhsT=ones_bf[0:1, :], rhs=brow[0:1, cb * 512:(cb + 1) * 512], start=False, stop=True),
                             R=[ones_bf, brow], W=[p])
                        sp_ = cb // 2
                        hs = (cb % 2) * 512
                        dst = m[:, cb * 512:(cb + 1) * 512]
                        if sp_ == 1:
                            S.op("dve", lambda e: e.scalar_tensor_tensor(out=dst, in0=p[:], scalar=1.0, in1=gmix[:, hs:hs + 512], op0=ALU.add, op1=ALU.mult),
                                 R=[p, gmix], W=[m])
                        elif sp_ == 4:
                            S.op("dve", lambda e: e.scalar_tensor_tensor(out=dst, in0=p[:], scalar=1.0, in1=gffn[:, hs:hs + 512], op0=ALU.add, op1=ALU.mult),
                                 R=[p, gffn], W=[m])
                        else:
                            S.op("act", lambda e: e.copy(out=dst, in_=p[:]), R=[p], W=[m])
                    S.store(MODV[l, s], m[:], R=[m])
            S.barrier()
        if stop_after == "P0":
            return nc
        MV_B1, MV_A1, MV_G1, MV_B2, MV_A2, MV_G2 = 0, 1, 2, 3, 4, 5

        def mv_ap(l, s, j):
            return MODV[l, s][:, j * D:(j + 1) * D]


        def evac(idx, out_ap, in_ap, R, W):
            if idx % 4 != 3:
                S.op("act", lambda e: e.copy(out=out_ap, in_=in_ap), R=R, W=W)
            else:
                S.op("dve", lambda e: e.tensor_copy(out=out_ap, in_=in_ap), R=R, W=W)

        def fft_stage1(src, srcR, F1t, A1v, PAb):
            for g_ in range(16):
                p = S.ps()
                for q in range(4):
                    n2 = 4 * g_ + q
                    S.op("pe", lambda e: e.matmul(p[:, q * 128:(q + 1) * 128], lhsT=F1t[:, n2, :], rhs=src[:, n2, :], start=True, stop=True), R=[F1t] + srcR, W=[p])
                S.op("act", lambda e: e.copy(out=A1v[:, 4 * g_:4 * g_ + 4, :], in_=p[:].rearrange("p (a b) -> p a b", a=4)), R=[p], W=[PAb[g_]])

        def fft_transA(blk, A1v, PAb, Q, Qb):
            c0 = blk * 8
            pb = S.psb()
            for q in range(8):
                S.op("pe", lambda e: e.transpose(out=pb[0:64, q * 128:(q + 1) * 128], in_=A1v[:, :, c0 + q], identity=ident_bf[:]), R=PAb + [ident_bf], W=[pb])
            evac(blk, Q[:, c0:c0 + 8, :], pb[0:64, :].rearrange("p (a b) -> p a b", a=8), [pb], [Qb[blk]])

        def fft_stage2(blk, Q, Qb, F2t, i0):
            c0 = blk * 8
            pa = S.ps()
            pb_ = S.ps()
            for pp_, ia in ((pa, i0), (pb_, i0 + 2)):
                S.op("pe", lambda e: e.matmul(pp_[:], lhsT=F2t[:, ia, :], rhs=Q[:, c0:c0 + 8, 0:64], start=True, stop=False), R=[F2t, Qb[blk]], W=[pp_])
                S.op("pe", lambda e: e.matmul(pp_[:], lhsT=F2t[:, ia + 1, :], rhs=Q[:, c0:c0 + 8, 64:128], start=False, stop=True), R=[F2t, Qb[blk]], W=[pp_])
            return pa, pb_

        def spec_gen(es2, l, a2, w3):
            F1k = tile(es2, "s_F1k", [64, 64, 128], BF16)
            F2t = tile(es2, "s_F2", [64, 8, 128], BF16)
            a2b = tile(es2, "s_a2b", [64, 4096], BF16)
            w3b = tile(es2, "s_w3b", [64, 2048], BF16)
            kc = tile(es2, "s_kc", [64, 64, 256], BF16)
            wt = [tile(es2, f"s_wt{i}", [64, 4, 256], F32) for i in range(2)]
            skr = tile(es2, "s_skr", [1, 1024], F32)
            A1 = tile(es2, "s_A1", [128, 64, 128], BF16)
            A1T = tile(es2, "s_A1T", [64, 128, 128], BF16)
            kab = [tile(es2, f"s_kab{i}", [128, 2, 512], BF16) for i in range(2)]
            PAb = [Buf() for _ in range(16)]
            Qb = [Buf() for _ in range(16)]
            kcb = [Buf() for _ in range(16)]
            S.load(F1k[:], C["F1k"], W=[F1k])
            S.load(F2t[:], C["F2"], W=[F2t])
            S.load(skr[:], hy_skip[l].rearrange("(x o) c -> x (o c)", x=1), W=[skr])
            S.op("act", lambda e: e.copy(out=a2b[:], in_=a2[:]), R=[a2], W=[a2b])
            S.op("pool", lambda e: e.tensor_copy(out=w3b[:], in_=w3[:]), R=[w3], W=[w3b])
            it = 0
            wi = 0
            for o in range(2):
                for ch in range(2):
                    colf = 0 * 1024 + o * 512 + ch * 256
                    colb = 1 * 1024 + o * 512 + ch * 256
                    for g_ in range(16):
                        w_ = wt[wi % 2]
                        wi += 1
                        S.load(w_[:], C["wtab"][:, 4 * g_:4 * g_ + 4, ch * 256:(ch + 1) * 256], W=[w_])
                        for q2 in range(2):
                            pf = S.ps()
                            for q in range(2):
                                n2 = 4 * g_ + 2 * q2 + q
                                S.op("pe", lambda e: e.matmul(pf[0:64, q * 256:(q + 1) * 256], lhsT=a2b[:, n2::64], rhs=w3b[:, colf:colf + 256], start=True, stop=True), R=[a2b, w3b], W=[pf])
                            pg = S.ps()
                            for q in range(2):
                                n2 = 4 * g_ + 2 * q2 + q
                                S.op("pe", lambda e: e.matmul(pg[0:64, q * 256:(q + 1) * 256], lhsT=a2b[:, n2::64], rhs=w3b[:, colb:colb + 256], start=True, stop=True), R=[a2b, w3b], W=[pg])
                            n2a = 4 * g_ + 2 * q2
                            S.op("dve", lambda e: e.tensor_tensor(out=kc[0:32, n2a:n2a + 2, :], in0=pf[0:32, :].rearrange("p (a b) -> p a b", a=2), in1=w_[0:32, 2 * q2:2 * q2 + 2, :], op=ALU.mult),
                                 R=[pf, w_], W=[kcb[g_]])
                            S.op("dve", lambda e: e.tensor_tensor(out=kc[32:64, n2a:n2a + 2, :], in0=pg[32:64, :].rearrange("p (a b) -> p a b", a=2), in1=w_[32:64, 2 * q2:2 * q2 + 2, :], op=ALU.mult),
                                 R=[pg, w_], W=[kcb[g_]])
                            if n2a == 0:
                                S.op("dve", lambda e: e.tensor_tensor(out=kc[0:1, 0, :], in0=kc[0:1, 0, :], in1=skr[0:1, o * 512 + ch * 256:o * 512 + (ch + 1) * 256], op=ALU.add),
                                     R=[kcb[0], skr], W=[kcb[0]])
                    for c2 in range(2):
                        cc = 2 * ch + c2
                        fft_stage1(kc[:, :, c2 * 128:(c2 + 1) * 128], kcb, F1k, A1, PAb)
                        for blk_ in range(17):
                            if blk_ < 16:
                                fft_transA(blk_, A1, PAb, A1T, Qb)
                            if blk_ >= 1:
                                blk = blk_ - 1
                                pa, pb_ = fft_stage2(blk, A1T, Qb, F2t, 4)
                                kb_ = kab[it % 2]
                                it += 1
                                S.op("act", lambda e: e.copy(out=kb_[:, 0, :], in_=pa[:]), R=[pa], W=[kb_])
                                S.op("dve", lambda e: e.tensor_copy(out=kb_[:, 1, :], in_=pb_[:]), R=[pb_], W=[kb_])
                                S.store(KF[o, cc][:, :, blk * 512:(blk + 1) * 512].rearrange("t p n -> p t n"), kb_[:], R=[kb_])

        def hyena_fft(l):
            with ExitStack() as es2:
                F1x = tile(es2, "x_F1x", [64, 64, 128], BF16)
                F2t = tile(es2, "x_F2", [64, 8, 128], BF16)
                F3t = tile(es2, "x_F3", [128, 128], BF16)
                F4t = tile(es2, "x_F4", [64, 64, 2, 64], BF16)
                for t_, nm in ((F1x, "F1x"), (F2t, "F2"), (F3t, "F3"), (F4t, "F4")):
                    S.load(t_[:], C[nm], W=[t_])
                VY = tile(es2, "x_VY", [64, 64, 128], BF16)
                X = tile(es2, "x_X", [64, 64, 128], BF16)
                Z = tile(es2, "x_Z", [64, 64, 128], BF16)
                A1 = tile(es2, "x_A1", [128, 64, 128], BF16)
                Yt = tile(es2, "x_Yt", [128, 128, 64], BF16)
                G1 = tile(es2, "x_G1", [128, 128, 64], BF16)
                Q = tile(es2, "x_Q", [64, 128, 128], BF16)
                yT = tile(es2, "x_yT", [128, 2, T], BF16)
                kab = [tile(es2, f"x_kab{i}", [128, 2, 512], BF16) for i in range(4)]
                t1 = [tile(es2, f"x_t1{i}", [128, 512], F32) for i in range(2)]
                t2 = [tile(es2, f"x_t2{i}", [128, 512], F32) for i in range(2)]
                PAb = [Buf() for _ in range(16)]
                Qb = [Buf() for _ in range(16)]
                Ytb = [Buf() for _ in range(16)]
                Gb = [Buf() for _ in range(16)]
                VYb = [Buf() for _ in range(16)]
                Zb = [Buf() for _ in range(16)]
                cnt = [0]

                def conv(src, srcb, o, cc, dst, dstb):
                    fft_stage1(src, srcb, F1x, A1, PAb)
                    st = {}

                    def sA(blk):
                        fft_transA(blk, A1, PAb, Q, Qb)

                    def s2(blk):
                        c0 = blk * 8
                        kb_ = kab[cnt[0] % 4]
                        ta, tb = t1[cnt[0] % 2], t2[cnt[0] % 2]
                        cnt[0] += 1
                        S.load(kb_[:], KF[o, cc][:, :, blk * 512:(blk + 1) * 512].rearrange("t p n -> p t n"), W=[kb_])
                        pa, pb_ = fft_stage2(blk, Q, Qb, F2t, 0)
                        S.op("dve", lambda e: e.tensor_tensor(out=ta[:], in0=pa[:], in1=kb_[:, 0, :], op=ALU.mult), R=[pa, kb_], W=[ta])
                        S.op("dve", lambda e: e.tensor_tensor(out=tb[:], in0=pb_[:], in1=kb_[:, 1, :], op=ALU.mult), R=[pb_, kb_], W=[tb])
                        S.op("pool", lambda e: e.tensor_tensor(out=Yt[:, c0:c0 + 8, :].rearrange("p a b -> p (a b)"), in0=ta[:], in1=tb[:], op=ALU.add), R=[ta, tb], W=[Ytb[blk]])

                    def s3(blk):
                        c0 = blk * 8
                        p = S.ps()
                        S.op("pe", lambda e: e.matmul(p[:], lhsT=F3t[:], rhs=Yt[:, c0:c0 + 8, :].rearrange("p a b -> p (a b)"), start=True, stop=True), R=[F3t, Ytb[blk]], W=[p])
                        S.op("act", lambda e: e.copy(out=G1[:, c0:c0 + 8, :].rearrange("p a b -> p (a b)"), in_=p[:]), R=[p], W=[Gb[blk]])

                    def sB(blk):
                        c0 = blk * 8
                        pb = S.psb()
                        for q in range(8):
                            S.op("pe", lambda e: e.transpose(out=pb[0:64, q * 128:(q + 1) * 128], in_=G1[:, c0 + q, :], identity=ident_bf[:]), R=[Gb[blk], ident_bf], W=[pb])
                        evac(blk + 1, Q[:, c0:c0 + 8, :], pb[0:64, :].rearrange("p (a b) -> p a b", a=8), [pb], [Qb[blk]])

                    for i_ in range(16 + 6):
                        if i_ < 16:
                            sA(i_)
                        if 0 <= i_ - 2 < 16:
                            s2(i_ - 2)
                        if 0 <= i_ - 4 < 16:
                            s3(i_ - 4)
                        if 0 <= i_ - 6 < 16:
                            sB(i_ - 6)
                    for g_ in range(16):
                        p = S.ps()
                        for q in range(4):
                            n2 = 4 * g_ + q
                            S.op("pe", lambda e: e.matmul(p[0:64, q * 128:(q + 1) * 128], lhsT=F4t[:, n2, 0, :], rhs=Q[:, :, n2], start=True, stop=False), R=[F4t] + Qb, W=[p])
                            S.op("pe", lambda e: e.matmul(p[0:64, q * 128:(q + 1) * 128], lhsT=F4t[:, n2, 1, :], rhs=Q[:, :, 64 + n2], start=False, stop=True), R=[F4t] + Qb, W=[p])
                        S.op("dve", lambda e: e.tensor_tensor(out=dst[:, 4 * g_:4 * g_ + 4, :], in0=X[:, 4 * g_:4 * g_ + 4, :], in1=p[0:64, :].rearrange("p (a b) -> p a b", a=4), op=ALU.mult),
                             R=[X, p], W=[dstb[g_]])

                for cc in range(4):
                    for pr in range(2):
                        for ri in range(2):
                            b = 2 * pr + ri
                            S.load(VY[32 * ri:32 * ri + 32], U[8 + cc, b].rearrange("(n1 n2) c -> n1 n2 c", n2=64), W=VYb)
                        for ri in range(2):
                            S.load(X[32 * ri:32 * ri + 32], U[0 + cc, 2 * pr + ri].rearrange("(n1 n2) c -> n1 n2 c", n2=64), W=[X])
                        conv(VY, VYb, 0, cc, Z, Zb)
                        for ri in range(2):
                            S.load(X[32 * ri:32 * ri + 32], U[4 + cc, 2 * pr + ri].rearrange("(n1 n2) c -> n1 n2 c", n2=64), W=[X])
                        conv(Z, Zb, 1, cc, VY, VYb)
                        for n2q in range(0, 64, 16):
                            pb = S.psb()
                            for q in range(16):
                                S.op("pe", lambda e: e.transpose(out=pb[:, q * 64:(q + 1) * 64], in_=VY[:, n2q + q, :], identity=ident_bf[0:64, 0:64]), R=VYb + [ident_bf], W=[pb])
                            S.op("act", lambda e: e.copy(out=yT[:].rearrange("c b (n1 n2) -> c b n1 n2", n2=64)[:, :, :, n2q:n2q + 16],
                                                         in_=pb[:].rearrange("c (n2 b n1) -> c b n1 n2", n2=16, b=2)), R=[pb], W=[yT])
                        for ri in range(2):
                            S.store(YT[2 * pr + ri, cc * 128:(cc + 1) * 128, :], yT[:, ri, :], R=[yT])
                S.barrier()

        def chk(name):
            if name == stop_after:
                S.barrier()
                S.dead = True

        for l in range(2):
            streams = []
            for b in range(NB):
                if l == 0:
                    streams.append(("c", b, TC, ctx_in[b], HXTC[b], 4))
                else:
                    streams.append(("c", b, TC, CXS[b], HXTC[b], 4))
                streams.append(("x", b, T, x_in[b] if l == 0 else XS[b], HXT[b], b))

            with ExitStack() as es:
                w1 = tile(es, "f_w1", [32, 64], F32)
                w2 = tile(es, "f_w2", [64, 64], F32)
                w3 = tile(es, "f_w3", [64, 2048], F32)
                fb = tile(es, "f_fb", [64, 2], F32)
                S.op("dve", lambda e: e.memset(w1[:], 0.0), W=[w1])
                S.load(w1[0:17, :], hy_w1[l], W=[w1])
                S.load(w2[:], hy_w2[l], W=[w2])
                S.load(w3[:], hy_w3[l], W=[w3])
                ppl = pp[l]
                S.op("dve", lambda e: e.tensor_tensor(out=fb[:, 0:1], in0=ppl[0:64, 94:95], in1=ppl[0:64, 95:96], op=ALU.mult), R=[ppl], W=[fb])
                S.op("dve", lambda e: e.tensor_tensor(out=fb[:, 1:2], in0=ppl[0:64, 94:95], in1=ppl[0:64, 96:97], op=ALU.mult), R=[ppl], W=[fb])
                for kind, L_, zsrc, psrc, kdst in (("x", T, C["zT_f"], C["posn_x"], KB), ("c", TC, C["zT_c"], C["posn_c"], KBC)):
                    if kind == "c" and l == 1:
                        continue
                    NP = 2 * L_
                    with ExitStack() as es2:
                        zT = tile(es2, "f_zT", [32, NP], F32)
                        a1 = tile(es2, "f_a1", [64, NP], F32)
                        a2 = tile(es2, "f_a2", [64, NP], F32)
                        tA = tile(es2, "f_tA", [64, 512], F32)
                        tB = tile(es2, "f_tB", [64, 512], F32)
                        S.load(zT[:], zsrc, W=[zT])
                        if kind == "c":
                            posn = tile(es2, "f_posn", [128, 2, NP], F32)
                            win = [tile(es2, f"f_win{i}", [128, 512], F32) for i in range(2)]
                            krow = [tile(es2, f"f_krow{i}", [128, NP], F32) for i in range(2)]
                            kbf = [tile(es2, f"f_kbf{i}", [128, NP], BF16) for i in range(2)]
                            S.load(posn[:], psrc, W=[posn])

                        def sin_layer(dst, lhsT_ap, lhs_tl, src, fbcol):
                            for c0 in range(0, NP, 512):
                                p = S.ps()
                                S.op("pe", lambda e: e.matmul(p[0:64, :], lhsT=lhsT_ap, rhs=src[:, c0:c0 + 512], start=True, stop=True),
                                     R=[lhs_tl, src], W=[p])
                                S.op("dve", lambda e: e.tensor_scalar(out=tA[:], in0=p[0:64, :], scalar1=ppl[0:64, 94:95], scalar2=fb[:, fbcol:fbcol + 1], op0=ALU.mult, op1=ALU.add),
                                     R=[p, ppl, fb], W=[tA])
                                S.op("dve", lambda e: e.tensor_scalar(out=tB[:], in0=tA[:], scalar1=PI, scalar2=2 * PI, op0=ALU.is_gt, op1=ALU.mult), R=[tA], W=[tB])
                                S.op("dve", lambda e: e.tensor_tensor(out=tA[:], in0=tA[:], in1=tB[:], op=ALU.subtract), R=[tA, tB], W=[tA])
                                S.op("dve", lambda e: e.tensor_scalar(out=tB[:], in0=tA[:], scalar1=-PI, scalar2=2 * PI, op0=ALU.is_lt, op1=ALU.mult), R=[tA], W=[tB])
                                S.op("dve", lambda e: e.tensor_tensor(out=tA[:], in0=tA[:], in1=tB[:], op=ALU.add), R=[tA, tB], W=[tA])
                                S.op("act", lambda e: e.activation(out=dst[:, c0:c0 + 512], in_=tA[:], func=AF.Sin), R=[tA], W=[dst])

                        sin_layer(a1, w1[:, :], w1, zT, 0)
                        sin_layer(a2, w2[:, :], w2, a1, 1)
                        if kind == "x":
                            spec_gen(es2, l, a2, w3)
                            S.barrier()
                            continue
                        it = 0
                        for o in range(2):
                            for cc in range(4):
                                kr, kb_ = krow[it % 2], kbf[it % 2]
                                it += 1
                                if o == 1:
                                    S.op("pool", lambda e: e.memset(kr[:, 0:1], 0.0), W=[kr])
                                BS = min(512, L_)
                                for blk in range(NP // BS):
                                    first_half = blk * BS < L_
                                    if o == 0:
                                        dr = 0 if first_half else 1
                                        s0, d0, n = blk * BS, blk * BS, BS
                                    else:
                                        dr = 1 if first_half else 0
                                        if blk == 0:
                                            s0, d0, n = 0, 1, BS - 1
                                        else:
                                            s0, d0, n = blk * BS - 1, blk * BS, BS
                                    col0 = dr * 1024 + o * 512 + cc * 128
                                    p = S.ps()
                                    S.op("pe", lambda e: e.matmul(p[:, 0:n], lhsT=w3[:, col0:col0 + 128], rhs=a2[:, s0:s0 + n], start=True, stop=True),
                                         R=[w3, a2], W=[p])
                                    wt = win[blk % 2]
                                    S.op("act", lambda e: e.activation(out=wt[:, 0:n], in_=posn[:, o, d0:d0 + n], func=AF.Exp, scale=ppl[:, 98 + cc:99 + cc]),
                                         R=[posn, ppl], W=[wt])
                                    S.op("dve", lambda e: e.tensor_tensor(out=kr[:, d0:d0 + n], in0=p[:, 0:n], in1=wt[:, 0:n], op=ALU.mult), R=[p, wt], W=[kr])
                                zc = (L_ - 1) if o == 0 else L_
                                S.op("dve", lambda e: e.tensor_scalar(out=kr[:, zc:zc + 1], in0=kr[:, zc:zc + 1], scalar1=ppl[:, 102 + 4 * o + cc:103 + 4 * o + cc], scalar2=None, op0=ALU.add),
                                     R=[kr, ppl], W=[kr])
                                S.op("pool", lambda e: e.tensor_copy(out=kb_[:], in_=kr[:]), R=[kr], W=[kb_])
                                S.store(kdst[o, cc * 128:(cc + 1) * 128, :], kb_[:], R=[kb_])
                        S.barrier()
                S.barrier()
            if stop_after == f"PF{l}":
                return nc

            with ExitStack() as es:
                why = tile(es, "a_why", [128, 8, 1536], BF16)
                for k in range(8):
                    S.load(why[:, k, :], Wb[("in", l)][k * 128:(k + 1) * 128, 0:1536], W=[why])
                ppl = pp[l]
                hxT = tile(es, "a_hxT", [128, 8, T], BF16)
                ab = tile(es, "a_ab", [128, 2, D], F32)
                xt = [tile(es, f"a_xt{i}", [128, D], F32) for i in range(2)]
                junk = tile(es, "a_junk", [128, D], F32)
                ss = tile(es, "a_ss", [128, 2], F32)
                hx = [tile(es, f"a_hx{i}", [128, D], BF16) for i in range(2)]
                raw = [tile(es, f"a_raw{i}", [128, T + 2], F32) for i in range(2)]
                tmp = tile(es, "a_tmp", [128, T], F32)
                ub = [tile(es, f"a_u{i}", [128, T], BF16) for i in range(2)]
                stg = [tile(es, f"a_stg{i}", [128, 16, 128], BF16) for i in range(2)]
                for r_ in raw:
                    S.op("pool", lambda e: e.memset(r_[:], 0.0), W=[r_])
                last_mod = None
                for (kind, b, Tn, xsrc, hdst, ms) in streams:
                    if kind == "c" and l == 1:
                        pass
                    NT = Tn // 128
                    if last_mod != ms:
                        S.load(ab[:, 0, :], mv_ap(l, ms, MV_A1), W=[ab])
                        S.load(ab[:, 1, :], mv_ap(l, ms, MV_B1), W=[ab])
                        last_mod = ms
                    for i in range(NT):
                        x_ = xt[i % 2]
                        h_ = hx[i % 2]
                        S.load(x_[:], xsrc[i * 128:(i + 1) * 128, :], W=[x_])
                        S.op("act", lambda e: e.activation(out=junk[:], in_=x_[:], func=AF.Square, accum_out=ss[:, 0:1]), R=[x_], W=[junk, ss])
                        S.op("act", lambda e: e.activation(out=ss[:, 1:2], in_=ss[:, 0:1], func=AF.Sqrt, scale=1.0 / D, bias=EPS), R=[ss], W=[ss])
                        S.op("dve", lambda e: e.reciprocal(out=ss[:, 1:2], in_=ss[:, 1:2]), R=[ss], W=[ss])
                        S.op("dve", lambda e: e.scalar_tensor_tensor(out=junk[:], in0=x_[:], scalar=ss[:, 1:2], in1=ab[:, 0, :], op0=ALU.mult, op1=ALU.mult),
                             R=[x_, ss, ab], W=[junk])
                        S.op("dve", lambda e: e.tensor_tensor(out=h_[:], in0=junk[:], in1=ab[:, 1, :], op=ALU.add), R=[junk, ab], W=[h_])
                        pb = S.psb()
                        for k in range(8):
                            S.op("pe", lambda e: e.transpose(out=pb[:, k * 128:(k + 1) * 128], in_=h_[:, k * 128:(k + 1) * 128], identity=ident_bf[:]),
                                 R=[h_, ident_bf], W=[pb])
                        S.op("act", lambda e: e.copy(out=hxT[:, :, i * 128:(i + 1) * 128], in_=pb[:].rearrange("p (k t) -> p k t", k=8)), R=[pb], W=[hxT])
                    for k in range(8):
                        S.store(hdst[:, k * Tn:(k + 1) * Tn], hxT[:, k, 0:Tn], R=[hxT])
                    if kind == "c" and l == 1:
                        continue
                    udst = U if kind == "x" else UC
                    for j in range(12):
                        rw = raw[j % 2]
                        u_ = ub[j % 2]
                        for t0 in range(0, Tn, 512):
                            tw = min(512, Tn - t0)
                            p = S.ps()
                            for k in range(8):
                                S.op("pe", lambda e: e.matmul(p[:, 0:tw], lhsT=why[:, k, j * 128:(j + 1) * 128], rhs=hxT[:, k, t0:t0 + tw], start=(k == 0), stop=(k == 7)),
                                     R=[why, hxT], W=[p])
                            S.op("act", lambda e: e.activation(out=rw[:, 1 + t0:1 + t0 + tw], in_=p[:, 0:tw], func=AF.Identity, bias=ppl[:, j:j + 1]), R=[p, ppl], W=[rw])
                        if Tn < T:
                            S.op("pool", lambda e: e.memset(rw[:, Tn + 1:Tn + 2], 0.0), W=[rw])
                        S.op("act", lambda e: e.activation(out=tmp[:, 0:Tn], in_=rw[:, 1:Tn + 1], func=AF.Identity, scale=ppl[:, 24 + j:25 + j], bias=ppl[:, 48 + j:49 + j]),
                             R=[rw, ppl], W=[tmp])
                        S.op("dve", lambda e: e.scalar_tensor_tensor(out=tmp[:, 0:Tn], in0=rw[:, 0:Tn], scalar=ppl[:, 12 + j:13 + j], in1=tmp[:, 0:Tn], op0=ALU.mult, op1=ALU.add),
                             R=[rw, ppl, tmp], W=[tmp])
                        S.op("dve", lambda e: e.scalar_tensor_tensor(out=u_[:, 0:Tn], in0=rw[:, 2:Tn + 2], scalar=ppl[:, 36 + j:37 + j], in1=tmp[:, 0:Tn], op0=ALU.mult, op1=ALU.add),
                             R=[rw, ppl, tmp], W=[u_])
                        sg = stg[j % 2]
                        for i0 in range(0, NT, 8):
                            ni = min(8, NT - i0)
                            pb = S.psb()
                            for ii in range(ni):
                                S.op("pe", lambda e: e.transpose(out=pb[:, ii * 128:(ii + 1) * 128], in_=u_[:, (i0 + ii) * 128:(i0 + ii + 1) * 128], identity=ident_bf[:]),
                                     R=[u_, ident_bf], W=[pb])
                            S.op("act" if (i0 // 8) % 2 == 0 else "dve",
                                 (lambda e: e.copy(out=sg[:, i0:i0 + ni, :], in_=pb[:, 0:ni * 128].rearrange("p (i c) -> p i c", i=ni))) if (i0 // 8) % 2 == 0 else
                                 (lambda e: e.tensor_copy(out=sg[:, i0:i0 + ni, :], in_=pb[:, 0:ni * 128].rearrange("p (i c) -> p i c", i=ni))),
                                 R=[pb], W=[sg])
                        S.store(udst[j, b].rearrange("(i p) c -> p i c", p=128), sg[:, 0:NT, :], R=[sg])
                S.barrier()
            if stop_after == f"PA{l}":
                return nc

            with ExitStack() as es:
                hyena_fft(l)
                for kind, Tn, usrc, ksrc, ydst in (("c", TC, UC, KBC, YTC),):
                    if kind == "c" and l == 1:
                        continue
                    NBk = Tn // 128
                    GW = (2 * NBk - 1) * 128 + 1
                    with ExitStack() as es2:
                        vt = tile(es2, "h_v", [128, NB, NBk, 128], BF16)
                        x1 = tile(es2, "h_x1", [128, NB, NBk, 128], BF16)
                        x1r = tile(es2, "h_x1r", [128, NB, NBk, 128], BF16)
                        x2 = tile(es2, "h_x2", [128, NB, NBk, 128], BF16)
                        zt = tile(es2, "h_z", [128, NB, NBk, 128], BF16)
                        yh = tile(es2, "h_y", [128, NB, NBk, 128], BF16)
                        yT = tile(es2, "h_yT", [128, NB, Tn], BF16)
                        g = [tile(es2, f"h_g{i}", [128, GW], BF16) for i in range(3)]
                        gi = 0
                        NCOL = NB * NBk
                        for cc in range(4):
                            for b in range(NB):
                                S.load(vt[:, b], usrc[8 + cc, b].rearrange("(i p) c -> p i c", p=128), W=[vt])
                                S.load(x1[:, b], usrc[0 + cc, b].rearrange("(i p) c -> p i c", p=128), W=[x1])
                                S.load(x2[:, b], usrc[4 + cc, b].rearrange("(i p) c -> p i c", p=128), W=[x2])
                            x1f = x1[:].rearrange("p b i c -> p (b i c)")
                            x1rf = x1r[:].rearrange("p b i c -> p (b i c)")
                            for c0 in range(0, NCOL * 128, 512):
                                p = S.ps()
                                S.op("pe", lambda e: e.matmul(p[:], lhsT=identJ_bf[:], rhs=x1f[:, c0:c0 + 512], start=True, stop=True), R=[identJ_bf, x1], W=[p])
                                S.op("act", lambda e: e.copy(out=x1rf[:, c0:c0 + 512], in_=p[:]), R=[p], W=[x1r])
                            for o in range(2):
                                src_t = vt if o == 0 else zt
                                gate_t = x1r if o == 0 else x2
                                dst_t = zt if o == 0 else yh
                                CPB = 512 // NCOL if NCOL <= 512 else 1
                                for c8 in range(0, 128, CPB):
                                    p = S.ps()
                                    for ci in range(CPB):
                                        c = c8 + ci
                                        gt = g[gi % 3]
                                        gi += 1
                                        row = ksrc[o, cc * 128 + c]
                                        S.load(gt[:], bass.AP(row.tensor, row.offset, [[1, 128], [1, GW]]), W=[gt])
                                        lags = [0] + [d for d in range(-(NBk - 1), NBk) if d != 0]
                                        for n_, d in enumerate(lags):
                                            i_lo, i_hi = max(0, d), min(NBk - 1, NBk - 1 + d)
                                            ni = i_hi - i_lo + 1
                                            j_lo = i_lo - d
                                            g0 = (NBk - 1 - d) * 128 if o == 0 else (d + NBk - 1) * 128 + 1
                                            oap = p[:, ci * NCOL:(ci + 1) * NCOL].rearrange("p (b i) -> p b i", b=NB)[:, :, i_lo:i_lo + ni]
                                            S.op("pe", lambda e: e.matmul(oap, lhsT=gt[:, g0:g0 + 128], rhs=src_t[:, :, j_lo:j_lo + ni, c],
                                                                          start=(n_ == 0), stop=(n_ == len(lags) - 1)),
                                                 R=[gt, src_t], W=[p])
                                    S.op("dve", lambda e: e.tensor_tensor(out=dst_t[:].rearrange("p b i c -> p (b i) c")[:, :, c8:c8 + CPB],
                                                                          in0=gate_t[:].rearrange("p b i c -> p (b i) c")[:, :, c8:c8 + CPB],
                                                                          in1=p[:, 0:CPB * NCOL].rearrange("p (c n) -> p n c", c=CPB), op=ALU.mult),
                                         R=[gate_t, p], W=[dst_t])
                            for b in range(NB):
                                for i0 in range(0, NBk, 8):
                                    ni = min(8, NBk - i0)
                                    pb = S.psb()
                                    for ii in range(ni):
                                        S.op("pe", lambda e: e.transpose(out=pb[:, ii * 128:(ii + 1) * 128], in_=yh[:, b, i0 + ii, :], identity=ident_bf[:]),
                                             R=[yh, ident_bf], W=[pb])
                                    S.op("act", lambda e: e.copy(out=yT[:, b, i0 * 128:(i0 + ni) * 128], in_=pb[:, 0:ni * 128]), R=[pb], W=[yT])
                                S.store(ydst[b, cc * 128:(cc + 1) * 128, :], yT[:, b, :], R=[yT])
                        S.barrier()
            if stop_after == f"PH{l}":
                return nc
            if stop_after is not None and stop_after.startswith('PH'):
                pass
            try:
                PC(nc, S, l, streams, locals())
            except StopBuild:
                S.barrier()
                return nc
            if stop_after == f"PC{l}":
                return nc
        S.barrier()
    return nc


def _host_inputs(inp, core, consts, pps):
    b0 = core * NB
    m = {}
    m["x"] = np.ascontiguousarray(inp["x"][b0:b0 + NB])
    m["ctx"] = np.ascontiguousarray(inp["ctx"][b0:b0 + NB])
    call = np.concatenate([inp["c"][b0:b0 + NB], inp["c_ctx"][None]], 0)
    m["cT"] = np.ascontiguousarray(call.T.reshape(8, 128, 5).transpose(1, 0, 2).reshape(128, 40))
    for k in ["w_mod", "b_mod", "norm_mix", "norm_ffn", "w_in", "b_in", "hy_f_w1", "hy_f_w2", "hy_f_w3", "gla_wa_f", "gla_ba_f",
              "gla_wa_b", "gla_ba_b", "gla_norm_w", "hy_skip", "w_br_hy", "w_br_gla", "w_br_pool", "w_out", "w_up", "w_down", "norm_final"]:
        m[k] = inp[k]
    m["pool_w"] = inp["pool_w"].reshape(2, 512, 128)
    m["pp"] = pps
    for k, v in consts.items():
        m["c_" + k] = v
    return m


def kernel(**inputs):
    inp = {k: np.ascontiguousarray(np.asarray(v, dtype=np.float32)) for k, v in inputs.items()}
    consts = _consts()
    pps = np.stack([_pp_pack(inp, 0), _pp_pack(inp, 1)])
    nc = build()
    in_maps = [_host_inputs(inp, c, consts, pps) for c in range(NCORES)]
    res = run_bass_kernel_spmd(nc, in_maps, core_ids=list(range(NCORES)))
    outs = [np.asarray(r["out"], dtype=np.float32) for r in res.results]
    return np.concatenate(outs, axis=0)
```

```python
import math
import numpy as np
import ml_dtypes
from contextlib import ExitStack
import concourse.bass as bass
import concourse.mybir as mybir
from concourse.bass_utils import run_bass_kernel_spmd

F32 = mybir.dt.float32
BF16 = mybir.dt.bfloat16
AF = mybir.ActivationFunctionType
ALU = mybir.AluOpType
AX = mybir.AxisListType

D = 1024
T = 2048
TC = 256
NB = 4
NCORES = 8
EPS = 1e-6
N_IN = 6688
GLA_OFF = 1536
POOL_OFF = 3104
GATE_OFF = 3616
EPOCH = 30000
NDS = 40
PI = math.pi


class Buf:
    __slots__ = ("w", "r")

    def __init__(self):
        self.w = None
        self.r = {}


class Tl:
    def __init__(self, t):
        self.t = t
        self.b = Buf()

    def __getitem__(self, k):
        return self.t[k]


class Sched:
    def __init__(self, nc, es):
        self.nc = nc
        self.es = es
        self.eng = {"pe": nc.tensor, "act": nc.scalar, "dve": nc.vector, "pool": nc.gpsimd, "sp": nc.sync}
        self.cnt = {k: 0 for k in self.eng}
        self.sems = {k: [] for k in self.eng}
        self.seen = {k: {} for k in self.eng}
        self.dsem = [es.enter_context(nc.semaphore(f"dq{i}")) for i in range(NDS)]
        self.dval = [0] * NDS
        self.dnext = 0
        self.nwait = 0
        self.ps_ring = []
        self.ps_i = 0
        self.psb_ring = []
        self.psb_i = 0
        self.ldq = 0
        self.dead = False

    def _sem(self, k, ep):
        while len(self.sems[k]) <= ep:
            self.sems[k].append(self.es.enter_context(self.nc.semaphore(f"s_{k}_{len(self.sems[k])}")))
        return self.sems[k][ep]

    def _wait(self, k, tok):
        if tok is None:
            return
        if tok[0] == "e":
            _, src, n = tok
            if n == 0:
                return
            if src == "pe" and k == "pe":
                return
            if self.seen[k].get(src, 0) >= n:
                return
            self.seen[k][src] = n
            self.eng[k].wait_ge(self._sem(src, (n - 1) // EPOCH), (n - 1) % EPOCH + 1)
        else:
            _, i, v = tok
            key = ("d", i)
            if self.seen[k].get(key, 0) >= v:
                return
            self.seen[k][key] = v
            self.eng[k].wait_ge(self.dsem[i], v)
        self.nwait += 1

    def _deps(self, k, R, W):
        for b in R:
            self._wait(k, b.w)
        for b in W:
            self._wait(k, b.w)
            for t in b.r.values():
                self._wait(k, t)

    def op(self, k, fn, R=(), W=()):
        if self.dead:
            return
        R = [x.b if isinstance(x, Tl) else x for x in R]
        W = [x.b if isinstance(x, Tl) else x for x in W]
        self._deps(k, R, W)
        ins = fn(self.eng[k])
        self.cnt[k] += 1
        n = self.cnt[k]
        ins.then_inc(self._sem(k, (n - 1) // EPOCH), 1)
        t = ("e", k, n)
        for b in R:
            b.r[k] = t
        for b in W:
            b.w = t
            b.r = {}

    def dma(self, q, out, in_, R=(), W=()):
        if self.dead:
            return
        R = [x.b if isinstance(x, Tl) else x for x in R]
        W = [x.b if isinstance(x, Tl) else x for x in W]
        i = self.dnext
        self.dnext = (i + 1) % NDS
        if self.dval[i] > 0:
            self._wait(q, ("d", i, self.dval[i]))
        self._deps(q, R, W)
        self.dval[i] += 16
        self.eng[q].dma_start(out=out, in_=in_).then_inc(self.dsem[i], 16)
        t = ("d", i, self.dval[i])
        for b in R:
            b.r[("d", i)] = t
        for b in W:
            b.w = t
            b.r = {}

    def load(self, out, in_, R=(), W=()):
        self.dma("sp", out, in_, R, W)

    def store(self, out, in_, R=(), W=()):
        self.dma("pool", out, in_, R, W)

    def barrier(self):
        if self.dead:
            return
        toks = [("e", k, self.cnt[k]) for k in self.eng] + [("d", i, self.dval[i]) for i in range(NDS) if self.dval[i] > 0]
        for k in self.eng:
            for t in toks:
                if t[0] == "e" and t[1] == k:
                    continue
                self._wait(k, t)

    def ps(self):
        p = self.ps_ring[self.ps_i]
        self.ps_i = (self.ps_i + 1) % len(self.ps_ring)
        return p

    def psb(self):
        p = self.psb_ring[self.ps_i]
        self.ps_i = (self.ps_i + 1) % len(self.ps_ring)
        return p


def _pool_tables():
    mats = []
    seen = {}
    nbrs = {}
    rcnt = {}
    for mode, L in (("g", T), ("c", TC)):
        t = np.arange(L)
        rc_all = np.zeros((4, L), np.float32)
        for g, w in enumerate((2, 4, 8, 16)):
            if mode == "g":
                r = t // 64
                c = t % 64
                rl = np.clip(r - w // 2, 0, 32)
                rh = np.clip(r - w // 2 + w, 0, 32)
                cl = np.clip(c - w // 2, 0, 64)
                ch = np.clip(c - w // 2 + w, 0, 64)
                P = ((r[:, None] >= rl[None, :]) & (r[:, None] < rh[None, :]) & (c[:, None] >= cl[None, :]) & (c[:, None] < ch[None, :])).astype(np.float32)
                cnt = ((rh - rl) * (ch - cl)).astype(np.float32)
            else:
                lo = np.clip(t - w // 2, 0, L)
                hi = np.clip(t - w // 2 + w, 0, L)
                P = ((t[:, None] >= lo[None, :]) & (t[:, None] < hi[None, :])).astype(np.float32)
                cnt = (hi - lo).astype(np.float32)
            P = P - np.diag(cnt)
            rc_all[g] = 1.0 / cnt
            for n in range(L // 128):
                lst = []
                for n2 in range(L // 128):
                    blk = np.ascontiguousarray(P[n2 * 128:(n2 + 1) * 128, n * 128:(n + 1) * 128])
                    if np.any(blk):
                        key = blk.tobytes()
                        if key not in seen:
                            seen[key] = len(mats)
                            mats.append(blk)
                        lst.append((n2, seen[key]))
                nbrs[(mode, g, n)] = lst
        rcnt[mode] = rc_all
    pm = np.stack(mats, 1)
    return pm, nbrs, rcnt


_POOL = _pool_tables()
NU = _POOL[0].shape[1]


def _z_table(L, NP):
    pos = np.abs(np.arange(NP) - (L - 1)).astype(np.float32)
    bands = np.arange(1, 9, dtype=np.float32)[None, :]
    ang = (np.float32(2.0 * math.pi / L) * pos[:, None]) * bands
    z = np.concatenate([pos[:, None] / np.float32(L), np.cos(ang), np.sin(ang)], -1).astype(np.float32)
    zT = np.zeros((32, NP), np.float32)
    zT[:17] = z.T
    posn = np.zeros((2, NP), np.float32)
    posn[0] = np.abs(np.arange(NP) - (L - 1)) / np.float32(L)
    posn[1] = np.abs(np.arange(NP) - L) / np.float32(L)
    return zT, np.ascontiguousarray(np.broadcast_to(posn[None], (128, 2, NP)))


def _fft_consts():
    N = 4096
    bf = ml_dtypes.bfloat16
    n1 = np.arange(32)
    n2 = np.arange(64)
    k1 = np.arange(64)
    th = 2 * np.pi * (((64 * n1[:, None, None] + n2[None, :, None]) * k1[None, None, :]) % N) / N
    F1x = np.zeros((64, 64, 128))
    F1x[0:32, :, 0:64] = np.cos(th)
    F1x[32:64, :, 0:64] = np.sin(th)
    F1x[0:32, :, 64:128] = -np.sin(th)
    F1x[32:64, :, 64:128] = np.cos(th)
    n1f = np.arange(64)
    thk = 2 * np.pi * (((64 * n1f[:, None, None] + n2[None, :, None]) * k1[None, None, :]) % N) / N
    F1k = np.zeros((64, 64, 128))
    F1k[:, :, 0:64] = np.cos(thk)
    F1k[:, :, 64:128] = -np.sin(thk)
    k2 = np.arange(64)
    ph = 2 * np.pi * ((n2[:, None] * k2[None, :]) % 64) / 64
    c2, s2 = np.cos(ph), np.sin(ph)
    F2 = np.zeros((64, 8, 128))
    for i, (a_, b_) in enumerate(((c2, -s2), (s2, c2), (-s2, c2), (c2, s2), (c2, c2), (s2, s2), (s2, -s2), (-c2, c2))):
        F2[:, i, 0:64] = a_
        F2[:, i, 64:128] = b_
    ph3 = ph.T
    F3 = np.zeros((128, 128))
    F3[0:64, 0:64] = np.cos(ph3)
    F3[64:, 0:64] = -np.sin(ph3)
    F3[0:64, 64:] = np.sin(ph3)
    F3[64:, 64:] = np.cos(ph3)
    F3 /= N
    th4 = 2 * np.pi * (((64 * n1[None, None, :] + n2[None, :, None]) * k1[:, None, None]) % N) / N
    F4 = np.zeros((64, 64, 2, 64))
    F4[:, :, 0, 0:32] = np.cos(th4)
    F4[:, :, 0, 32:] = np.sin(th4)
    F4[:, :, 1, 0:32] = -np.sin(th4)
    F4[:, :, 1, 32:] = np.cos(th4)
    n = 64 * n1f[:, None] + n2[None, :]
    tau = np.minimum(n, N - n).astype(np.float64)
    deltas = np.abs(np.linspace(math.log(1e-2) / 1.5, math.log(1e-2) / 0.3, 512, dtype=np.float32)).astype(np.float64)
    wtab = np.exp(-(tau[:, :, None] / 2048.0) * deltas[None, None, :])
    wtab[32, 0, :] = 0.0
    pos = np.minimum(np.arange(N), N - np.arange(N)).astype(np.float32)
    bands = np.arange(1, 9, dtype=np.float32)[None, :]
    ang = (np.float32(2.0 * math.pi / 2048) * pos[:, None]) * bands
    z = np.concatenate([pos[:, None] / np.float32(2048), np.cos(ang), np.sin(ang)], -1).astype(np.float32)
    zT = np.zeros((32, N), np.float32)
    zT[:17] = z.T
    return {"F1x": F1x.astype(bf), "F1k": F1k.astype(bf), "F2": F2.astype(bf), "F3": F3.astype(bf), "F4": F4.astype(bf),
            "wtab": wtab.astype(np.float32), "zT_f": zT}


def _consts():
    c = {}
    eye = np.eye(128, dtype=np.float32)
    c["ident_bf"] = eye.astype(ml_dtypes.bfloat16)
    c["identJ_bf"] = eye[::-1].copy().astype(ml_dtypes.bfloat16)
    c["ident_f"] = eye
    s = np.arange(128)[:, None]
    t = np.arange(128)[None, :]
    same = (s // 64) == (t // 64)
    tri = np.zeros((128, 4, 128), np.float32)
    tri[:, 0] = np.where(same & (s <= t), -1.0 / 16, 0)
    tri[:, 1] = np.where(same & (s > t), -1.0 / 16, 0)
    tri[:, 2] = np.where(same & (s >= t), -1.0 / 16, 0)
    tri[:, 3] = np.where(same & (s < t), -1.0 / 16, 0)
    c["tri"] = tri.astype(ml_dtypes.bfloat16)
    msk = np.zeros((128, 2, 128), np.float32)
    msk[:, 0] = np.where(same & (t >= s), 1.0, 0)
    msk[:, 1] = np.where(same & (t <= s), 1.0, 0)
    c["msk"] = msk
    c["pmat"] = _POOL[0].astype(ml_dtypes.bfloat16)
    c["rcnt_g"] = np.ascontiguousarray(np.broadcast_to(_POOL[2]["g"][None], (128, 4, T)))
    c["rcnt_c"] = np.ascontiguousarray(np.broadcast_to(_POOL[2]["c"][None], (128, 4, TC)))
    c["zT_x"], c["posn_x"] = _z_table(T, 2 * T)
    c["zT_c"], c["posn_c"] = _z_table(TC, 2 * TC)
    c.update(_fft_consts())
    c["ones_f"] = np.ones((1, 128), np.float32)
    c["ones_bf"] = np.ones((1, 128), ml_dtypes.bfloat16)
    return c


NPP = 112


def _pp_pack(inp, l):
    pp = np.zeros((128, NPP), np.float32)
    b_in = inp["b_in"][l]
    pp[:, 0:12] = b_in[0:1536].reshape(12, 128).T
    for k in range(3):
        pp[:, 12 + k * 12:24 + k * 12] = inp["hy_short_w"][l][k].reshape(12, 128).T
    pp[:, 48:60] = inp["hy_short_b"][l].reshape(12, 128).T
    pp[:, 60:64] = b_in[1536:2048].reshape(4, 128).T
    pp[:, 64:88] = b_in[GATE_OFF:GATE_OFF + 3072].reshape(24, 128).T
    pp[:, 88:92] = inp["pool_scale"][l].reshape(4, 128).T
    pp[0:16, 92] = b_in[3072:3088]
    pp[0:16, 93] = b_in[3088:3104]
    pp[0:64, 94] = inp["hy_f_freq"][l]
    pp[0:64, 95] = inp["hy_f_b1"][l]
    pp[0:64, 96] = inp["hy_f_b2"][l]
    deltas = np.abs(np.linspace(math.log(1e-2) / 1.5, math.log(1e-2) / 0.3, 512, dtype=np.float32))
    pp[:, 98:102] = (-deltas).reshape(4, 128).T
    pp[:, 102:106] = inp["hy_skip"][l][0].reshape(4, 128).T
    pp[:, 106:110] = inp["hy_skip"][l][1].reshape(4, 128).T
    return pp


def PC(nc, S, l, streams, E):
    tile = E["tile"]; pp = E["pp"]; Wb = E["Wb"]; C = E["C"]; MODV = E["MODV"]
    ident_bf = E["ident_bf"]; ones_bf = E["ones_bf"]; ones_f = E["ones_f"]
    XS = E["XS"]; CXS = E["CXS"]; YT = E["YT"]; YTC = E["YTC"]; out = E["out"]
    b_in = E["b_in"]; wa_f = E["wa_f"]; wa_b = E["wa_b"]; ba_f = E["ba_f"]; ba_b = E["ba_b"]
    gla_nw = E["gla_nw"]; norm_final = E["norm_final"]; bcast_row = E["bcast_row"]
    ppl = pp[l]
    chk = E["chk"]
    last = (l == 1)
    nbrs = _POOL[1]
    win = Wb[("in", l)]

    def mvap(s, j):
        return MODV[l, s][:, j * D:(j + 1) * D]

    with ExitStack() as es:
        hxT = tile(es, "c_hxT", [128, 8, T], BF16)
        yglaT = tile(es, "c_yglaT", [128, 4, T], BF16)
        Sst = tile(es, "c_Sst", [128, 2, 4, 128], F32)
        SstB = [Buf(), Buf()]
        tri = tile(es, "c_tri", [128, 4, 128], BF16)
        msk = tile(es, "c_msk", [128, 2, 128], F32)
        gnw = tile(es, "c_gnw", [128, 128], F32)
        brow = tile(es, "c_brow", [1, N_IN], BF16)
        waug = tile(es, "c_waug", [32, 2, 256], BF16)
        es_row = ExitStack()
        rowf = tile(es_row, "c_rowf", [1, N_IN], F32)
        waf = tile(es_row, "c_waf", [32, 2, 256], F32)
        S.load(tri[:], C["tri"], W=[tri])
        S.load(msk[:], C["msk"], W=[msk])
        bcast_row(es, gnw, gla_nw[l:l + 1, :], 128, rowf)
        S.load(rowf[:], b_in[l:l + 1, :], W=[rowf])
        S.op("dve", lambda e: e.tensor_copy(out=brow[:], in_=rowf[:]), R=[rowf], W=[brow])
        S.op("dve", lambda e: e.memset(waf[:], 0.0), W=[waf])
        S.load(waf[0:16, 0, :], wa_f[l], W=[waf])
        S.load(waf[0:16, 1, :], wa_b[l], W=[waf])
        S.load(waf[16:17, 0, :], ba_f[l:l + 1, :], W=[waf])
        S.load(waf[16:17, 1, :], ba_b[l:l + 1, :], W=[waf])
        S.op("dve", lambda e: e.tensor_copy(out=waug[:], in_=waf[:]), R=[waf], W=[waug])
        S.barrier()
        es_row.close()

        def tm_proj(p, ncols, i, wt, c0, bcol0):
            for k in range(8):
                S.op("pe", lambda e: e.matmul(p[:, 0:ncols], lhsT=hxT[:, k, i * 128:(i + 1) * 128], rhs=wt[:, k, c0:c0 + ncols], start=(k == 0), stop=False),
                     R=[hxT, wt], W=[p])
            S.op("pe", lambda e: e.matmul(p[:, 0:ncols], lhsT=ones_bf[0:1, :], rhs=brow[0:1, bcol0:bcol0 + ncols], start=False, stop=True),
                 R=[ones_bf, brow], W=[p])

        for (kind, b, Tn, xsrc, hdst, ms) in streams:
            NT = Tn // 128
            ctx_l1 = (kind == "c" and l == 1)
            for k in range(8):
                S.load(hxT[:, k, 0:Tn], hdst[:, k * Tn:(k + 1) * Tn], W=[hxT])
            with ExitStack() as e2:
                wg = tile(e2, "g_w", [128, 8, 1568], BF16)
                for k in range(8):
                    S.load(wg[:, k, :], win[k * 128:(k + 1) * 128, GLA_OFF:POOL_OFF], W=[wg])
                vtm = tile(e2, "g_v", [128, 16, 512], BF16)
                ktm = tile(e2, "g_k", [128, 16, 256], BF16)
                qkT = tile(e2, "g_qkT", [128, 4, T], BF16)
                gTa = tile(e2, "g_gT", [32, 2, T], BF16)
                oacc = tile(e2, "g_o", [128, 16, 512], F32)
                oab = [Buf() for _ in range(16)]
                S.op("pool", lambda e: e.memset(gTa[:], 1.0), W=[gTa])
                if not ctx_l1:
                    S.op("pool", lambda e: e.memset(oacc[:, 0:NT, :], 0.0), W=oab[0:NT])
                if kind == "c":
                    S.op("pool", lambda e: e.memset(Sst[:], 0.0), W=[SstB[0], SstB[1]])
                for i in range(NT):
                    p = S.ps()
                    tm_proj(p, 512, i, wg, 512, 2048)
                    S.op("act", lambda e: e.copy(out=vtm[:, i, :], in_=p[:]), R=[p], W=[vtm])
                    p = S.ps()
                    tm_proj(p, 256, i, wg, 256, 1792)
                    S.op("dve", lambda e: e.tensor_copy(out=ktm[:, i, :], in_=p[:, 0:256]), R=[p], W=[ktm])
                for j in range(4):
                    for t0 in range(0, Tn, 512):
                        tw = min(512, Tn - t0)
                        p = S.ps()
                        for k in range(8):
                            S.op("pe", lambda e: e.matmul(p[:, 0:tw], lhsT=wg[:, k, j * 128:(j + 1) * 128], rhs=hxT[:, k, t0:t0 + tw], start=(k == 0), stop=(k == 7)),
                                 R=[wg, hxT], W=[p])
                        S.op("act", lambda e: e.activation(out=qkT[:, j, t0:t0 + tw], in_=p[:, 0:tw], func=AF.Identity, bias=ppl[:, 60 + j:61 + j]), R=[p, ppl], W=[qkT])
                for dr in range(2):
                    for t0 in range(0, Tn, 512):
                        tw = min(512, Tn - t0)
                        p = S.ps()
                        for k in range(8):
                            S.op("pe", lambda e: e.matmul(p[0:16, 0:tw], lhsT=wg[:, k, 1536 + 16 * dr:1552 + 16 * dr], rhs=hxT[:, k, t0:t0 + tw], start=(k == 0), stop=(k == 7)),
                                 R=[wg, hxT], W=[p])
                        S.op("act", lambda e: e.activation(out=gTa[0:16, dr, t0:t0 + tw], in_=p[0:16, 0:tw], func=AF.Identity, bias=ppl[0:16, 92 + dr:93 + dr]), R=[p, ppl], W=[gTa])

                def chain(dr, P0, P1, P2):
                    spt = tile(e2, f"g_sp{dr}", [128, 256], BF16)
                    erb = tile(e2, f"g_erb{dr}", [128, 256], F32)
                    kend = tile(e2, f"g_kend{dr}", [128, 256], BF16)
                    eb = tile(e2, f"g_eb{dr}", [128, 2, 128], F32)
                    ebi = tile(e2, f"g_ebi{dr}", [128, 2, 128], F32)
                    qd = tile(e2, f"g_qd{dr}", [128, 2, 128], BF16)
                    ki = tile(e2, f"g_ki{dr}", [128, 2, 128], BF16)
                    qA = tile(e2, f"g_qA{dr}", [128, 2, 128], BF16)
                    qB = tile(e2, f"g_qB{dr}", [128, 2, 128], BF16)
                    scm = [tile(e2, f"g_scm{dr}{i}", [128, 128], BF16) for i in range(2)]
                    Smid = tile(e2, f"g_Smid{dr}", [128, 4, 128], F32)
                    Sb0 = tile(e2, f"g_Sb0{dr}", [128, 4, 128], BF16)
                    Sb1 = tile(e2, f"g_Sb1{dr}", [128, 4, 128], BF16)
                    Sd = SstB[dr]
                    S.op("pool", lambda e: e.memset(qA[:], 0.0), W=[qA])
                    S.op("pool", lambda e: e.memset(qB[:], 0.0), W=[qB])
                    yield
                    order = list(range(NT)) if dr == 0 else list(range(NT - 1, -1, -1))
                    for i in order:
                        tsl = slice(i * 128, (i + 1) * 128)
                        p = P0
                        S.op("pe", lambda e: e.matmul(p[:, 0:256], lhsT=gTa[0:32, dr, tsl], rhs=waug[0:32, dr, :], start=True, stop=True), R=[gTa, waug], W=[p])
                        yield
                        S.op("act", lambda e: e.activation(out=erb[:], in_=p[:, 0:256], func=AF.Exp, scale=-1.0), R=[p], W=[erb])
                        S.op("act", lambda e: e.activation(out=spt[:], in_=erb[:], func=AF.Ln, bias=1.0), R=[erb], W=[spt])
                        yield
                        S.op("pe", lambda e: e.matmul(P1[:, 0:256], lhsT=tri[:, 2 * dr + 1, :], rhs=spt[:], start=True, stop=True), R=[tri, spt], W=[P1])
                        for ch in range(2):
                            S.op("pe", lambda e: e.matmul(P2[:, ch * 128:(ch + 1) * 128], lhsT=spt[:, ch * 128:(ch + 1) * 128], rhs=tri[:, 2 * dr, :], start=True, stop=True),
                                 R=[tri, spt], W=[P2])
                        yield
                        S.op("act", lambda e: e.activation(out=erb[:], in_=P1[:, 0:256], func=AF.Exp), R=[P1], W=[erb])
                        S.op("act", lambda e: e.activation(out=eb[:].rearrange("p a b -> p (a b)"), in_=P2[:, 0:256], func=AF.Exp), R=[P2], W=[eb])
                        S.op("act", lambda e: e.activation(out=ebi[:].rearrange("p a b -> p (a b)"), in_=P2[:, 0:256], func=AF.Exp, scale=-1.0), R=[P2], W=[ebi])
                        yield
                        S.op("dve", lambda e: e.tensor_tensor(out=kend[:], in0=ktm[:, i, :], in1=erb[:], op=ALU.mult), R=[ktm, erb], W=[kend])
                        S.op("dve", lambda e: e.scalar_tensor_tensor(out=qd[:], in0=eb[:], scalar=0.125, in1=qkT[:, 0:2, tsl], op0=ALU.mult, op1=ALU.mult), R=[eb, qkT], W=[qd])
                        S.op("dve", lambda e: e.tensor_tensor(out=ki[:], in0=ebi[:], in1=qkT[:, 2:4, tsl], op=ALU.mult), R=[ebi, qkT], W=[ki])
                        yield
                        S.op("pool", lambda e: e.tensor_copy(out=qA[:, :, 0:64], in_=qd[:, :, 0:64]), R=[qd], W=[qA])
                        S.op("pool", lambda e: e.tensor_copy(out=qB[:, :, 64:128], in_=qd[:, :, 64:128]), R=[qd], W=[qB])
                        pk0, pk1 = P1, P2
                        for h in range(4):
                            cq = h // 2
                            S.op("pe", lambda e: e.matmul(pk0[:, h * 128:(h + 1) * 128], lhsT=kend[0:64, cq * 128:(cq + 1) * 128], rhs=vtm[0:64, i, h * 128:(h + 1) * 128], start=True, stop=True),
                                 R=[kend, vtm], W=[pk0])
                            S.op("pe", lambda e: e.matmul(pk1[:, h * 128:(h + 1) * 128], lhsT=kend[64:128, cq * 128:(cq + 1) * 128], rhs=vtm[64:128, i, h * 128:(h + 1) * 128], start=True, stop=True),
                                 R=[kend, vtm], W=[pk1])
                        if dr == 0:
                            pF, pS_, cF, cS, qF, qS = pk0, pk1, 63, 127, qA, qB
                        else:
                            pF, pS_, cF, cS, qF, qS = pk1, pk0, 64, 0, qB, qA
                        S.op("act", lambda e: e.copy(out=Sb0[:], in_=Sst[:, dr]), R=[Sd], W=[Sb0])
                        yield
                        for cq in range(2):
                            S.op("dve", lambda e: e.scalar_tensor_tensor(out=Smid[:, 2 * cq:2 * cq + 2, :], in0=Sst[:, dr, 2 * cq:2 * cq + 2, :], scalar=eb[:, cq, cF:cF + 1],
                                                                          in1=pF[:, cq * 256:(cq + 1) * 256].rearrange("p (a b) -> p a b", a=2), op0=ALU.mult, op1=ALU.add),
                                 R=[Sd, eb, pF], W=[Smid])
                        yield
                        S.op("act", lambda e: e.copy(out=Sb1[:], in_=Smid[:]), R=[Smid], W=[Sb1])
                        for cq in range(2):
                            S.op("dve", lambda e: e.scalar_tensor_tensor(out=Sst[:, dr, 2 * cq:2 * cq + 2, :], in0=Smid[:, 2 * cq:2 * cq + 2, :], scalar=eb[:, cq, cS:cS + 1],
                                                                          in1=pS_[:, cq * 256:(cq + 1) * 256].rearrange("p (a b) -> p a b", a=2), op0=ALU.mult, op1=ALU.add),
                                 R=[Smid, eb, pS_], W=[Sd])
                        yield
                        if ctx_l1:
                            continue
                        po = P0
                        for h in range(4):
                            cq = h // 2
                            r0 = 64 * (h % 2)
                            p_s = P1 if h % 2 == 0 else P2
                            S.op("pe", lambda e: e.matmul(p_s[:, 0:128], lhsT=ki[r0:r0 + 64, cq, :], rhs=qd[r0:r0 + 64, cq, :], start=True, stop=True), R=[ki, qd], W=[p_s])
                            yield
                            sc = scm[h % 2]
                            S.op("dve", lambda e: e.tensor_tensor(out=sc[:], in0=p_s[:, 0:128], in1=msk[:, dr, :], op=ALU.mult), R=[p_s, msk], W=[sc])
                            yield
                            S.op("pe", lambda e: e.matmul(po[:, h * 128:(h + 1) * 128], lhsT=sc[:], rhs=vtm[:, i, h * 128:(h + 1) * 128], start=True, stop=False), R=[sc, vtm], W=[po])
                            S.op("pe", lambda e: e.matmul(po[:, h * 128:(h + 1) * 128], lhsT=qF[r0:r0 + 64, cq, :], rhs=Sb0[r0:r0 + 64, h, :], start=False, stop=False), R=[qF, Sb0], W=[po])
                            S.op("pe", lambda e: e.matmul(po[:, h * 128:(h + 1) * 128], lhsT=qS[r0:r0 + 64, cq, :], rhs=Sb1[r0:r0 + 64, h, :], start=False, stop=True), R=[qS, Sb1], W=[po])
                        yield
                        S.op("dve", lambda e: e.tensor_tensor(out=oacc[:, i, :], in0=oacc[:, i, :], in1=po[:], op=ALU.add), R=[po, oab[i]], W=[oab[i]])
                        yield

                gens = [chain(0, S.ps_ring[0], S.ps_ring[1], S.ps_ring[2]), chain(1, S.ps_ring[3], S.ps_ring[4], S.ps_ring[5])]
                alive = list(gens)
                while alive:
                    for g_ in list(alive):
                        try:
                            next(g_)
                        except StopIteration:
                            alive.remove(g_)
                if not ctx_l1:
                    sr = tile(e2, "g_sr", [128, 512], F32)
                    tmpo = tile(e2, "g_tmpo", [128, 512], F32)
                    ybfs = [tile(e2, f"g_ybf{i}", [128, 512], BF16) for i in range(2)]
                    ss4 = tile(e2, "g_ss4", [128, 8], F32)

                    def nA(i):
                        ybf = ybfs[i % 2]
                        p = S.ps()
                        tm_proj(p, 512, i, wg, 1024, 2560)
                        S.op("act", lambda e: e.activation(out=sr[:], in_=p[:], func=AF.Silu), R=[p], W=[sr])
                        for h in range(4):
                            S.op("act", lambda e: e.activation(out=tmpo[:, h * 128:(h + 1) * 128], in_=oacc[:, i, h * 128:(h + 1) * 128], func=AF.Square, accum_out=ss4[:, h:h + 1]),
                                 R=[oab[i]], W=[tmpo, ss4])
                        S.op("act", lambda e: e.activation(out=ss4[:, 4:8], in_=ss4[:, 0:4], func=AF.Sqrt, scale=1.0 / 128, bias=EPS), R=[ss4], W=[ss4])
                        S.op("dve", lambda e: e.reciprocal(out=ss4[:, 4:8], in_=ss4[:, 4:8]), R=[ss4], W=[ss4])
                        for h in range(4):
                            S.op("dve", lambda e: e.scalar_tensor_tensor(out=tmpo[:, h * 128:(h + 1) * 128], in0=oacc[:, i, h * 128:(h + 1) * 128], scalar=ss4[:, 4 + h:5 + h],
                                                                          in1=gnw[:], op0=ALU.mult, op1=ALU.mult), R=[oab[i], ss4, gnw], W=[tmpo])
                        S.op("dve", lambda e: e.tensor_tensor(out=ybf[:], in0=tmpo[:], in1=sr[:], op=ALU.mult), R=[tmpo, sr], W=[ybf])

                    def nB(i):
                        ybf = ybfs[i % 2]
                        pb = S.psb()
                        for k in range(4):
                            S.op("pe", lambda e: e.transpose(out=pb[:, k * 128:(k + 1) * 128], in_=ybf[:, k * 128:(k + 1) * 128], identity=ident_bf[:]), R=[ybf, ident_bf], W=[pb])
                        S.op("act", lambda e: e.copy(out=yglaT[:, :, i * 128:(i + 1) * 128], in_=pb[:, 0:512].rearrange("p (k t) -> p k t", k=4)), R=[pb], W=[yglaT])

                    nA(0)
                    for i in range(NT):
                        if i + 1 < NT:
                            nA(i + 1)
                        nB(i)
                S.barrier()
            chk(f"PC{l}_gla_{kind}{b}")
            if ctx_l1:
                continue
            e3 = ExitStack()
            ypoolT = tile(e3, "c_ypoolT", [128, 4, T], BF16)
            with ExitStack() as e2:
                wp = tile(e2, "p_w", [128, 8, 512], BF16)
                for k in range(8):
                    S.load(wp[:, k, :], win[k * 128:(k + 1) * 128, POOL_OFF:GATE_OFF], W=[wp])
                up = tile(e2, "p_up", [128, 16, 512], BF16)
                rc = tile(e2, "p_rc", [128, 4, Tn], F32)
                pm = tile(e2, "p_pm", [128, NU, 128], BF16)
                pw = tile(e2, "p_pw", [128, 4, 128], BF16)
                mT = tile(e2, "p_mT", [128, 4, T], BF16)
                S.load(rc[:], C["rcnt_g"] if kind == "x" else C["rcnt_c"], W=[rc])
                S.load(pm[:], C["pmat"], W=[pm])
                for g in range(4):
                    S.load(pw[:, g, :], Wb[("pw", l)][g * 128:(g + 1) * 128, :], W=[pw])
                for i in range(NT):
                    p = S.ps()
                    tm_proj(p, 512, i, wp, 0, POOL_OFF)
                    S.op("act", lambda e: e.copy(out=up[:, i, :], in_=p[:]), R=[p], W=[up])
                mode = "g" if kind == "x" else "c"
                for g in range(4):
                    for n0 in range(0, NT, 4):
                        nn = min(4, NT - n0)
                        p = S.ps()
                        for n in range(n0, n0 + nn):
                            lst = nbrs[(mode, g, n)]
                            for q_, (n2, idx) in enumerate(lst):
                                S.op("pe", lambda e: e.matmul(p[:, (n - n0) * 128:(n - n0 + 1) * 128], lhsT=up[:, n2, g * 128:(g + 1) * 128], rhs=pm[:, idx, :],
                                                              start=(q_ == 0), stop=(q_ == len(lst) - 1)), R=[up, pm], W=[p])
                        S.op("dve", lambda e: e.tensor_tensor(out=mT[:, g, n0 * 128:(n0 + nn) * 128], in0=p[:, 0:nn * 128], in1=rc[:, g, n0 * 128:(n0 + nn) * 128], op=ALU.mult),
                             R=[p, rc], W=[mT])
                for g in range(4):
                    for t0 in range(0, Tn, 512):
                        tw = min(512, Tn - t0)
                        p = S.ps()
                        S.op("pe", lambda e: e.matmul(p[:, 0:tw], lhsT=pw[:, g, :], rhs=mT[:, g, t0:t0 + tw], start=True, stop=True), R=[pw, mT], W=[p])
                        S.op("act", lambda e: e.activation(out=ypoolT[:, g, t0:t0 + tw], in_=p[:, 0:tw], func=AF.Copy, scale=ppl[:, 88 + g:89 + g]), R=[p, ppl], W=[ypoolT])
                S.barrier()
            chk(f"PC{l}_pool_{kind}{b}")
            mergedT = tile(e3, "c_mergedT", [128, 8, T], BF16)
            with ExitStack() as e2:
                yhyT = tile(e2, "m_yhy", [128, 4, T], BF16)
                ysrc = YT[b] if kind == "x" else YTC[b]
                for cc in range(4):
                    S.load(yhyT[:, cc, 0:Tn], ysrc[cc * 128:(cc + 1) * 128, :], W=[yhyT])
                wbr = tile(e2, "m_wbr", [128, 3, 4, D], BF16)
                for br in range(3):
                    for k in range(4):
                        S.load(wbr[:, br, k, :], Wb[("br", br, l)][k * 128:(k + 1) * 128, :], W=[wbr])
                wgd = [tile(e2, f"m_wgd{i}", [128, 8, 3, 128], BF16) for i in range(2)]
                gts = [tile(e2, f"m_gt{i}", [128, 512], F32) for i in range(2)]
                acc = tile(e2, "m_acc", [128, 512], F32)
                tmpm = tile(e2, "m_tmp", [128, 512], F32)
                srcs = [yhyT, yglaT, ypoolT]
                for dc in range(8):
                    wg_ = wgd[dc % 2]
                    for br in range(3):
                        c0 = GATE_OFF + br * 1024 + dc * 128
                        S.load(wg_[:, :, br, :], win[:, c0:c0 + 128].rearrange("(k p) c -> p k c", p=128), W=[wg_])
                    for t0 in range(0, Tn, 512):
                        tw = min(512, Tn - t0)
                        for br in range(3):
                            pj = S.ps()
                            for k in range(4):
                                S.op("pe", lambda e: e.matmul(pj[:, 0:tw], lhsT=wbr[:, br, k, dc * 128:(dc + 1) * 128], rhs=srcs[br][:, k, t0:t0 + tw], start=(k == 0), stop=(k == 3)),
                                     R=[wbr, srcs[br]], W=[pj])
                            pg = S.ps()
                            for k in range(8):
                                S.op("pe", lambda e: e.matmul(pg[:, 0:tw], lhsT=wg_[:, k, br, :], rhs=hxT[:, k, t0:t0 + tw], start=(k == 0), stop=(k == 7)), R=[wg_, hxT], W=[pg])
                            gt_ = gts[br % 2]
                            S.op("act", lambda e: e.activation(out=gt_[:, 0:tw], in_=pg[:, 0:tw], func=AF.Sigmoid, bias=ppl[:, 64 + br * 8 + dc:65 + br * 8 + dc]), R=[pg, ppl], W=[gt_])
                            if br == 0:
                                S.op("dve", lambda e: e.tensor_tensor(out=acc[:, 0:tw], in0=gt_[:, 0:tw], in1=pj[:, 0:tw], op=ALU.mult), R=[gt_, pj], W=[acc])
                            else:
                                S.op("dve", lambda e: e.tensor_tensor(out=tmpm[:, 0:tw], in0=gt_[:, 0:tw], in1=pj[:, 0:tw], op=ALU.mult), R=[gt_, pj], W=[tmpm])
                                if br == 1:
                                    S.op("pool", lambda e: e.tensor_tensor(out=acc[:, 0:tw], in0=acc[:, 0:tw], in1=tmpm[:, 0:tw], op=ALU.add), R=[acc, tmpm], W=[acc])
                                else:
                                    S.op("pool", lambda e: e.tensor_tensor(out=mergedT[:, dc, t0:t0 + tw], in0=acc[:, 0:tw], in1=tmpm[:, 0:tw], op=ALU.add), R=[acc, tmpm], W=[mergedT])
                S.barrier()
            chk(f"PC{l}_merge_{kind}{b}")
            with ExitStack() as e2:
                wo = tile(e2, "o_w", [128, 8, D], BF16)
                for k in range(8):
                    S.load(wo[:, k, :], Wb[("out", l)][k * 128:(k + 1) * 128, :], W=[wo])
                g1 = tile(e2, "o_g1", [128, D], F32)
                S.load(g1[:], mvap(ms, 2), W=[g1])
                xt = [tile(e2, f"o_xt{i}", [128, D], F32) for i in range(2)]
                t2 = [tile(e2, f"o_t2{i}", [128, D], F32) for i in range(2)]
                xdst = XS[b] if kind == "x" else CXS[b]
                for i in range(NT):
                    x_ = xt[i % 2]
                    t_ = t2[i % 2]
                    S.load(x_[:], xsrc[i * 128:(i + 1) * 128, :], W=[x_])
                    for hf in range(2):
                        p = S.ps()
                        for k in range(8):
                            S.op("pe", lambda e: e.matmul(p[:], lhsT=mergedT[:, k, i * 128:(i + 1) * 128], rhs=wo[:, k, hf * 512:(hf + 1) * 512], start=(k == 0), stop=(k == 7)),
                                 R=[mergedT, wo], W=[p])
                        S.op("dve", lambda e: e.tensor_tensor(out=t_[:, hf * 512:(hf + 1) * 512], in0=p[:], in1=g1[:, hf * 512:(hf + 1) * 512], op=ALU.mult), R=[p, g1], W=[t_])
                    S.op("pool", lambda e: e.tensor_tensor(out=t_[:], in0=t_[:], in1=x_[:], op=ALU.add), R=[t_, x_], W=[t_])
                    S.store(xdst[i * 128:(i + 1) * 128, :], t_[:], R=[t_])
                S.barrier()
            e3.close()
            chk(f"PC{l}_wout_{kind}{b}")
        S.barrier()

    chk(f"PC{l}_mix")
    with ExitStack() as es:
        wu = tile(es, "f_wu", [128, 8, 4096], BF16)
        wd = tile(es, "f_wd", [128, 32, D], BF16)
        for k in range(8):
            S.load(wu[:, k, :], Wb[("up", l)][k * 128:(k + 1) * 128, :], W=[wu])
        for k in range(32):
            S.load(wd[:, k, :], Wb[("down", l)][k * 128:(k + 1) * 128, :], W=[wd])
        ab = tile(es, "f_ab", [128, 3, D], F32)
        gfin = tile(es, "f_gfin", [128, D], F32)
        junk = tile(es, "f_junk", [128, D], F32)
        if last:
            bcast_row(es, gfin, norm_final.rearrange("(o d) -> o d", o=1), D, junk)
        xt = [tile(es, f"f_xt{i}", [128, D], F32) for i in range(4)]
        junk2 = tile(es, "f_junk2", [128, D], BF16)
        h2 = [tile(es, f"f_h2{i}", [128, D], BF16) for i in range(2)]
        h2T = tile(es, "f_h2T", [128, 8, 256], BF16)
        aT = tile(es, "f_aT", [128, 32, 256], BF16)
        rl = [tile(es, f"f_rl{i}", [128, 512], F32) for i in range(2)]
        t2s = [tile(es, f"f_t2{i}", [128, D], F32) for i in range(2)]
        ss = tile(es, "f_ss", [128, 2], F32)
        ssf = [tile(es, f"f_ssf{i}", [128, 2], F32) for i in range(2)]
        for (kind, b, Tn, xsrc, hdst, ms) in streams:
            if kind == "c" and l == 1:
                continue
            src = XS[b] if kind == "x" else CXS[b]
            S.load(ab[:, 0, :], mvap(ms, 4), W=[ab])
            S.load(ab[:, 1, :], mvap(ms, 3), W=[ab])
            S.load(ab[:, 2, :], mvap(ms, 5), W=[ab])
            blocks = list(range(0, Tn, 256))

            def ln1(bi):
                t0 = blocks[bi]
                for ii in range(2):
                    i = t0 // 128 + ii
                    x_ = xt[(bi % 2) * 2 + ii]
                    h_ = h2[ii]
                    S.load(x_[:], src[i * 128:(i + 1) * 128, :], W=[x_])
                    S.op("act", lambda e: e.activation(out=junk[:], in_=x_[:], func=AF.Square, accum_out=ss[:, 0:1]), R=[x_], W=[junk, ss])
                    S.op("act", lambda e: e.activation(out=ss[:, 1:2], in_=ss[:, 0:1], func=AF.Sqrt, scale=1.0 / D, bias=EPS), R=[ss], W=[ss])
                    S.op("dve", lambda e: e.reciprocal(out=ss[:, 1:2], in_=ss[:, 1:2]), R=[ss], W=[ss])
                    S.op("dve", lambda e: e.scalar_tensor_tensor(out=junk[:], in0=x_[:], scalar=ss[:, 1:2], in1=ab[:, 0, :], op0=ALU.mult, op1=ALU.mult), R=[x_, ss, ab], W=[junk])
                    S.op("dve", lambda e: e.tensor_tensor(out=h_[:], in0=junk[:], in1=ab[:, 1, :], op=ALU.add), R=[junk, ab], W=[h_])

            def ln2(bi):
                for ii in range(2):
                    h_ = h2[ii]
                    pb = S.psb()
                    for k in range(8):
                        S.op("pe", lambda e: e.transpose(out=pb[:, k * 128:(k + 1) * 128], in_=h_[:, k * 128:(k + 1) * 128], identity=ident_bf[:]), R=[h_, ident_bf], W=[pb])
                    S.op("act", lambda e: e.copy(out=h2T[:, :, ii * 128:(ii + 1) * 128], in_=pb[:].rearrange("p (k t) -> p k t", k=8)), R=[pb], W=[h2T])

            def up(bi):
                for f0 in range(0, 32, 2):
                    p = S.ps()
                    for ff in range(2):
                        f = f0 + ff
                        for k in range(8):
                            S.op("pe", lambda e: e.matmul(p[:, ff * 256:(ff + 1) * 256], lhsT=wu[:, k, f * 128:(f + 1) * 128], rhs=h2T[:, k, :], start=(k == 0), stop=(k == 7)),
                                 R=[wu, h2T], W=[p])
                    r_ = rl[(f0 // 2) % 2]
                    S.op("act", lambda e: e.activation(out=r_[:], in_=p[:], func=AF.Relu), R=[p], W=[r_])
                    S.op("pool" if (f0 // 2) % 2 else "dve", lambda e: e.tensor_tensor(out=aT[:, f0:f0 + 2, :].rearrange("p a b -> p (a b)"), in0=r_[:], in1=r_[:], op=ALU.mult), R=[r_], W=[aT])

            def down(bi):
                t0 = blocks[bi]
                for ii in range(2):
                    i = t0 // 128 + ii
                    x_ = xt[(bi % 2) * 2 + ii]
                    t2 = t2s[ii]
                    sf = ssf[ii]
                    for hf in range(2):
                        p = S.ps()
                        for f in range(32):
                            S.op("pe", lambda e: e.matmul(p[:], lhsT=aT[:, f, ii * 128:(ii + 1) * 128], rhs=wd[:, f, hf * 512:(hf + 1) * 512], start=(f == 0), stop=(f == 31)), R=[aT, wd], W=[p])
                        S.op("dve", lambda e: e.tensor_tensor(out=t2[:, hf * 512:(hf + 1) * 512], in0=p[:], in1=ab[:, 2, hf * 512:(hf + 1) * 512], op=ALU.mult), R=[p, ab], W=[t2])
                    S.op("pool", lambda e: e.tensor_tensor(out=t2[:], in0=t2[:], in1=x_[:], op=ALU.add), R=[t2, x_], W=[t2])
                    if last:
                        S.op("act", lambda e: e.activation(out=junk2[:], in_=t2[:], func=AF.Square, accum_out=sf[:, 0:1]), R=[t2], W=[junk2, sf])
                        S.op("act", lambda e: e.activation(out=sf[:, 1:2], in_=sf[:, 0:1], func=AF.Sqrt, scale=1.0 / D, bias=EPS), R=[sf], W=[sf])
                        S.op("dve", lambda e: e.reciprocal(out=sf[:, 1:2], in_=sf[:, 1:2]), R=[sf], W=[sf])
                        S.op("dve", lambda e: e.scalar_tensor_tensor(out=t2[:], in0=t2[:], scalar=sf[:, 1:2], in1=gfin[:], op0=ALU.mult, op1=ALU.mult), R=[t2, sf, gfin], W=[t2])
                        S.store(out[b, i * 128:(i + 1) * 128, :], t2[:], R=[t2])
                    else:
                        S.store(src[i * 128:(i + 1) * 128, :], t2[:], R=[t2])

            ln1(0)
            ln2(0)
            for bi in range(len(blocks)):
                up(bi)
                if bi + 1 < len(blocks):
                    ln1(bi + 1)
                down(bi)
                if bi + 1 < len(blocks):
                    ln2(bi + 1)
        S.barrier()


class StopBuild(Exception):
    pass


def build(stop_after=None, dbg=()):
    nc = bass.Bass("TRN2", target_bir_lowering=False)
    es0 = ExitStack()

    def din(name, shape, dt=F32):
        return nc.dram_tensor(name, list(shape), dt, kind="ExternalInput").ap()

    def dscr(name, shape, dt):
        kind = "ExternalOutput" if name in dbg else "Internal"
        return nc.dram_tensor(name, list(shape), dt, kind=kind).ap()

    x_in = din("x", [NB, T, D])
    ctx_in = din("ctx", [NB, TC, D])
    cT_in = din("cT", [128, 40])
    w_mod = din("w_mod", [2, D, 6144])
    b_mod = din("b_mod", [2, 6144])
    norm_mix = din("norm_mix", [2, D])
    norm_ffn = din("norm_ffn", [2, D])
    w_in = din("w_in", [2, D, N_IN])
    b_in = din("b_in", [2, N_IN])
    hy_w1 = din("hy_f_w1", [2, 17, 64])
    hy_w2 = din("hy_f_w2", [2, 64, 64])
    hy_w3 = din("hy_f_w3", [2, 64, 2048])
    wa_f = din("gla_wa_f", [2, 16, 256])
    ba_f = din("gla_ba_f", [2, 256])
    wa_b = din("gla_wa_b", [2, 16, 256])
    ba_b = din("gla_ba_b", [2, 256])
    gla_nw = din("gla_norm_w", [2, 128])
    pool_w = din("pool_w", [2, 512, 128])
    w_brs = [din("w_br_hy", [2, 512, D]), din("w_br_gla", [2, 512, D]), din("w_br_pool", [2, 512, D])]
    w_out = din("w_out", [2, D, D])
    w_up = din("w_up", [2, D, 4096])
    w_down = din("w_down", [2, 4096, D])
    norm_final = din("norm_final", [D])
    pp_in = din("pp", [2, 128, NPP])
    hy_skip = din("hy_skip", [2, 2, 512])
    C = {}
    for name, shape, dt in (("ident_bf", [128, 128], BF16), ("identJ_bf", [128, 128], BF16), ("ident_f", [128, 128], F32),
                            ("tri", [128, 4, 128], BF16), ("msk", [128, 2, 128], F32), ("pmat", [128, NU, 128], BF16),
                            ("rcnt_g", [128, 4, T], F32), ("rcnt_c", [128, 4, TC], F32),
                            ("zT_x", [32, 2 * T], F32), ("posn_x", [128, 2, 2 * T], F32),
                            ("zT_c", [32, 2 * TC], F32), ("posn_c", [128, 2, 2 * TC], F32),
                            ("F1x", [64, 64, 128], BF16), ("F1k", [64, 64, 128], BF16), ("F2", [64, 8, 128], BF16),
                            ("F3", [128, 128], BF16), ("F4", [64, 64, 2, 64], BF16), ("wtab", [64, 64, 512], F32), ("zT_f", [32, 4096], F32),
                            ("ones_f", [1, 128], F32), ("ones_bf", [1, 128], BF16)):
        C[name] = din("c_" + name, shape, dt)
    out = nc.dram_tensor("out", [NB, T, D], F32, kind="ExternalOutput").ap()

    Wb = {}
    for l in range(2):
        Wb[("mod", l)] = dscr(f"wb_mod{l}", [D, 6144], BF16)
        Wb[("in", l)] = dscr(f"wb_in{l}", [D, N_IN], BF16)
        for i in range(3):
            Wb[("br", i, l)] = dscr(f"wb_br{i}_{l}", [512, D], BF16)
        Wb[("out", l)] = dscr(f"wb_out{l}", [D, D], BF16)
        Wb[("up", l)] = dscr(f"wb_up{l}", [D, 4096], BF16)
        Wb[("down", l)] = dscr(f"wb_down{l}", [4096, D], BF16)
        Wb[("pw", l)] = dscr(f"wb_pw{l}", [512, 128], BF16)
    MODV = dscr("modv", [2, 5, 128, 6144], F32)
    XS = dscr("xs", [NB, T, D], F32)
    CXS = dscr("cxs", [NB, TC, D], F32)
    HXT = dscr("hxt", [NB, 128, 8 * T], BF16)
    HXTC = dscr("hxtc", [NB, 128, 8 * TC], BF16)
    U = dscr("u", [12, NB, T, 128], BF16)
    UC = dscr("uc", [12, NB, TC, 128], BF16)
    YT = dscr("yt", [NB, 512, T], BF16)
    YTC = dscr("ytc", [NB, 512, TC], BF16)
    KB = dscr("kb", [2, 512, 2 * T], BF16)
    KBC = dscr("kbc", [2, 512, 2 * TC], BF16)
    KF = dscr("kf", [2, 4, 2, 128, 8192], BF16)

    S = Sched(nc, es0)
    with es0:
        for i in range(8):
            S.ps_ring.append(Tl(es0.enter_context(nc.psum_tensor(f"ps{i}", [128, 512], F32))))
        for p_ in S.ps_ring:
            v_ = Tl(p_.t[:].bitcast(BF16))
            v_.b = p_.b
            S.psb_ring.append(v_)

        tcount = [0]

        def tile(es, name, shape, dt):
            tcount[0] += 1
            return Tl(es.enter_context(nc.sbuf_tensor(f"t{tcount[0]}_{name}", list(shape), dt)))

        ident_bf = tile(es0, "ident_bf", [128, 128], BF16)
        identJ_bf = tile(es0, "identJ_bf", [128, 128], BF16)
        ident_f = tile(es0, "ident_f", [128, 128], F32)
        ones_f = tile(es0, "ones_f", [1, 128], F32)
        ones_bf = tile(es0, "ones_bf", [1, 128], BF16)
        pp = [tile(es0, f"pp{l}", [128, NPP], F32) for l in range(2)]
        for tl_, src in ((ident_bf, C["ident_bf"]), (identJ_bf, C["identJ_bf"]), (ident_f, C["ident_f"]),
                         (ones_f, C["ones_f"]), (ones_bf, C["ones_bf"]), (pp[0], pp_in[0]), (pp[1], pp_in[1])):
            S.load(tl_[:], src, W=[tl_])

        engs3 = ["dve", "act", "pool"]

        def cast(k, out_ap, in_ap, R, W):
            if k == "act":
                S.op("act", lambda e: e.copy(out=out_ap, in_=in_ap), R=R, W=W)
            else:
                S.op(k, lambda e: e.tensor_copy(out=out_ap, in_=in_ap), R=R, W=W)

        def bcast_row(es, dst, src_row_ap, n, scratch_row):
            S.load(scratch_row[0:1, 0:n], src_row_ap, W=[scratch_row])
            for c0 in range(0, n, 512):
                cw = min(512, n - c0)
                p = S.ps()
                S.op("pe", lambda e: e.matmul(p[:, 0:cw], lhsT=ones_f[0:1, :], rhs=scratch_row[0:1, c0:c0 + cw], start=True, stop=True),
                     R=[ones_f, scratch_row], W=[p])
                S.op("act", lambda e: e.copy(out=dst[:, c0:c0 + cw], in_=p[:, 0:cw]), R=[p], W=[dst])

        with ExitStack() as es:
            wf = [tile(es, f"pw_f{i}", [128, N_IN], F32) for i in range(2)]
            wb = [tile(es, f"pw_b{i}", [128, N_IN], BF16) for i in range(2)]
            it = 0
            for l in range(2):
                jobs = [(w_mod[l], Wb[("mod", l)], D, 6144), (w_in[l], Wb[("in", l)], D, N_IN),
                        (w_brs[0][l], Wb[("br", 0, l)], 512, D), (w_brs[1][l], Wb[("br", 1, l)], 512, D),
                        (w_brs[2][l], Wb[("br", 2, l)], 512, D), (w_out[l], Wb[("out", l)], D, D),
                        (w_up[l], Wb[("up", l)], D, 4096), (w_down[l], Wb[("down", l)], 4096, D),
                        (pool_w[l], Wb[("pw", l)], 512, 128)]
                for src, dst, rows, cols in jobs:
                    for r0 in range(0, rows, 128):
                        a, b_ = wf[it % 2], wb[it % 2]
                        S.load(a[:, 0:cols], src[r0:r0 + 128, :], W=[a])
                        cast(engs3[it % 3], b_[:, 0:cols], a[:, 0:cols], [a], [b_])
                        S.store(dst[r0:r0 + 128, :], b_[:, 0:cols], R=[b_])
                        it += 1
            S.barrier()
        if stop_after == "PW":
            return nc

        with ExitStack() as es:
            cT = tile(es, "cT", [128, 40], F32)
            sT = tile(es, "sT", [128, 40], BF16)
            srep = tile(es, "srep", [128, 40, 128], BF16)
            wm = tile(es, "wm", [128, 8, 6144], BF16)
            brow_f = tile(es, "brow_f", [1, 6144], F32)
            brow = tile(es, "brow", [1, 6144], BF16)
            gmix = tile(es, "gmix", [128, D], F32)
            gffn = tile(es, "gffn", [128, D], F32)
            mv = [tile(es, f"mv{i}", [128, 6144], F32) for i in range(2)]
            S.load(cT[:], cT_in, W=[cT])
            S.op("act", lambda e: e.activation(out=sT[:], in_=cT[:], func=AF.Silu), R=[cT], W=[sT])
            S.op("dve", lambda e: e.tensor_copy(out=srep[:], in_=sT[:].unsqueeze(2).to_broadcast([128, 40, 128])), R=[sT], W=[srep])
            for l in range(2):
                for k in range(8):
                    S.load(wm[:, k, :], Wb[("mod", l)][k * 128:(k + 1) * 128, :], W=[wm])
                S.load(brow_f[:], b_mod[l:l + 1, :], W=[brow_f])
                S.op("dve", lambda e: e.tensor_copy(out=brow[:], in_=brow_f[:]), R=[brow_f], W=[brow])
                bcast_row(es, gmix, norm_mix[l:l + 1, :], D, brow_f)
                bcast_row(es, gffn, norm_ffn[l:l + 1, :], D, brow_f)
                for s in range(5):
                    m = mv[s % 2]
                    for cb in range(12):
                        p = S.ps()
                        for k in range(8):
                            S.op("pe", lambda e: e.matmul(p[:], lhsT=srep[:, k * 5 + s, :], rhs=wm[:, k, cb * 512:(cb + 1) * 512], start=(k == 0), stop=False),
                                 R=[srep, wm], W=[p])
                        S.op("pe", lambda e: e.matmul(p[:], lhsT=ones_bf[0:1, :], rhs=brow[0:1, cb * 512:(cb + 1) * 512], start=False, stop=True),
                             R=[ones_bf, brow], W=[p])
                        sp_ = cb // 2
                        hs = (cb % 2) * 512
                        dst = m[:, cb * 512:(cb + 1) * 512]
                        if sp_ == 1:
                            S.op("dve", lambda e: e.scalar_tensor_tensor(out=dst, in0=p[:], scalar=1.0, in1=gmix[:, hs:hs + 512], op0=ALU.add, op1=ALU.mult),
                                 R=[p, gmix], W=[m])
                        elif sp_ == 4:
                            S.op("dve", lambda e: e.scalar_tensor_tensor(out=dst, in0=p[:], scalar=1.0, in1=gffn[:, hs:hs + 512], op0=ALU.add, op1=ALU.mult),
                                 R=[p, gffn], W=[m])
                        else:
                            S.op("act", lambda e: e.copy(out=dst, in_=p[:]), R=[p], W=[m])
                    S.store(MODV[l, s], m[:], R=[m])
            S.barrier()
        if stop_after == "P0":
            return nc
        MV_B1, MV_A1, MV_G1, MV_B2, MV_A2, MV_G2 = 0, 1, 2, 3, 4, 5

        def mv_ap(l, s, j):
            return MODV[l, s][:, j * D:(j + 1) * D]


        def evac(idx, out_ap, in_ap, R, W):
            if idx % 2 == 0:
                S.op("act", lambda e: e.copy(out=out_ap, in_=in_ap), R=R, W=W)
            else:
                S.op("dve", lambda e: e.tensor_copy(out=out_ap, in_=in_ap), R=R, W=W)

        def fft_stage1(src, srcR, F1t, A1v, PAb):
            for g_ in range(16):
                p = S.ps()
                for q in range(4):
                    n2 = 4 * g_ + q
                    S.op("pe", lambda e: e.matmul(p[:, q * 128:(q + 1) * 128], lhsT=F1t[:, n2, :], rhs=src[:, n2, :], start=True, stop=True), R=[F1t] + srcR, W=[p])
                S.op("act", lambda e: e.copy(out=A1v[:, 4 * g_:4 * g_ + 4, :], in_=p[:].rearrange("p (a b) -> p a b", a=4)), R=[p], W=[PAb[g_]])

        def fft_transA(blk, A1v, PAb, Q, Qb):
            c0 = blk * 8
            for hf in range(2):
                p = S.ps()
                for q in range(4):
                    c = c0 + 4 * hf + q
                    S.op("pe", lambda e: e.matmul(p[0:64, q * 128:(q + 1) * 128], lhsT=A1v[:, :, c], rhs=ident_bf[:], start=True, stop=True), R=PAb + [ident_bf], W=[p])
                evac(0, Q[:, c0 + 4 * hf:c0 + 4 * hf + 4, :], p[0:64, :].rearrange("p (a b) -> p a b", a=4), [p], [Qb[blk]])

        def fft_stage2(blk, Q, Qb, F2t, i0):
            c0 = blk * 8
            pa = S.ps()
            pb_ = S.ps()
            for pp_, ia in ((pa, i0), (pb_, i0 + 2)):
                S.op("pe", lambda e: e.matmul(pp_[:], lhsT=F2t[:, ia, :], rhs=Q[:, c0:c0 + 8, 0:64], start=True, stop=False), R=[F2t, Qb[blk]], W=[pp_])
                S.op("pe", lambda e: e.matmul(pp_[:], lhsT=F2t[:, ia + 1, :], rhs=Q[:, c0:c0 + 8, 64:128], start=False, stop=True), R=[F2t, Qb[blk]], W=[pp_])
            return pa, pb_

        def spec_gen(es2, l, a2, w3):
            F1k = tile(es2, "s_F1k", [64, 64, 128], BF16)
            F2t = tile(es2, "s_F2", [64, 8, 128], BF16)
            a2b = tile(es2, "s_a2b", [64, 4096], BF16)
            w3b = tile(es2, "s_w3b", [64, 2048], BF16)
            kc = tile(es2, "s_kc", [64, 64, 256], BF16)
            wt = [tile(es2, f"s_wt{i}", [64, 4, 256], F32) for i in range(2)]
            skr = tile(es2, "s_skr", [1, 1024], F32)
            A1 = tile(es2, "s_A1", [128, 64, 128], BF16)
            A1T = tile(es2, "s_A1T", [64, 128, 128], BF16)
            kab = [tile(es2, f"s_kab{i}", [128, 2, 512], BF16) for i in range(2)]
            PAb = [Buf() for _ in range(16)]
            Qb = [Buf() for _ in range(16)]
            kcb = [Buf() for _ in range(16)]
            S.load(F1k[:], C["F1k"], W=[F1k])
            S.load(F2t[:], C["F2"], W=[F2t])
            S.load(skr[:], hy_skip[l].rearrange("(x o) c -> x (o c)", x=1), W=[skr])
            S.op("act", lambda e: e.copy(out=a2b[:], in_=a2[:]), R=[a2], W=[a2b])
            S.op("pool", lambda e: e.tensor_copy(out=w3b[:], in_=w3[:]), R=[w3], W=[w3b])
            it = 0
            wi = 0
            for o in range(2):
                for ch in range(2):
                    colf = 0 * 1024 + o * 512 + ch * 256
                    colb = 1 * 1024 + o * 512 + ch * 256
                    for g_ in range(16):
                        w_ = wt[wi % 2]
                        wi += 1
                        S.load(w_[:], C["wtab"][:, 4 * g_:4 * g_ + 4, ch * 256:(ch + 1) * 256], W=[w_])
                        for q2 in range(2):
                            pf = S.ps()
                            for q in range(2):
                                n2 = 4 * g_ + 2 * q2 + q
                                S.op("pe", lambda e: e.matmul(pf[0:64, q * 256:(q + 1) * 256], lhsT=a2b[:, n2::64], rhs=w3b[:, colf:colf + 256], start=True, stop=True), R=[a2b, w3b], W=[pf])
                            pg = S.ps()
                            for q in range(2):
                                n2 = 4 * g_ + 2 * q2 + q
                                S.op("pe", lambda e: e.matmul(pg[0:64, q * 256:(q + 1) * 256], lhsT=a2b[:, n2::64], rhs=w3b[:, colb:colb + 256], start=True, stop=True), R=[a2b, w3b], W=[pg])
                            n2a = 4 * g_ + 2 * q2
                            S.op("dve", lambda e: e.tensor_tensor(out=kc[0:32, n2a:n2a + 2, :], in0=pf[0:32, :].rearrange("p (a b) -> p a b", a=2), in1=w_[0:32, 2 * q2:2 * q2 + 2, :], op=ALU.mult),
                                 R=[pf, w_], W=[kcb[g_]])
                            S.op("dve", lambda e: e.tensor_tensor(out=kc[32:64, n2a:n2a + 2, :], in0=pg[32:64, :].rearrange("p (a b) -> p a b", a=2), in1=w_[32:64, 2 * q2:2 * q2 + 2, :], op=ALU.mult),
                                 R=[pg, w_], W=[kcb[g_]])
                            if n2a == 0:
                                S.op("dve", lambda e: e.tensor_tensor(out=kc[0:1, 0, :], in0=kc[0:1, 0, :], in1=skr[0:1, o * 512 + ch * 256:o * 512 + (ch + 1) * 256], op=ALU.add),
                                     R=[kcb[0], skr], W=[kcb[0]])
                    for c2 in range(2):
                        cc = 2 * ch + c2
                        fft_stage1(kc[:, :, c2 * 128:(c2 + 1) * 128], kcb, F1k, A1, PAb)
                        for blk_ in range(17):
                            if blk_ < 16:
                                fft_transA(blk_, A1, PAb, A1T, Qb)
                            if blk_ >= 1:
                                blk = blk_ - 1
                                pa, pb_ = fft_stage2(blk, A1T, Qb, F2t, 4)
                                kb_ = kab[it % 2]
                                it += 1
                                S.op("act", lambda e: e.copy(out=kb_[:, 0, :], in_=pa[:]), R=[pa], W=[kb_])
                                S.op("dve", lambda e: e.tensor_copy(out=kb_[:, 1, :], in_=pb_[:]), R=[pb_], W=[kb_])
                                S.store(KF[o, cc][:, :, blk * 512:(blk + 1) * 512].rearrange("t p n -> p t n"), kb_[:], R=[kb_])

        def hyena_fft(l):
            with ExitStack() as es2:
                F1x = tile(es2, "x_F1x", [64, 64, 128], BF16)
                F2t = tile(es2, "x_F2", [64, 8, 128], BF16)
                F3t = tile(es2, "x_F3", [128, 128], BF16)
                F4t = tile(es2, "x_F4", [64, 64, 2, 64], BF16)
                for t_, nm in ((F1x, "F1x"), (F2t, "F2"), (F3t, "F3"), (F4t, "F4")):
                    S.load(t_[:], C[nm], W=[t_])
                VY = tile(es2, "x_VY", [64, 64, 128], BF16)
                X = tile(es2, "x_X", [64, 64, 128], BF16)
                Z = tile(es2, "x_Z", [64, 64, 128], BF16)
                A1 = tile(es2, "x_A1", [128, 64, 128], BF16)
                Yt = tile(es2, "x_Yt", [128, 128, 64], BF16)
                G1 = tile(es2, "x_G1", [128, 128, 64], BF16)
                Q = tile(es2, "x_Q", [64, 128, 128], BF16)
                yT = tile(es2, "x_yT", [128, 2, T], BF16)
                kab = [tile(es2, f"x_kab{i}", [128, 2, 512], BF16) for i in range(4)]
                t1 = [tile(es2, f"x_t1{i}", [128, 512], F32) for i in range(2)]
                t2 = [tile(es2, f"x_t2{i}", [128, 512], F32) for i in range(2)]
                PAb = [Buf() for _ in range(16)]
                Qb = [Buf() for _ in range(16)]
                Ytb = [Buf() for _ in range(16)]
                Gb = [Buf() for _ in range(16)]
                VYb = [Buf() for _ in range(16)]
                Zb = [Buf() for _ in range(16)]
                cnt = [0]

                def conv(src, srcb, o, cc, dst, dstb):
                    fft_stage1(src, srcb, F1x, A1, PAb)
                    st = {}

                    def sA(blk):
                        fft_transA(blk, A1, PAb, Q, Qb)

                    def s2(blk):
                        c0 = blk * 8
                        kb_ = kab[cnt[0] % 4]
                        ta, tb = t1[cnt[0] % 2], t2[cnt[0] % 2]
                        cnt[0] += 1
                        S.load(kb_[:], KF[o, cc][:, :, blk * 512:(blk + 1) * 512].rearrange("t p n -> p t n"), W=[kb_])
                        pa, pb_ = fft_stage2(blk, Q, Qb, F2t, 0)
                        S.op("dve", lambda e: e.tensor_tensor(out=ta[:], in0=pa[:], in1=kb_[:, 0, :], op=ALU.mult), R=[pa, kb_], W=[ta])
                        S.op("dve", lambda e: e.tensor_tensor(out=tb[:], in0=pb_[:], in1=kb_[:, 1, :], op=ALU.mult), R=[pb_, kb_], W=[tb])
                        S.op("pool", lambda e: e.tensor_tensor(out=Yt[:, c0:c0 + 8, :].rearrange("p a b -> p (a b)"), in0=ta[:], in1=tb[:], op=ALU.add), R=[ta, tb], W=[Ytb[blk]])

                    def s3(blk):
                        c0 = blk * 8
                        p = S.ps()
                        S.op("pe", lambda e: e.matmul(p[:], lhsT=F3t[:], rhs=Yt[:, c0:c0 + 8, :].rearrange("p a b -> p (a b)"), start=True, stop=True), R=[F3t, Ytb[blk]], W=[p])
                        S.op("act", lambda e: e.copy(out=G1[:, c0:c0 + 8, :].rearrange("p a b -> p (a b)"), in_=p[:]), R=[p], W=[Gb[blk]])

                    def sB(blk):
                        c0 = blk * 8
                        for hf in range(2):
                            p = S.ps()
                            for q in range(4):
                                c = c0 + 4 * hf + q
                                S.op("pe", lambda e: e.matmul(p[0:64, q * 128:(q + 1) * 128], lhsT=G1[:, c, :], rhs=ident_bf[:], start=True, stop=True), R=[Gb[blk], ident_bf], W=[p])
                            evac(hf, Q[:, c0 + 4 * hf:c0 + 4 * hf + 4, :], p[0:64, :].rearrange("p (a b) -> p a b", a=4), [p], [Qb[blk]])

                    for i_ in range(16 + 6):
                        if i_ < 16:
                            sA(i_)
                        if 0 <= i_ - 2 < 16:
                            s2(i_ - 2)
                        if 0 <= i_ - 4 < 16:
                            s3(i_ - 4)
                        if 0 <= i_ - 6 < 16:
                            sB(i_ - 6)
                    for g_ in range(16):
                        p = S.ps()
                        for q in range(4):
                            n2 = 4 * g_ + q
                            S.op("pe", lambda e: e.matmul(p[0:64, q * 128:(q + 1) * 128], lhsT=F4t[:, n2, 0, :], rhs=Q[:, :, n2], start=True, stop=False), R=[F4t] + Qb, W=[p])
                            S.op("pe", lambda e: e.matmul(p[0:64, q * 128:(q + 1) * 128], lhsT=F4t[:, n2, 1, :], rhs=Q[:, :, 64 + n2], start=False, stop=True), R=[F4t] + Qb, W=[p])
                        S.op("dve", lambda e: e.tensor_tensor(out=dst[:, 4 * g_:4 * g_ + 4, :], in0=X[:, 4 * g_:4 * g_ + 4, :], in1=p[0:64, :].rearrange("p (a b) -> p a b", a=4), op=ALU.mult),
                             R=[X, p], W=[dstb[g_]])

                for cc in range(4):
                    for pr in range(2):
                        for ri in range(2):
                            b = 2 * pr + ri
                            S.load(VY[32 * ri:32 * ri + 32], U[8 + cc, b].rearrange("(n1 n2) c -> n1 n2 c", n2=64), W=VYb)
                        for ri in range(2):
                            S.load(X[32 * ri:32 * ri + 32], U[0 + cc, 2 * pr + ri].rearrange("(n1 n2) c -> n1 n2 c", n2=64), W=[X])
                        conv(VY, VYb, 0, cc, Z, Zb)
                        for ri in range(2):
                            S.load(X[32 * ri:32 * ri + 32], U[4 + cc, 2 * pr + ri].rearrange("(n1 n2) c -> n1 n2 c", n2=64), W=[X])
                        conv(Z, Zb, 1, cc, VY, VYb)
                        for n2q in range(0, 64, 16):
                            pb = S.psb()
                            for q in range(16):
                                S.op("pe", lambda e: e.transpose(out=pb[:, q * 64:(q + 1) * 64], in_=VY[:, n2q + q, :], identity=ident_bf[0:64, 0:64]), R=VYb + [ident_bf], W=[pb])
                            S.op("act", lambda e: e.copy(out=yT[:].rearrange("c b (n1 n2) -> c b n1 n2", n2=64)[:, :, :, n2q:n2q + 16],
                                                         in_=pb[:].rearrange("c (n2 b n1) -> c b n1 n2", n2=16, b=2)), R=[pb], W=[yT])
                        for ri in range(2):
                            S.store(YT[2 * pr + ri, cc * 128:(cc + 1) * 128, :], yT[:, ri, :], R=[yT])
                S.barrier()

        def chk(name):
            if name == stop_after:
                S.barrier()
                S.dead = True

        for l in range(2):
            streams = []
            for b in range(NB):
                if l == 0:
                    streams.append(("c", b, TC, ctx_in[b], HXTC[b], 4))
                else:
                    streams.append(("c", b, TC, CXS[b], HXTC[b], 4))
                streams.append(("x", b, T, x_in[b] if l == 0 else XS[b], HXT[b], b))

            with ExitStack() as es:
                w1 = tile(es, "f_w1", [32, 64], F32)
                w2 = tile(es, "f_w2", [64, 64], F32)
                w3 = tile(es, "f_w3", [64, 2048], F32)
                fb = tile(es, "f_fb", [64, 2], F32)
                S.op("dve", lambda e: e.memset(w1[:], 0.0), W=[w1])
                S.load(w1[0:17, :], hy_w1[l], W=[w1])
                S.load(w2[:], hy_w2[l], W=[w2])
                S.load(w3[:], hy_w3[l], W=[w3])
                ppl = pp[l]
                S.op("dve", lambda e: e.tensor_tensor(out=fb[:, 0:1], in0=ppl[0:64, 94:95], in1=ppl[0:64, 95:96], op=ALU.mult), R=[ppl], W=[fb])
                S.op("dve", lambda e: e.tensor_tensor(out=fb[:, 1:2], in0=ppl[0:64, 94:95], in1=ppl[0:64, 96:97], op=ALU.mult), R=[ppl], W=[fb])
                for kind, L_, zsrc, psrc, kdst in (("x", T, C["zT_f"], C["posn_x"], KB), ("c", TC, C["zT_c"], C["posn_c"], KBC)):
                    if kind == "c" and l == 1:
                        continue
                    NP = 2 * L_
                    with ExitStack() as es2:
                        zT = tile(es2, "f_zT", [32, NP], F32)
                        a1 = tile(es2, "f_a1", [64, NP], F32)
                        a2 = tile(es2, "f_a2", [64, NP], F32)
                        tA = tile(es2, "f_tA", [64, 512], F32)
                        tB = tile(es2, "f_tB", [64, 512], F32)
                        S.load(zT[:], zsrc, W=[zT])
                        if kind == "c":
                            posn = tile(es2, "f_posn", [128, 2, NP], F32)
                            win = [tile(es2, f"f_win{i}", [128, 512], F32) for i in range(2)]
                            krow = [tile(es2, f"f_krow{i}", [128, NP], F32) for i in range(2)]
                            kbf = [tile(es2, f"f_kbf{i}", [128, NP], BF16) for i in range(2)]
                            S.load(posn[:], psrc, W=[posn])

                        def sin_layer(dst, lhsT_ap, lhs_tl, src, fbcol):
                            for c0 in range(0, NP, 512):
                                p = S.ps()
                                S.op("pe", lambda e: e.matmul(p[0:64, :], lhsT=lhsT_ap, rhs=src[:, c0:c0 + 512], start=True, stop=True),
                                     R=[lhs_tl, src], W=[p])
                                S.op("dve", lambda e: e.tensor_scalar(out=tA[:], in0=p[0:64, :], scalar1=ppl[0:64, 94:95], scalar2=fb[:, fbcol:fbcol + 1], op0=ALU.mult, op1=ALU.add),
                                     R=[p, ppl, fb], W=[tA])
                                S.op("dve", lambda e: e.tensor_scalar(out=tB[:], in0=tA[:], scalar1=PI, scalar2=2 * PI, op0=ALU.is_gt, op1=ALU.mult), R=[tA], W=[tB])
                                S.op("dve", lambda e: e.tensor_tensor(out=tA[:], in0=tA[:], in1=tB[:], op=ALU.subtract), R=[tA, tB], W=[tA])
                                S.op("dve", lambda e: e.tensor_scalar(out=tB[:], in0=tA[:], scalar1=-PI, scalar2=2 * PI, op0=ALU.is_lt, op1=ALU.mult), R=[tA], W=[tB])
                                S.op("dve", lambda e: e.tensor_tensor(out=tA[:], in0=tA[:], in1=tB[:], op=ALU.add), R=[tA, tB], W=[tA])
                                S.op("act", lambda e: e.activation(out=dst[:, c0:c0 + 512], in_=tA[:], func=AF.Sin), R=[tA], W=[dst])

                        sin_layer(a1, w1[:, :], w1, zT, 0)
                        sin_layer(a2, w2[:, :], w2, a1, 1)
                        if kind == "x":
                            spec_gen(es2, l, a2, w3)
                            S.barrier()
                            continue
                        it = 0
                        for o in range(2):
                            for cc in range(4):
                                kr, kb_ = krow[it % 2], kbf[it % 2]
                                it += 1
                                if o == 1:
                                    S.op("pool", lambda e: e.memset(kr[:, 0:1], 0.0), W=[kr])
                                BS = min(512, L_)
                                for blk in range(NP // BS):
                                    first_half = blk * BS < L_
                                    if o == 0:
                                        dr = 0 if first_half else 1
                                        s0, d0, n = blk * BS, blk * BS, BS
                                    else:
                                        dr = 1 if first_half else 0
                                        if blk == 0:
                                            s0, d0, n = 0, 1, BS - 1
                                        else:
                                            s0, d0, n = blk * BS - 1, blk * BS, BS
                                    col0 = dr * 1024 + o * 512 + cc * 128
                                    p = S.ps()
                                    S.op("pe", lambda e: e.matmul(p[:, 0:n], lhsT=w3[:, col0:col0 + 128], rhs=a2[:, s0:s0 + n], start=True, stop=True),
                                         R=[w3, a2], W=[p])
                                    wt = win[blk % 2]
                                    S.op("act", lambda e: e.activation(out=wt[:, 0:n], in_=posn[:, o, d0:d0 + n], func=AF.Exp, scale=ppl[:, 98 + cc:99 + cc]),
                                         R=[posn, ppl], W=[wt])
                                    S.op("dve", lambda e: e.tensor_tensor(out=kr[:, d0:d0 + n], in0=p[:, 0:n], in1=wt[:, 0:n], op=ALU.mult), R=[p, wt], W=[kr])
                                zc = (L_ - 1) if o == 0 else L_
                                S.op("dve", lambda e: e.tensor_scalar(out=kr[:, zc:zc + 1], in0=kr[:, zc:zc + 1], scalar1=ppl[:, 102 + 4 * o + cc:103 + 4 * o + cc], scalar2=None, op0=ALU.add),
                                     R=[kr, ppl], W=[kr])
                                S.op("pool", lambda e: e.tensor_copy(out=kb_[:], in_=kr[:]), R=[kr], W=[kb_])
                                S.store(kdst[o, cc * 128:(cc + 1) * 128, :], kb_[:], R=[kb_])
                        S.barrier()
                S.barrier()
            if stop_after == f"PF{l}":
                return nc

            with ExitStack() as es:
                why = tile(es, "a_why", [128, 8, 1536], BF16)
                for k in range(8):
                    S.load(why[:, k, :], Wb[("in", l)][k * 128:(k + 1) * 128, 0:1536], W=[why])
                ppl = pp[l]
                hxT = tile(es, "a_hxT", [128, 8, T], BF16)
                ab = tile(es, "a_ab", [128, 2, D], F32)
                xt = [tile(es, f"a_xt{i}", [128, D], F32) for i in range(2)]
                junk = tile(es, "a_junk", [128, D], F32)
                ss = tile(es, "a_ss", [128, 2], F32)
                hx = [tile(es, f"a_hx{i}", [128, D], BF16) for i in range(2)]
                raw = [tile(es, f"a_raw{i}", [128, T + 2], F32) for i in range(2)]
                tmp = tile(es, "a_tmp", [128, T], F32)
                ub = [tile(es, f"a_u{i}", [128, T], BF16) for i in range(2)]
                stg = [tile(es, f"a_stg{i}", [128, 16, 128], BF16) for i in range(2)]
                for r_ in raw:
                    S.op("pool", lambda e: e.memset(r_[:], 0.0), W=[r_])
                last_mod = None
                for (kind, b, Tn, xsrc, hdst, ms) in streams:
                    if kind == "c" and l == 1:
                        pass
                    NT = Tn // 128
                    if last_mod != ms:
                        S.load(ab[:, 0, :], mv_ap(l, ms, MV_A1), W=[ab])
                        S.load(ab[:, 1, :], mv_ap(l, ms, MV_B1), W=[ab])
                        last_mod = ms
                    for i in range(NT):
                        x_ = xt[i % 2]
                        h_ = hx[i % 2]
                        S.load(x_[:], xsrc[i * 128:(i + 1) * 128, :], W=[x_])
                        S.op("act", lambda e: e.activation(out=junk[:], in_=x_[:], func=AF.Square, accum_out=ss[:, 0:1]), R=[x_], W=[junk, ss])
                        S.op("act", lambda e: e.activation(out=ss[:, 1:2], in_=ss[:, 0:1], func=AF.Sqrt, scale=1.0 / D, bias=EPS), R=[ss], W=[ss])
                        S.op("dve", lambda e: e.reciprocal(out=ss[:, 1:2], in_=ss[:, 1:2]), R=[ss], W=[ss])
                        S.op("dve", lambda e: e.scalar_tensor_tensor(out=junk[:], in0=x_[:], scalar=ss[:, 1:2], in1=ab[:, 0, :], op0=ALU.mult, op1=ALU.mult),
                             R=[x_, ss, ab], W=[junk])
                        S.op("dve", lambda e: e.tensor_tensor(out=h_[:], in0=junk[:], in1=ab[:, 1, :], op=ALU.add), R=[junk, ab], W=[h_])
                        pb = S.psb()
                        for k in range(8):
                            S.op("pe", lambda e: e.transpose(out=pb[:, k * 128:(k + 1) * 128], in_=h_[:, k * 128:(k + 1) * 128], identity=ident_bf[:]),
                                 R=[h_, ident_bf], W=[pb])
                        S.op("act", lambda e: e.copy(out=hxT[:, :, i * 128:(i + 1) * 128], in_=pb[:].rearrange("p (k t) -> p k t", k=8)), R=[pb], W=[hxT])
                    for k in range(8):
                        S.store(hdst[:, k * Tn:(k + 1) * Tn], hxT[:, k, 0:Tn], R=[hxT])
                    if kind == "c" and l == 1:
                        continue
                    udst = U if kind == "x" else UC
                    for j in range(12):
                        rw = raw[j % 2]
                        u_ = ub[j % 2]
                        for t0 in range(0, Tn, 512):
                            tw = min(512, Tn - t0)
                            p = S.ps()
                            for k in range(8):
                                S.op("pe", lambda e: e.matmul(p[:, 0:tw], lhsT=why[:, k, j * 128:(j + 1) * 128], rhs=hxT[:, k, t0:t0 + tw], start=(k == 0), stop=(k == 7)),
                                     R=[why, hxT], W=[p])
                            S.op("act", lambda e: e.activation(out=rw[:, 1 + t0:1 + t0 + tw], in_=p[:, 0:tw], func=AF.Identity, bias=ppl[:, j:j + 1]), R=[p, ppl], W=[rw])
                        if Tn < T:
                            S.op("pool", lambda e: e.memset(rw[:, Tn + 1:Tn + 2], 0.0), W=[rw])
                        S.op("act", lambda e: e.activation(out=tmp[:, 0:Tn], in_=rw[:, 1:Tn + 1], func=AF.Identity, scale=ppl[:, 24 + j:25 + j], bias=ppl[:, 48 + j:49 + j]),
                             R=[rw, ppl], W=[tmp])
                        S.op("dve", lambda e: e.scalar_tensor_tensor(out=tmp[:, 0:Tn], in0=rw[:, 0:Tn], scalar=ppl[:, 12 + j:13 + j], in1=tmp[:, 0:Tn], op0=ALU.mult, op1=ALU.add),
                             R=[rw, ppl, tmp], W=[tmp])
                        S.op("dve", lambda e: e.scalar_tensor_tensor(out=u_[:, 0:Tn], in0=rw[:, 2:Tn + 2], scalar=ppl[:, 36 + j:37 + j], in1=tmp[:, 0:Tn], op0=ALU.mult, op1=ALU.add),
                             R=[rw, ppl, tmp], W=[u_])
                        sg = stg[j % 2]
                        for i0 in range(0, NT, 8):
                            ni = min(8, NT - i0)
                            pb = S.psb()
                            for ii in range(ni):
                                S.op("pe", lambda e: e.transpose(out=pb[:, ii * 128:(ii + 1) * 128], in_=u_[:, (i0 + ii) * 128:(i0 + ii + 1) * 128], identity=ident_bf[:]),
                                     R=[u_, ident_bf], W=[pb])
                            S.op("act" if (i0 // 8) % 2 == 0 else "dve",
                                 (lambda e: e.copy(out=sg[:, i0:i0 + ni, :], in_=pb[:, 0:ni * 128].rearrange("p (i c) -> p i c", i=ni))) if (i0 // 8) % 2 == 0 else
                                 (lambda e: e.tensor_copy(out=sg[:, i0:i0 + ni, :], in_=pb[:, 0:ni * 128].rearrange("p (i c) -> p i c", i=ni))),
                                 R=[pb], W=[sg])
                        S.store(udst[j, b].rearrange("(i p) c -> p i c", p=128), sg[:, 0:NT, :], R=[sg])
                S.barrier()
            if stop_after == f"PA{l}":
                return nc

            with ExitStack() as es:
                hyena_fft(l)
                for kind, Tn, usrc, ksrc, ydst in (("c", TC, UC, KBC, YTC),):
                    if kind == "c" and l == 1:
                        continue
                    NBk = Tn // 128
                    GW = (2 * NBk - 1) * 128 + 1
                    with ExitStack() as es2:
                        vt = tile(es2, "h_v", [128, NB, NBk, 128], BF16)
                        x1 = tile(es2, "h_x1", [128, NB, NBk, 128], BF16)
                        x1r = tile(es2, "h_x1r", [128, NB, NBk, 128], BF16)
                        x2 = tile(es2, "h_x2", [128, NB, NBk, 128], BF16)
                        zt = tile(es2, "h_z", [128, NB, NBk, 128], BF16)
                        yh = tile(es2, "h_y", [128, NB, NBk, 128], BF16)
                        yT = tile(es2, "h_yT", [128, NB, Tn], BF16)
                        g = [tile(es2, f"h_g{i}", [128, GW], BF16) for i in range(3)]
                        gi = 0
                        NCOL = NB * NBk
                        for cc in range(4):
                            for b in range(NB):
                                S.load(vt[:, b], usrc[8 + cc, b].rearrange("(i p) c -> p i c", p=128), W=[vt])
                                S.load(x1[:, b], usrc[0 + cc, b].rearrange("(i p) c -> p i c", p=128), W=[x1])
                                S.load(x2[:, b], usrc[4 + cc, b].rearrange("(i p) c -> p i c", p=128), W=[x2])
                            x1f = x1[:].rearrange("p b i c -> p (b i c)")
                            x1rf = x1r[:].rearrange("p b i c -> p (b i c)")
                            for c0 in range(0, NCOL * 128, 512):
                                p = S.ps()
                                S.op("pe", lambda e: e.matmul(p[:], lhsT=identJ_bf[:], rhs=x1f[:, c0:c0 + 512], start=True, stop=True), R=[identJ_bf, x1], W=[p])
                                S.op("act", lambda e: e.copy(out=x1rf[:, c0:c0 + 512], in_=p[:]), R=[p], W=[x1r])
                            for o in range(2):
                                src_t = vt if o == 0 else zt
                                gate_t = x1r if o == 0 else x2
                                dst_t = zt if o == 0 else yh
                                CPB = 512 // NCOL if NCOL <= 512 else 1
                                for c8 in range(0, 128, CPB):
                                    p = S.ps()
                                    for ci in range(CPB):
                                        c = c8 + ci
                                        gt = g[gi % 3]
                                        gi += 1
                                        row = ksrc[o, cc * 128 + c]
                                        S.load(gt[:], bass.AP(row.tensor, row.offset, [[1, 128], [1, GW]]), W=[gt])
                                        lags = [0] + [d for d in range(-(NBk - 1), NBk) if d != 0]
                                        for n_, d in enumerate(lags):
                                            i_lo, i_hi = max(0, d), min(NBk - 1, NBk - 1 + d)
                                            ni = i_hi - i_lo + 1
                                            j_lo = i_lo - d
                                            g0 = (NBk - 1 - d) * 128 if o == 0 else (d + NBk - 1) * 128 + 1
                                            oap = p[:, ci * NCOL:(ci + 1) * NCOL].rearrange("p (b i) -> p b i", b=NB)[:, :, i_lo:i_lo + ni]
                                            S.op("pe", lambda e: e.matmul(oap, lhsT=gt[:, g0:g0 + 128], rhs=src_t[:, :, j_lo:j_lo + ni, c],
                                                                          start=(n_ == 0), stop=(n_ == len(lags) - 1)),
                                                 R=[gt, src_t], W=[p])
                                    S.op("dve", lambda e: e.tensor_tensor(out=dst_t[:].rearrange("p b i c -> p (b i) c")[:, :, c8:c8 + CPB],
                                                                          in0=gate_t[:].rearrange("p b i c -> p (b i) c")[:, :, c8:c8 + CPB],
                                                                          in1=p[:, 0:CPB * NCOL].rearrange("p (c n) -> p n c", c=CPB), op=ALU.mult),
                                         R=[gate_t, p], W=[dst_t])
                            for b in range(NB):
                                for i0 in range(0, NBk, 8):
                                    ni = min(8, NBk - i0)
                                    pb = S.psb()
                                    for ii in range(ni):
                                        S.op("pe", lambda e: e.transpose(out=pb[:, ii * 128:(ii + 1) * 128], in_=yh[:, b, i0 + ii, :], identity=ident_bf[:]),
                                             R=[yh, ident_bf], W=[pb])
                                    S.op("act", lambda e: e.copy(out=yT[:, b, i0 * 128:(i0 + ni) * 128], in_=pb[:, 0:ni * 128]), R=[pb], W=[yT])
                                S.store(ydst[b, cc * 128:(cc + 1) * 128, :], yT[:, b, :], R=[yT])
                        S.barrier()
            if stop_after == f"PH{l}":
                return nc
            if stop_after is not None and stop_after.startswith('PH'):
                pass
            try:
                PC(nc, S, l, streams, locals())
            except StopBuild:
                S.barrier()
                return nc
            if stop_after == f"PC{l}":
                return nc
        S.barrier()
    return nc


def _host_inputs(inp, core, consts, pps):
    b0 = core * NB
    m = {}
    m["x"] = np.ascontiguousarray(inp["x"][b0:b0 + NB])
    m["ctx"] = np.ascontiguousarray(inp["ctx"][b0:b0 + NB])
    call = np.concatenate([inp["c"][b0:b0 + NB], inp["c_ctx"][None]], 0)
    m["cT"] = np.ascontiguousarray(call.T.reshape(8, 128, 5).transpose(1, 0, 2).reshape(128, 40))
    for k in ["w_mod", "b_mod", "norm_mix", "norm_ffn", "w_in", "b_in", "hy_f_w1", "hy_f_w2", "hy_f_w3", "gla_wa_f", "gla_ba_f",
              "gla_wa_b", "gla_ba_b", "gla_norm_w", "hy_skip", "w_br_hy", "w_br_gla", "w_br_pool", "w_out", "w_up", "w_down", "norm_final"]:
        m[k] = inp[k]
    m["pool_w"] = inp["pool_w"].reshape(2, 512, 128)
    m["pp"] = pps
    for k, v in consts.items():
        m["c_" + k] = v
    return m


def kernel(**inputs):
    inp = {k: np.ascontiguousarray(np.asarray(v, dtype=np.float32)) for k, v in inputs.items()}
    consts = _consts()
    pps = np.stack([_pp_pack(inp, 0), _pp_pack(inp, 1)])
    nc = build()
    in_maps = [_host_inputs(inp, c, consts, pps) for c in range(NCORES)]
    res = run_bass_kernel_spmd(nc, in_maps, core_ids=list(range(NCORES)))
    outs = [np.asarray(r["out"], dtype=np.float32) for r in res.results]
    return np.concatenate(outs, axis=0)
```

```python
import math
import numpy as np
import ml_dtypes
from contextlib import ExitStack
import concourse.bass as bass
import concourse.mybir as mybir
from concourse.bass_utils import run_bass_kernel_spmd

F32 = mybir.dt.float32
BF16 = mybir.dt.bfloat16
AF = mybir.ActivationFunctionType
ALU = mybir.AluOpType
AX = mybir.AxisListType

D = 1024
T = 2048
TC = 256
NB = 4
NCORES = 8
EPS = 1e-6
N_IN = 6688
GLA_OFF = 1536
POOL_OFF = 3104
GATE_OFF = 3616
EPOCH = 30000
NDS = 40
PI = math.pi


class Buf:
    __slots__ = ("w", "r")

    def __init__(self):
        self.w = None
        self.r = {}


class Tl:
    def __init__(self, t):
        self.t = t
        self.b = Buf()

    def __getitem__(self, k):
        return self.t[k]


class Sched:
    def __init__(self, nc, es):
        self.nc = nc
        self.es = es
        self.eng = {"pe": nc.tensor, "act": nc.scalar, "dve": nc.vector, "pool": nc.gpsimd, "sp": nc.sync}
        self.cnt = {k: 0 for k in self.eng}
        self.sems = {k: [] for k in self.eng}
        self.seen = {k: {} for k in self.eng}
        self.dsem = [es.enter_context(nc.semaphore(f"dq{i}")) for i in range(NDS)]
        self.dval = [0] * NDS
        self.dnext = 0
        self.nwait = 0
        self.ps_ring = []
        self.ps_i = 0
        self.psb_ring = []
        self.psb_i = 0
        self.ldq = 0
        self.dead = False

    def _sem(self, k, ep):
        while len(self.sems[k]) <= ep:
            self.sems[k].append(self.es.enter_context(self.nc.semaphore(f"s_{k}_{len(self.sems[k])}")))
        return self.sems[k][ep]

    def _wait(self, k, tok):
        if tok is None:
            return
        if tok[0] == "e":
            _, src, n = tok
            if n == 0:
                return
            if src == "pe" and k == "pe":
                return
            if self.seen[k].get(src, 0) >= n:
                return
            self.seen[k][src] = n
            self.eng[k].wait_ge(self._sem(src, (n - 1) // EPOCH), (n - 1) % EPOCH + 1)
        else:
            _, i, v = tok
            key = ("d", i)
            if self.seen[k].get(key, 0) >= v:
                return
            self.seen[k][key] = v
            self.eng[k].wait_ge(self.dsem[i], v)
        self.nwait += 1

    def _deps(self, k, R, W):
        for b in R:
            self._wait(k, b.w)
        for b in W:
            self._wait(k, b.w)
            for t in b.r.values():
                self._wait(k, t)

    def op(self, k, fn, R=(), W=()):
        if self.dead:
            return
        R = [x.b if isinstance(x, Tl) else x for x in R]
        W = [x.b if isinstance(x, Tl) else x for x in W]
        self._deps(k, R, W)
        ins = fn(self.eng[k])
        self.cnt[k] += 1
        n = self.cnt[k]
        ins.then_inc(self._sem(k, (n - 1) // EPOCH), 1)
        t = ("e", k, n)
        for b in R:
            b.r[k] = t
        for b in W:
            b.w = t
            b.r = {}

    def dma(self, q, out, in_, R=(), W=()):
        if self.dead:
            return
        R = [x.b if isinstance(x, Tl) else x for x in R]
        W = [x.b if isinstance(x, Tl) else x for x in W]
        i = self.dnext
        self.dnext = (i + 1) % NDS
        if self.dval[i] > 0:
            self._wait(q, ("d", i, self.dval[i]))
        self._deps(q, R, W)
        self.dval[i] += 16
        self.eng[q].dma_start(out=out, in_=in_).then_inc(self.dsem[i], 16)
        t = ("d", i, self.dval[i])
        for b in R:
            b.r[("d", i)] = t
        for b in W:
            b.w = t
            b.r = {}

    def load(self, out, in_, R=(), W=()):
        self.dma("sp", out, in_, R, W)

    def store(self, out, in_, R=(), W=()):
        self.dma("pool", out, in_, R, W)

    def barrier(self):
        if self.dead:
            return
        toks = [("e", k, self.cnt[k]) for k in self.eng] + [("d", i, self.dval[i]) for i in range(NDS) if self.dval[i] > 0]
        for k in self.eng:
            for t in toks:
                if t[0] == "e" and t[1] == k:
                    continue
                self._wait(k, t)

    def ps(self):
        p = self.ps_ring[self.ps_i]
        self.ps_i = (self.ps_i + 1) % len(self.ps_ring)
        return p

    def psb(self):
        p = self.psb_ring[self.ps_i]
        self.ps_i = (self.ps_i + 1) % len(self.ps_ring)
        return p


def _pool_tables():
    mats = []
    seen = {}
    nbrs = {}
    rcnt = {}
    for mode, L in (("g", T), ("c", TC)):
        t = np.arange(L)
        rc_all = np.zeros((4, L), np.float32)
        for g, w in enumerate((2, 4, 8, 16)):
            if mode == "g":
                r = t // 64
                c = t % 64
                rl = np.clip(r - w // 2, 0, 32)
                rh = np.clip(r - w // 2 + w, 0, 32)
                cl = np.clip(c - w // 2, 0, 64)
                ch = np.clip(c - w // 2 + w, 0, 64)
                P = ((r[:, None] >= rl[None, :]) & (r[:, None] < rh[None, :]) & (c[:, None] >= cl[None, :]) & (c[:, None] < ch[None, :])).astype(np.float32)
                cnt = ((rh - rl) * (ch - cl)).astype(np.float32)
            else:
                lo = np.clip(t - w // 2, 0, L)
                hi = np.clip(t - w // 2 + w, 0, L)
                P = ((t[:, None] >= lo[None, :]) & (t[:, None] < hi[None, :])).astype(np.float32)
                cnt = (hi - lo).astype(np.float32)
            P = P - np.diag(cnt)
            rc_all[g] = 1.0 / cnt
            for n in range(L // 128):
                lst = []
                for n2 in range(L // 128):
                    blk = np.ascontiguousarray(P[n2 * 128:(n2 + 1) * 128, n * 128:(n + 1) * 128])
                    if np.any(blk):
                        key = blk.tobytes()
                        if key not in seen:
                            seen[key] = len(mats)
                            mats.append(blk)
                        lst.append((n2, seen[key]))
                nbrs[(mode, g, n)] = lst
        rcnt[mode] = rc_all
    pm = np.stack(mats, 1)
    return pm, nbrs, rcnt


_POOL = _pool_tables()
NU = _POOL[0].shape[1]


def _z_table(L, NP):
    pos = np.abs(np.arange(NP) - (L - 1)).astype(np.float32)
    bands = np.arange(1, 9, dtype=np.float32)[None, :]
    ang = (np.float32(2.0 * math.pi / L) * pos[:, None]) * bands
    z = np.concatenate([pos[:, None] / np.float32(L), np.cos(ang), np.sin(ang)], -1).astype(np.float32)
    zT = np.zeros((32, NP), np.float32)
    zT[:17] = z.T
    posn = np.zeros((2, NP), np.float32)
    posn[0] = np.abs(np.arange(NP) - (L - 1)) / np.float32(L)
    posn[1] = np.abs(np.arange(NP) - L) / np.float32(L)
    return zT, np.ascontiguousarray(np.broadcast_to(posn[None], (128, 2, NP)))


def _fft_consts():
    N = 4096
    bf = ml_dtypes.bfloat16
    n1 = np.arange(32)
    n2 = np.arange(64)
    k1 = np.arange(64)
    th = 2 * np.pi * (((64 * n1[:, None, None] + n2[None, :, None]) * k1[None, None, :]) % N) / N
    F1x = np.zeros((64, 64, 128))
    F1x[0:32, :, 0:64] = np.cos(th)
    F1x[32:64, :, 0:64] = np.sin(th)
    F1x[0:32, :, 64:128] = -np.sin(th)
    F1x[32:64, :, 64:128] = np.cos(th)
    n1f = np.arange(64)
    thk = 2 * np.pi * (((64 * n1f[:, None, None] + n2[None, :, None]) * k1[None, None, :]) % N) / N
    F1k = np.zeros((64, 64, 128))
    F1k[:, :, 0:64] = np.cos(thk)
    F1k[:, :, 64:128] = -np.sin(thk)
    k2 = np.arange(64)
    ph = 2 * np.pi * ((n2[:, None] * k2[None, :]) % 64) / 64
    c2, s2 = np.cos(ph), np.sin(ph)
    F2 = np.zeros((64, 8, 128))
    for i, (a_, b_) in enumerate(((c2, -s2), (s2, c2), (-s2, c2), (c2, s2), (c2, c2), (s2, s2), (s2, -s2), (-c2, c2))):
        F2[:, i, 0:64] = a_
        F2[:, i, 64:128] = b_
    ph3 = ph.T
    F3 = np.zeros((128, 128))
    F3[0:64, 0:64] = np.cos(ph3)
    F3[64:, 0:64] = -np.sin(ph3)
    F3[0:64, 64:] = np.sin(ph3)
    F3[64:, 64:] = np.cos(ph3)
    F3 /= N
    th4 = 2 * np.pi * (((64 * n1[None, None, :] + n2[None, :, None]) * k1[:, None, None]) % N) / N
    F4 = np.zeros((64, 64, 2, 64))
    F4[:, :, 0, 0:32] = np.cos(th4)
    F4[:, :, 0, 32:] = np.sin(th4)
    F4[:, :, 1, 0:32] = -np.sin(th4)
    F4[:, :, 1, 32:] = np.cos(th4)
    n = 64 * n1f[:, None] + n2[None, :]
    tau = np.minimum(n, N - n).astype(np.float64)
    deltas = np.abs(np.linspace(math.log(1e-2) / 1.5, math.log(1e-2) / 0.3, 512, dtype=np.float32)).astype(np.float64)
    wtab = np.exp(-(tau[:, :, None] / 2048.0) * deltas[None, None, :])
    wtab[32, 0, :] = 0.0
    pos = np.minimum(np.arange(N), N - np.arange(N)).astype(np.float32)
    bands = np.arange(1, 9, dtype=np.float32)[None, :]
    ang = (np.float32(2.0 * math.pi / 2048) * pos[:, None]) * bands
    z = np.concatenate([pos[:, None] / np.float32(2048), np.cos(ang), np.sin(ang)], -1).astype(np.float32)
    zT = np.zeros((32, N), np.float32)
    zT[:17] = z.T
    return {"F1x": F1x.astype(bf), "F1k": F1k.astype(bf), "F2": F2.astype(bf), "F3": F3.astype(bf), "F4": F4.astype(bf),
            "wtab": wtab.astype(np.float32), "zT_f": zT}


def _consts():
    c = {}
    eye = np.eye(128, dtype=np.float32)
    c["ident_bf"] = eye.astype(ml_dtypes.bfloat16)
    c["identJ_bf"] = eye[::-1].copy().astype(ml_dtypes.bfloat16)
    c["ident_f"] = eye
    s = np.arange(128)[:, None]
    t = np.arange(128)[None, :]
    same = (s // 64) == (t // 64)
    tri = np.zeros((128, 4, 128), np.float32)
    tri[:, 0] = np.where(same & (s <= t), -1.0 / 16, 0)
    tri[:, 1] = np.where(same & (s > t), -1.0 / 16, 0)
    tri[:, 2] = np.where(same & (s >= t), -1.0 / 16, 0)
    tri[:, 3] = np.where(same & (s < t), -1.0 / 16, 0)
    c["tri"] = tri.astype(ml_dtypes.bfloat16)
    msk = np.zeros((128, 2, 128), np.float32)
    msk[:, 0] = np.where(same & (t >= s), 1.0, 0)
    msk[:, 1] = np.where(same & (t <= s), 1.0, 0)
    c["msk"] = msk
    c["pmat"] = _POOL[0].astype(ml_dtypes.bfloat16)
    c["rcnt_g"] = np.ascontiguousarray(np.broadcast_to(_POOL[2]["g"][None], (128, 4, T)))
    c["rcnt_c"] = np.ascontiguousarray(np.broadcast_to(_POOL[2]["c"][None], (128, 4, TC)))
    c["zT_x"], c["posn_x"] = _z_table(T, 2 * T)
    c["zT_c"], c["posn_c"] = _z_table(TC, 2 * TC)
    c.update(_fft_consts())
    c["ones_f"] = np.ones((1, 128), np.float32)
    c["ones_bf"] = np.ones((1, 128), ml_dtypes.bfloat16)
    return c


NPP = 112


def _pp_pack(inp, l):
    pp = np.zeros((128, NPP), np.float32)
    b_in = inp["b_in"][l]
    pp[:, 0:12] = b_in[0:1536].reshape(12, 128).T
    for k in range(3):
        pp[:, 12 + k * 12:24 + k * 12] = inp["hy_short_w"][l][k].reshape(12, 128).T
    pp[:, 48:60] = inp["hy_short_b"][l].reshape(12, 128).T
    pp[:, 60:64] = b_in[1536:2048].reshape(4, 128).T
    pp[:, 64:88] = b_in[GATE_OFF:GATE_OFF + 3072].reshape(24, 128).T
    pp[:, 88:92] = inp["pool_scale"][l].reshape(4, 128).T
    pp[0:16, 92] = b_in[3072:3088]
    pp[0:16, 93] = b_in[3088:3104]
    pp[0:64, 94] = inp["hy_f_freq"][l]
    pp[0:64, 95] = inp["hy_f_b1"][l]
    pp[0:64, 96] = inp["hy_f_b2"][l]
    deltas = np.abs(np.linspace(math.log(1e-2) / 1.5, math.log(1e-2) / 0.3, 512, dtype=np.float32))
    pp[:, 98:102] = (-deltas).reshape(4, 128).T
    pp[:, 102:106] = inp["hy_skip"][l][0].reshape(4, 128).T
    pp[:, 106:110] = inp["hy_skip"][l][1].reshape(4, 128).T
    return pp


def PC(nc, S, l, streams, E):
    tile = E["tile"]; pp = E["pp"]; Wb = E["Wb"]; C = E["C"]; MODV = E["MODV"]
    ident_bf = E["ident_bf"]; ones_bf = E["ones_bf"]; ones_f = E["ones_f"]
    XS = E["XS"]; CXS = E["CXS"]; YT = E["YT"]; YTC = E["YTC"]; out = E["out"]
    b_in = E["b_in"]; wa_f = E["wa_f"]; wa_b = E["wa_b"]; ba_f = E["ba_f"]; ba_b = E["ba_b"]
    gla_nw = E["gla_nw"]; norm_final = E["norm_final"]; bcast_row = E["bcast_row"]
    ppl = pp[l]
    chk = E["chk"]
    last = (l == 1)
    nbrs = _POOL[1]
    win = Wb[("in", l)]

    def mvap(s, j):
        return MODV[l, s][:, j * D:(j + 1) * D]

    with ExitStack() as es:
        hxT = tile(es, "c_hxT", [128, 8, T], BF16)
        yglaT = tile(es, "c_yglaT", [128, 4, T], BF16)
        Sst = tile(es, "c_Sst", [128, 2, 4, 128], F32)
        SstB = [Buf(), Buf()]
        tri = tile(es, "c_tri", [128, 4, 128], BF16)
        msk = tile(es, "c_msk", [128, 2, 128], F32)
        gnw = tile(es, "c_gnw", [128, 128], F32)
        brow = tile(es, "c_brow", [1, N_IN], BF16)
        waug = tile(es, "c_waug", [32, 2, 256], BF16)
        es_row = ExitStack()
        rowf = tile(es_row, "c_rowf", [1, N_IN], F32)
        waf = tile(es_row, "c_waf", [32, 2, 256], F32)
        S.load(tri[:], C["tri"], W=[tri])
        S.load(msk[:], C["msk"], W=[msk])
        bcast_row(es, gnw, gla_nw[l:l + 1, :], 128, rowf)
        S.load(rowf[:], b_in[l:l + 1, :], W=[rowf])
        S.op("dve", lambda e: e.tensor_copy(out=brow[:], in_=rowf[:]), R=[rowf], W=[brow])
        S.op("dve", lambda e: e.memset(waf[:], 0.0), W=[waf])
        S.load(waf[0:16, 0, :], wa_f[l], W=[waf])
        S.load(waf[0:16, 1, :], wa_b[l], W=[waf])
        S.load(waf[16:17, 0, :], ba_f[l:l + 1, :], W=[waf])
        S.load(waf[16:17, 1, :], ba_b[l:l + 1, :], W=[waf])
        S.op("dve", lambda e: e.tensor_copy(out=waug[:], in_=waf[:]), R=[waf], W=[waug])
        S.barrier()
        es_row.close()

        def tm_proj(p, ncols, i, wt, c0, bcol0):
            for k in range(8):
                S.op("pe", lambda e: e.matmul(p[:, 0:ncols], lhsT=hxT[:, k, i * 128:(i + 1) * 128], rhs=wt[:, k, c0:c0 + ncols], start=(k == 0), stop=False),
                     R=[hxT, wt], W=[p])
            S.op("pe", lambda e: e.matmul(p[:, 0:ncols], lhsT=ones_bf[0:1, :], rhs=brow[0:1, bcol0:bcol0 + ncols], start=False, stop=True),
                 R=[ones_bf, brow], W=[p])

        for (kind, b, Tn, xsrc, hdst, ms) in streams:
            NT = Tn // 128
            ctx_l1 = (kind == "c" and l == 1)
            for k in range(8):
                S.load(hxT[:, k, 0:Tn], hdst[:, k * Tn:(k + 1) * Tn], W=[hxT])
            with ExitStack() as e2:
                wg = tile(e2, "g_w", [128, 8, 1568], BF16)
                for k in range(8):
                    S.load(wg[:, k, :], win[k * 128:(k + 1) * 128, GLA_OFF:POOL_OFF], W=[wg])
                vtm = tile(e2, "g_v", [128, 16, 512], BF16)
                ktm = tile(e2, "g_k", [128, 16, 256], BF16)
                qkT = tile(e2, "g_qkT", [128, 4, T], BF16)
                gTa = tile(e2, "g_gT", [32, 2, T], BF16)
                oacc = tile(e2, "g_o", [128, 16, 512], F32)
                oab = [Buf() for _ in range(16)]
                S.op("pool", lambda e: e.memset(gTa[:], 1.0), W=[gTa])
                if not ctx_l1:
                    S.op("pool", lambda e: e.memset(oacc[:, 0:NT, :], 0.0), W=oab[0:NT])
                if kind == "c":
                    S.op("pool", lambda e: e.memset(Sst[:], 0.0), W=[SstB[0], SstB[1]])
                for i in range(NT):
                    p = S.ps()
                    tm_proj(p, 512, i, wg, 512, 2048)
                    S.op("act", lambda e: e.copy(out=vtm[:, i, :], in_=p[:]), R=[p], W=[vtm])
                    p = S.ps()
                    tm_proj(p, 256, i, wg, 256, 1792)
                    S.op("dve", lambda e: e.tensor_copy(out=ktm[:, i, :], in_=p[:, 0:256]), R=[p], W=[ktm])
                for j in range(4):
                    for t0 in range(0, Tn, 512):
                        tw = min(512, Tn - t0)
                        p = S.ps()
                        for k in range(8):
                            S.op("pe", lambda e: e.matmul(p[:, 0:tw], lhsT=wg[:, k, j * 128:(j + 1) * 128], rhs=hxT[:, k, t0:t0 + tw], start=(k == 0), stop=(k == 7)),
                                 R=[wg, hxT], W=[p])
                        S.op("act", lambda e: e.activation(out=qkT[:, j, t0:t0 + tw], in_=p[:, 0:tw], func=AF.Identity, bias=ppl[:, 60 + j:61 + j]), R=[p, ppl], W=[qkT])
                for dr in range(2):
                    for t0 in range(0, Tn, 512):
                        tw = min(512, Tn - t0)
                        p = S.ps()
                        for k in range(8):
                            S.op("pe", lambda e: e.matmul(p[0:16, 0:tw], lhsT=wg[:, k, 1536 + 16 * dr:1552 + 16 * dr], rhs=hxT[:, k, t0:t0 + tw], start=(k == 0), stop=(k == 7)),
                                 R=[wg, hxT], W=[p])
                        S.op("act", lambda e: e.activation(out=gTa[0:16, dr, t0:t0 + tw], in_=p[0:16, 0:tw], func=AF.Identity, bias=ppl[0:16, 92 + dr:93 + dr]), R=[p, ppl], W=[gTa])

                def chain(dr, P0, P1, P2):
                    spt = tile(e2, f"g_sp{dr}", [128, 256], BF16)
                    erb = tile(e2, f"g_erb{dr}", [128, 256], F32)
                    kend = tile(e2, f"g_kend{dr}", [128, 256], BF16)
                    eb = tile(e2, f"g_eb{dr}", [128, 2, 128], F32)
                    ebi = tile(e2, f"g_ebi{dr}", [128, 2, 128], F32)
                    qd = tile(e2, f"g_qd{dr}", [128, 2, 128], BF16)
                    ki = tile(e2, f"g_ki{dr}", [128, 2, 128], BF16)
                    qA = tile(e2, f"g_qA{dr}", [128, 2, 128], BF16)
                    qB = tile(e2, f"g_qB{dr}", [128, 2, 128], BF16)
                    scm = [tile(e2, f"g_scm{dr}{i}", [128, 128], BF16) for i in range(2)]
                    Smid = tile(e2, f"g_Smid{dr}", [128, 4, 128], F32)
                    Sb0 = tile(e2, f"g_Sb0{dr}", [128, 4, 128], BF16)
                    Sb1 = tile(e2, f"g_Sb1{dr}", [128, 4, 128], BF16)
                    Sd = SstB[dr]
                    S.op("pool", lambda e: e.memset(qA[:], 0.0), W=[qA])
                    S.op("pool", lambda e: e.memset(qB[:], 0.0), W=[qB])
                    yield
                    order = list(range(NT)) if dr == 0 else list(range(NT - 1, -1, -1))
                    for i in order:
                        tsl = slice(i * 128, (i + 1) * 128)
                        p = P0
                        S.op("pe", lambda e: e.matmul(p[:, 0:256], lhsT=gTa[0:32, dr, tsl], rhs=waug[0:32, dr, :], start=True, stop=True), R=[gTa, waug], W=[p])
                        yield
                        S.op("act", lambda e: e.activation(out=erb[:], in_=p[:, 0:256], func=AF.Exp, scale=-1.0), R=[p], W=[erb])
                        S.op("act", lambda e: e.activation(out=spt[:], in_=erb[:], func=AF.Ln, bias=1.0), R=[erb], W=[spt])
                        yield
                        S.op("pe", lambda e: e.matmul(P1[:, 0:256], lhsT=tri[:, 2 * dr + 1, :], rhs=spt[:], start=True, stop=True), R=[tri, spt], W=[P1])
                        for ch in range(2):
                            S.op("pe", lambda e: e.matmul(P2[:, ch * 128:(ch + 1) * 128], lhsT=spt[:, ch * 128:(ch + 1) * 128], rhs=tri[:, 2 * dr, :], start=True, stop=True),
                                 R=[tri, spt], W=[P2])
                        yield
                        S.op("act", lambda e: e.activation(out=erb[:], in_=P1[:, 0:256], func=AF.Exp), R=[P1], W=[erb])
                        S.op("act", lambda e: e.activation(out=eb[:].rearrange("p a b -> p (a b)"), in_=P2[:, 0:256], func=AF.Exp), R=[P2], W=[eb])
                        S.op("act", lambda e: e.activation(out=ebi[:].rearrange("p a b -> p (a b)"), in_=P2[:, 0:256], func=AF.Exp, scale=-1.0), R=[P2], W=[ebi])
                        yield
                        S.op("dve", lambda e: e.tensor_tensor(out=kend[:], in0=ktm[:, i, :], in1=erb[:], op=ALU.mult), R=[ktm, erb], W=[kend])
                        S.op("dve", lambda e: e.scalar_tensor_tensor(out=qd[:], in0=eb[:], scalar=0.125, in1=qkT[:, 0:2, tsl], op0=ALU.mult, op1=ALU.mult), R=[eb, qkT], W=[qd])
                        S.op("dve", lambda e: e.tensor_tensor(out=ki[:], in0=ebi[:], in1=qkT[:, 2:4, tsl], op=ALU.mult), R=[ebi, qkT], W=[ki])
                        yield
                        S.op("pool", lambda e: e.tensor_copy(out=qA[:, :, 0:64], in_=qd[:, :, 0:64]), R=[qd], W=[qA])
                        S.op("pool", lambda e: e.tensor_copy(out=qB[:, :, 64:128], in_=qd[:, :, 64:128]), R=[qd], W=[qB])
                        pk0, pk1 = P1, P2
                        for h in range(4):
                            cq = h // 2
                            S.op("pe", lambda e: e.matmul(pk0[:, h * 128:(h + 1) * 128], lhsT=kend[0:64, cq * 128:(cq + 1) * 128], rhs=vtm[0:64, i, h * 128:(h + 1) * 128], start=True, stop=True),
                                 R=[kend, vtm], W=[pk0])
                            S.op("pe", lambda e: e.matmul(pk1[:, h * 128:(h + 1) * 128], lhsT=kend[64:128, cq * 128:(cq + 1) * 128], rhs=vtm[64:128, i, h * 128:(h + 1) * 128], start=True, stop=True),
                                 R=[kend, vtm], W=[pk1])
                        if dr == 0:
                            pF, pS_, cF, cS, qF, qS = pk0, pk1, 63, 127, qA, qB
                        else:
                            pF, pS_, cF, cS, qF, qS = pk1, pk0, 64, 0, qB, qA
                        S.op("act", lambda e: e.copy(out=Sb0[:], in_=Sst[:, dr]), R=[Sd], W=[Sb0])
                        yield
                        for cq in range(2):
                            S.op("dve", lambda e: e.scalar_tensor_tensor(out=Smid[:, 2 * cq:2 * cq + 2, :], in0=Sst[:, dr, 2 * cq:2 * cq + 2, :], scalar=eb[:, cq, cF:cF + 1],
                                                                          in1=pF[:, cq * 256:(cq + 1) * 256].rearrange("p (a b) -> p a b", a=2), op0=ALU.mult, op1=ALU.add),
                                 R=[Sd, eb, pF], W=[Smid])
                        yield
                        S.op("act", lambda e: e.copy(out=Sb1[:], in_=Smid[:]), R=[Smid], W=[Sb1])
                        for cq in range(2):
                            S.op("dve", lambda e: e.scalar_tensor_tensor(out=Sst[:, dr, 2 * cq:2 * cq + 2, :], in0=Smid[:, 2 * cq:2 * cq + 2, :], scalar=eb[:, cq, cS:cS + 1],
                                                                          in1=pS_[:, cq * 256:(cq + 1) * 256].rearrange("p (a b) -> p a b", a=2), op0=ALU.mult, op1=ALU.add),
                                 R=[Smid, eb, pS_], W=[Sd])
                        yield
                        if ctx_l1:
                            continue
                        po = P0
                        for h in range(4):
                            cq = h // 2
                            r0 = 64 * (h % 2)
                            p_s = P1 if h % 2 == 0 else P2
                            S.op("pe", lambda e: e.matmul(p_s[:, 0:128], lhsT=ki[r0:r0 + 64, cq, :], rhs=qd[r0:r0 + 64, cq, :], start=True, stop=True), R=[ki, qd], W=[p_s])
                            yield
                            sc = scm[h % 2]
                            S.op("dve", lambda e: e.tensor_tensor(out=sc[:], in0=p_s[:, 0:128], in1=msk[:, dr, :], op=ALU.mult), R=[p_s, msk], W=[sc])
                            yield
                            S.op("pe", lambda e: e.matmul(po[:, h * 128:(h + 1) * 128], lhsT=sc[:], rhs=vtm[:, i, h * 128:(h + 1) * 128], start=True, stop=False), R=[sc, vtm], W=[po])
                            S.op("pe", lambda e: e.matmul(po[:, h * 128:(h + 1) * 128], lhsT=qF[r0:r0 + 64, cq, :], rhs=Sb0[r0:r0 + 64, h, :], start=False, stop=False), R=[qF, Sb0], W=[po])
                            S.op("pe", lambda e: e.matmul(po[:, h * 128:(h + 1) * 128], lhsT=qS[r0:r0 + 64, cq, :], rhs=Sb1[r0:r0 + 64, h, :], start=False, stop=True), R=[qS, Sb1], W=[po])
                        yield
                        S.op("dve", lambda e: e.tensor_tensor(out=oacc[:, i, :], in0=oacc[:, i, :], in1=po[:], op=ALU.add), R=[po, oab[i]], W=[oab[i]])
                        yield

                gens = [chain(0, S.ps_ring[0], S.ps_ring[1], S.ps_ring[2]), chain(1, S.ps_ring[3], S.ps_ring[4], S.ps_ring[5])]
                alive = list(gens)
                while alive:
                    for g_ in list(alive):
                        try:
                            next(g_)
                        except StopIteration:
                            alive.remove(g_)
                if not ctx_l1:
                    sr = tile(e2, "g_sr", [128, 512], F32)
                    tmpo = tile(e2, "g_tmpo", [128, 512], F32)
                    ybfs = [tile(e2, f"g_ybf{i}", [128, 512], BF16) for i in range(2)]
                    ss4 = tile(e2, "g_ss4", [128, 8], F32)

                    def nA(i):
                        ybf = ybfs[i % 2]
                        p = S.ps()
                        tm_proj(p, 512, i, wg, 1024, 2560)
                        S.op("act", lambda e: e.activation(out=sr[:], in_=p[:], func=AF.Silu), R=[p], W=[sr])
                        for h in range(4):
                            S.op("act", lambda e: e.activation(out=tmpo[:, h * 128:(h + 1) * 128], in_=oacc[:, i, h * 128:(h + 1) * 128], func=AF.Square, accum_out=ss4[:, h:h + 1]),
                                 R=[oab[i]], W=[tmpo, ss4])
                        S.op("act", lambda e: e.activation(out=ss4[:, 4:8], in_=ss4[:, 0:4], func=AF.Sqrt, scale=1.0 / 128, bias=EPS), R=[ss4], W=[ss4])
                        S.op("dve", lambda e: e.reciprocal(out=ss4[:, 4:8], in_=ss4[:, 4:8]), R=[ss4], W=[ss4])
                        for h in range(4):
                            S.op("dve", lambda e: e.scalar_tensor_tensor(out=tmpo[:, h * 128:(h + 1) * 128], in0=oacc[:, i, h * 128:(h + 1) * 128], scalar=ss4[:, 4 + h:5 + h],
                                                                          in1=gnw[:], op0=ALU.mult, op1=ALU.mult), R=[oab[i], ss4, gnw], W=[tmpo])
                        S.op("dve", lambda e: e.tensor_tensor(out=ybf[:], in0=tmpo[:], in1=sr[:], op=ALU.mult), R=[tmpo, sr], W=[ybf])

                    def nB(i):
                        ybf = ybfs[i % 2]
                        pb = S.psb()
                        for k in range(4):
                            S.op("pe", lambda e: e.transpose(out=pb[:, k * 128:(k + 1) * 128], in_=ybf[:, k * 128:(k + 1) * 128], identity=ident_bf[:]), R=[ybf, ident_bf], W=[pb])
                        S.op("act", lambda e: e.copy(out=yglaT[:, :, i * 128:(i + 1) * 128], in_=pb[:, 0:512].rearrange("p (k t) -> p k t", k=4)), R=[pb], W=[yglaT])

                    nA(0)
                    for i in range(NT):
                        if i + 1 < NT:
                            nA(i + 1)
                        nB(i)
                S.barrier()
            chk(f"PC{l}_gla_{kind}{b}")
            if ctx_l1:
                continue
            e3 = ExitStack()
            ypoolT = tile(e3, "c_ypoolT", [128, 4, T], BF16)
            with ExitStack() as e2:
                wp = tile(e2, "p_w", [128, 8, 512], BF16)
                for k in range(8):
                    S.load(wp[:, k, :], win[k * 128:(k + 1) * 128, POOL_OFF:GATE_OFF], W=[wp])
                up = tile(e2, "p_up", [128, 16, 512], BF16)
                rc = tile(e2, "p_rc", [128, 4, Tn], F32)
                pm = tile(e2, "p_pm", [128, NU, 128], BF16)
                pw = tile(e2, "p_pw", [128, 4, 128], BF16)
                mT = tile(e2, "p_mT", [128, 4, T], BF16)
                S.load(rc[:], C["rcnt_g"] if kind == "x" else C["rcnt_c"], W=[rc])
                S.load(pm[:], C["pmat"], W=[pm])
                for g in range(4):
                    S.load(pw[:, g, :], Wb[("pw", l)][g * 128:(g + 1) * 128, :], W=[pw])
                for i in range(NT):
                    p = S.ps()
                    tm_proj(p, 512, i, wp, 0, POOL_OFF)
                    S.op("act", lambda e: e.copy(out=up[:, i, :], in_=p[:]), R=[p], W=[up])
                mode = "g" if kind == "x" else "c"
                for g in range(4):
                    for n0 in range(0, NT, 4):
                        nn = min(4, NT - n0)
                        p = S.ps()
                        for n in range(n0, n0 + nn):
                            lst = nbrs[(mode, g, n)]
                            for q_, (n2, idx) in enumerate(lst):
                                S.op("pe", lambda e: e.matmul(p[:, (n - n0) * 128:(n - n0 + 1) * 128], lhsT=up[:, n2, g * 128:(g + 1) * 128], rhs=pm[:, idx, :],
                                                              start=(q_ == 0), stop=(q_ == len(lst) - 1)), R=[up, pm], W=[p])
                        S.op("dve", lambda e: e.tensor_tensor(out=mT[:, g, n0 * 128:(n0 + nn) * 128], in0=p[:, 0:nn * 128], in1=rc[:, g, n0 * 128:(n0 + nn) * 128], op=ALU.mult),
                             R=[p, rc], W=[mT])
                for g in range(4):
                    for t0 in range(0, Tn, 512):
                        tw = min(512, Tn - t0)
                        p = S.ps()
                        S.op("pe", lambda e: e.matmul(p[:, 0:tw], lhsT=pw[:, g, :], rhs=mT[:, g, t0:t0 + tw], start=True, stop=True), R=[pw, mT], W=[p])
                        S.op("act", lambda e: e.activation(out=ypoolT[:, g, t0:t0 + tw], in_=p[:, 0:tw], func=AF.Copy, scale=ppl[:, 88 + g:89 + g]), R=[p, ppl], W=[ypoolT])
                S.barrier()
            chk(f"PC{l}_pool_{kind}{b}")
            mergedT = tile(e3, "c_mergedT", [128, 8, T], BF16)
            with ExitStack() as e2:
                yhyT = tile(e2, "m_yhy", [128, 4, T], BF16)
                ysrc = YT[b] if kind == "x" else YTC[b]
                for cc in range(4):
                    S.load(yhyT[:, cc, 0:Tn], ysrc[cc * 128:(cc + 1) * 128, :], W=[yhyT])
                wbr = tile(e2, "m_wbr", [128, 3, 4, D], BF16)
                for br in range(3):
                    for k in range(4):
                        S.load(wbr[:, br, k, :], Wb[("br", br, l)][k * 128:(k + 1) * 128, :], W=[wbr])
                wgd = [tile(e2, f"m_wgd{i}", [128, 8, 3, 128], BF16) for i in range(2)]
                gts = [tile(e2, f"m_gt{i}", [128, 512], F32) for i in range(2)]
                acc = tile(e2, "m_acc", [128, 512], F32)
                tmpm = tile(e2, "m_tmp", [128, 512], F32)
                srcs = [yhyT, yglaT, ypoolT]
                for dc in range(8):
                    wg_ = wgd[dc % 2]
                    for br in range(3):
                        c0 = GATE_OFF + br * 1024 + dc * 128
                        S.load(wg_[:, :, br, :], win[:, c0:c0 + 128].rearrange("(k p) c -> p k c", p=128), W=[wg_])
                    for t0 in range(0, Tn, 512):
                        tw = min(512, Tn - t0)
                        for br in range(3):
                            pj = S.ps()
                            for k in range(4):
                                S.op("pe", lambda e: e.matmul(pj[:, 0:tw], lhsT=wbr[:, br, k, dc * 128:(dc + 1) * 128], rhs=srcs[br][:, k, t0:t0 + tw], start=(k == 0), stop=(k == 3)),
                                     R=[wbr, srcs[br]], W=[pj])
                            pg = S.ps()
                            for k in range(8):
                                S.op("pe", lambda e: e.matmul(pg[:, 0:tw], lhsT=wg_[:, k, br, :], rhs=hxT[:, k, t0:t0 + tw], start=(k == 0), stop=(k == 7)), R=[wg_, hxT], W=[pg])
                            gt_ = gts[br % 2]
                            S.op("act", lambda e: e.activation(out=gt_[:, 0:tw], in_=pg[:, 0:tw], func=AF.Sigmoid, bias=ppl[:, 64 + br * 8 + dc:65 + br * 8 + dc]), R=[pg, ppl], W=[gt_])
                            if br == 0:
                                S.op("dve", lambda e: e.tensor_tensor(out=acc[:, 0:tw], in0=gt_[:, 0:tw], in1=pj[:, 0:tw], op=ALU.mult), R=[gt_, pj], W=[acc])
                            else:
                                S.op("dve", lambda e: e.tensor_tensor(out=tmpm[:, 0:tw], in0=gt_[:, 0:tw], in1=pj[:, 0:tw], op=ALU.mult), R=[gt_, pj], W=[tmpm])
                                if br == 1:
                                    S.op("pool", lambda e: e.tensor_tensor(out=acc[:, 0:tw], in0=acc[:, 0:tw], in1=tmpm[:, 0:tw], op=ALU.add), R=[acc, tmpm], W=[acc])
                                else:
                                    S.op("pool", lambda e: e.tensor_tensor(out=mergedT[:, dc, t0:t0 + tw], in0=acc[:, 0:tw], in1=tmpm[:, 0:tw], op=ALU.add), R=[acc, tmpm], W=[mergedT])
                S.barrier()
            chk(f"PC{l}_merge_{kind}{b}")
            with ExitStack() as e2:
                wo = tile(e2, "o_w", [128, 8, D], BF16)
                for k in range(8):
                    S.load(wo[:, k, :], Wb[("out", l)][k * 128:(k + 1) * 128, :], W=[wo])
                g1 = tile(e2, "o_g1", [128, D], F32)
                S.load(g1[:], mvap(ms, 2), W=[g1])
                xt = [tile(e2, f"o_xt{i}", [128, D], F32) for i in range(2)]
                t2 = [tile(e2, f"o_t2{i}", [128, D], F32) for i in range(2)]
                xdst = XS[b] if kind == "x" else CXS[b]
                for i in range(NT):
                    x_ = xt[i % 2]
                    t_ = t2[i % 2]
                    S.load(x_[:], xsrc[i * 128:(i + 1) * 128, :], W=[x_])
                    for hf in range(2):
                        p = S.ps()
                        for k in range(8):
                            S.op("pe", lambda e: e.matmul(p[:], lhsT=mergedT[:, k, i * 128:(i + 1) * 128], rhs=wo[:, k, hf * 512:(hf + 1) * 512], start=(k == 0), stop=(k == 7)),
                                 R=[mergedT, wo], W=[p])
                        S.op("dve", lambda e: e.tensor_tensor(out=t_[:, hf * 512:(hf + 1) * 512], in0=p[:], in1=g1[:, hf * 512:(hf + 1) * 512], op=ALU.mult), R=[p, g1], W=[t_])
                    S.op("pool", lambda e: e.tensor_tensor(out=t_[:], in0=t_[:], in1=x_[:], op=ALU.add), R=[t_, x_], W=[t_])
                    S.store(xdst[i * 128:(i + 1) * 128, :], t_[:], R=[t_])
                S.barrier()
            e3.close()
            chk(f"PC{l}_wout_{kind}{b}")
        S.barrier()

    chk(f"PC{l}_mix")
    with ExitStack() as es:
        wu = tile(es, "f_wu", [128, 8, 4096], BF16)
        wd = tile(es, "f_wd", [128, 32, D], BF16)
        for k in range(8):
            S.load(wu[:, k, :], Wb[("up", l)][k * 128:(k + 1) * 128, :], W=[wu])
        for k in range(32):
            S.load(wd[:, k, :], Wb[("down", l)][k * 128:(k + 1) * 128, :], W=[wd])
        ab = tile(es, "f_ab", [128, 3, D], F32)
        gfin = tile(es, "f_gfin", [128, D], F32)
        junk = tile(es, "f_junk", [128, D], F32)
        if last:
            bcast_row(es, gfin, norm_final.rearrange("(o d) -> o d", o=1), D, junk)
        xt = [tile(es, f"f_xt{i}", [128, D], F32) for i in range(4)]
        junk2 = tile(es, "f_junk2", [128, D], BF16)
        h2 = [tile(es, f"f_h2{i}", [128, D], BF16) for i in range(2)]
        h2T = tile(es, "f_h2T", [128, 8, 256], BF16)
        aT = tile(es, "f_aT", [128, 32, 256], BF16)
        rl = [tile(es, f"f_rl{i}", [128, 512], F32) for i in range(2)]
        t2s = [tile(es, f"f_t2{i}", [128, D], F32) for i in range(2)]
        ss = tile(es, "f_ss", [128, 2], F32)
        ssf = [tile(es, f"f_ssf{i}", [128, 2], F32) for i in range(2)]
        for (kind, b, Tn, xsrc, hdst, ms) in streams:
            if kind == "c" and l == 1:
                continue
            src = XS[b] if kind == "x" else CXS[b]
            S.load(ab[:, 0, :], mvap(ms, 4), W=[ab])
            S.load(ab[:, 1, :], mvap(ms, 3), W=[ab])
            S.load(ab[:, 2, :], mvap(ms, 5), W=[ab])
            blocks = list(range(0, Tn, 256))

            def ln1(bi):
                t0 = blocks[bi]
                for ii in range(2):
                    i = t0 // 128 + ii
                    x_ = xt[(bi % 2) * 2 + ii]
                    h_ = h2[ii]
                    S.load(x_[:], src[i * 128:(i + 1) * 128, :], W=[x_])
                    S.op("act", lambda e: e.activation(out=junk[:], in_=x_[:], func=AF.Square, accum_out=ss[:, 0:1]), R=[x_], W=[junk, ss])
                    S.op("act", lambda e: e.activation(out=ss[:, 1:2], in_=ss[:, 0:1], func=AF.Sqrt, scale=1.0 / D, bias=EPS), R=[ss], W=[ss])
                    S.op("dve", lambda e: e.reciprocal(out=ss[:, 1:2], in_=ss[:, 1:2]), R=[ss], W=[ss])
                    S.op("dve", lambda e: e.scalar_tensor_tensor(out=junk[:], in0=x_[:], scalar=ss[:, 1:2], in1=ab[:, 0, :], op0=ALU.mult, op1=ALU.mult), R=[x_, ss, ab], W=[junk])
                    S.op("dve", lambda e: e.tensor_tensor(out=h_[:], in0=junk[:], in1=ab[:, 1, :], op=ALU.add), R=[junk, ab], W=[h_])

            def ln2(bi):
                for ii in range(2):
                    h_ = h2[ii]
                    pb = S.psb()
                    for k in range(8):
                        S.op("pe", lambda e: e.transpose(out=pb[:, k * 128:(k + 1) * 128], in_=h_[:, k * 128:(k + 1) * 128], identity=ident_bf[:]), R=[h_, ident_bf], W=[pb])
                    S.op("act", lambda e: e.copy(out=h2T[:, :, ii * 128:(ii + 1) * 128], in_=pb[:].rearrange("p (k t) -> p k t", k=8)), R=[pb], W=[h2T])

            def up(bi):
                for f0 in range(0, 32, 2):
                    p = S.ps()
                    for ff in range(2):
                        f = f0 + ff
                        for k in range(8):
                            S.op("pe", lambda e: e.matmul(p[:, ff * 256:(ff + 1) * 256], lhsT=wu[:, k, f * 128:(f + 1) * 128], rhs=h2T[:, k, :], start=(k == 0), stop=(k == 7)),
                                 R=[wu, h2T], W=[p])
                    r_ = rl[(f0 // 2) % 2]
                    S.op("act", lambda e: e.activation(out=r_[:], in_=p[:], func=AF.Relu), R=[p], W=[r_])
                    S.op("pool" if (f0 // 2) % 2 else "dve", lambda e: e.tensor_tensor(out=aT[:, f0:f0 + 2, :].rearrange("p a b -> p (a b)"), in0=r_[:], in1=r_[:], op=ALU.mult), R=[r_], W=[aT])

            def down(bi):
                t0 = blocks[bi]
                for ii in range(2):
                    i = t0 // 128 + ii
                    x_ = xt[(bi % 2) * 2 + ii]
                    t2 = t2s[ii]
                    sf = ssf[ii]
                    for hf in range(2):
                        p = S.ps()
                        for f in range(32):
                            S.op("pe", lambda e: e.matmul(p[:], lhsT=aT[:, f, ii * 128:(ii + 1) * 128], rhs=wd[:, f, hf * 512:(hf + 1) * 512], start=(f == 0), stop=(f == 31)), R=[aT, wd], W=[p])
                        S.op("dve", lambda e: e.tensor_tensor(out=t2[:, hf * 512:(hf + 1) * 512], in0=p[:], in1=ab[:, 2, hf * 512:(hf + 1) * 512], op=ALU.mult), R=[p, ab], W=[t2])
                    S.op("pool", lambda e: e.tensor_tensor(out=t2[:], in0=t2[:], in1=x_[:], op=ALU.add), R=[t2, x_], W=[t2])
                    if last:
                        S.op("act", lambda e: e.activation(out=junk2[:], in_=t2[:], func=AF.Square, accum_out=sf[:, 0:1]), R=[t2], W=[junk2, sf])
                        S.op("act", lambda e: e.activation(out=sf[:, 1:2], in_=sf[:, 0:1], func=AF.Sqrt, scale=1.0 / D, bias=EPS), R=[sf], W=[sf])
                        S.op("dve", lambda e: e.reciprocal(out=sf[:, 1:2], in_=sf[:, 1:2]), R=[sf], W=[sf])
                        S.op("dve", lambda e: e.scalar_tensor_tensor(out=t2[:], in0=t2[:], scalar=sf[:, 1:2], in1=gfin[:], op0=ALU.mult, op1=ALU.mult), R=[t2, sf, gfin], W=[t2])
                        S.store(out[b, i * 128:(i + 1) * 128, :], t2[:], R=[t2])
                    else:
                        S.store(src[i * 128:(i + 1) * 128, :], t2[:], R=[t2])

            ln1(0)
            ln2(0)
            for bi in range(len(blocks)):
                up(bi)
                if bi + 1 < len(blocks):
                    ln1(bi + 1)
                down(bi)
                if bi + 1 < len(blocks):
                    ln2(bi + 1)
        S.barrier()


class StopBuild(Exception):
    pass


def build(stop_after=None, dbg=()):
    nc = bass.Bass("TRN2", target_bir_lowering=False)
    es0 = ExitStack()

    def din(name, shape, dt=F32):
        return nc.dram_tensor(name, list(shape), dt, kind="ExternalInput").ap()

    def dscr(name, shape, dt):
        kind = "ExternalOutput" if name in dbg else "Internal"
        return nc.dram_tensor(name, list(shape), dt, kind=kind).ap()

    x_in = din("x", [NB, T, D])
    ctx_in = din("ctx", [NB, TC, D])
    cT_in = din("cT", [128, 40])
    w_mod = din("w_mod", [2, D, 6144])
    b_mod = din("b_mod", [2, 6144])
    norm_mix = din("norm_mix", [2, D])
    norm_ffn = din("norm_ffn", [2, D])
    w_in = din("w_in", [2, D, N_IN])
    b_in = din("b_in", [2, N_IN])
    hy_w1 = din("hy_f_w1", [2, 17, 64])
    hy_w2 = din("hy_f_w2", [2, 64, 64])
    hy_w3 = din("hy_f_w3", [2, 64, 2048])
    wa_f = din("gla_wa_f", [2, 16, 256])
    ba_f = din("gla_ba_f", [2, 256])
    wa_b = din("gla_wa_b", [2, 16, 256])
    ba_b = din("gla_ba_b", [2, 256])
    gla_nw = din("gla_norm_w", [2, 128])
    pool_w = din("pool_w", [2, 512, 128])
    w_brs = [din("w_br_hy", [2, 512, D]), din("w_br_gla", [2, 512, D]), din("w_br_pool", [2, 512, D])]
    w_out = din("w_out", [2, D, D])
    w_up = din("w_up", [2, D, 4096])
    w_down = din("w_down", [2, 4096, D])
    norm_final = din("norm_final", [D])
    pp_in = din("pp", [2, 128, NPP])
    hy_skip = din("hy_skip", [2, 2, 512])
    C = {}
    for name, shape, dt in (("ident_bf", [128, 128], BF16), ("identJ_bf", [128, 128], BF16), ("ident_f", [128, 128], F32),
                            ("tri", [128, 4, 128], BF16), ("msk", [128, 2, 128], F32), ("pmat", [128, NU, 128], BF16),
                            ("rcnt_g", [128, 4, T], F32), ("rcnt_c", [128, 4, TC], F32),
                            ("zT_x", [32, 2 * T], F32), ("posn_x", [128, 2, 2 * T], F32),
                            ("zT_c", [32, 2 * TC], F32), ("posn_c", [128, 2, 2 * TC], F32),
                            ("F1x", [64, 64, 128], BF16), ("F1k", [64, 64, 128], BF16), ("F2", [64, 8, 128], BF16),
                            ("F3", [128, 128], BF16), ("F4", [64, 64, 2, 64], BF16), ("wtab", [64, 64, 512], F32), ("zT_f", [32, 4096], F32),
                            ("ones_f", [1, 128], F32), ("ones_bf", [1, 128], BF16)):
        C[name] = din("c_" + name, shape, dt)
    out = nc.dram_tensor("out", [NB, T, D], F32, kind="ExternalOutput").ap()

    Wb = {}
    for l in range(2):
        Wb[("mod", l)] = dscr(f"wb_mod{l}", [D, 6144], BF16)
        Wb[("in", l)] = dscr(f"wb_in{l}", [D, N_IN], BF16)
        for i in range(3):
            Wb[("br", i, l)] = dscr(f"wb_br{i}_{l}", [512, D], BF16)
        Wb[("out", l)] = dscr(f"wb_out{l}", [D, D], BF16)
        Wb[("up", l)] = dscr(f"wb_up{l}", [D, 4096], BF16)
        Wb[("down", l)] = dscr(f"wb_down{l}", [4096, D], BF16)
        Wb[("pw", l)] = dscr(f"wb_pw{l}", [512, 128], BF16)
    MODV = dscr("modv", [2, 5, 128, 6144], F32)
    XS = dscr("xs", [NB, T, D], F32)
    CXS = dscr("cxs", [NB, TC, D], F32)
    HXT = dscr("hxt", [NB, 128, 8 * T], BF16)
    HXTC = dscr("hxtc", [NB, 128, 8 * TC], BF16)
    U = dscr("u", [12, NB, T, 128], BF16)
    UC = dscr("uc", [12, NB, TC, 128], BF16)
    YT = dscr("yt", [NB, 512, T], BF16)
    YTC = dscr("ytc", [NB, 512, TC], BF16)
    KB = dscr("kb", [2, 512, 2 * T], BF16)
    KBC = dscr("kbc", [2, 512, 2 * TC], BF16)
    KF = dscr("kf", [2, 4, 2, 128, 8192], BF16)

    S = Sched(nc, es0)
    with es0:
        for i in range(8):
            S.ps_ring.append(Tl(es0.enter_context(nc.psum_tensor(f"ps{i}", [128, 512], F32))))
        for p_ in S.ps_ring:
            v_ = Tl(p_.t[:].bitcast(BF16))
            v_.b = p_.b
            S.psb_ring.append(v_)

        tcount = [0]

        def tile(es, name, shape, dt):
            tcount[0] += 1
            return Tl(es.enter_context(nc.sbuf_tensor(f"t{tcount[0]}_{name}", list(shape), dt)))

        ident_bf = tile(es0, "ident_bf", [128, 128], BF16)
        identJ_bf = tile(es0, "identJ_bf", [128, 128], BF16)
        ident_f = tile(es0, "ident_f", [128, 128], F32)
        ones_f = tile(es0, "ones_f", [1, 128], F32)
        ones_bf = tile(es0, "ones_bf", [1, 128], BF16)
        pp = [tile(es0, f"pp{l}", [128, NPP], F32) for l in range(2)]
        for tl_, src in ((ident_bf, C["ident_bf"]), (identJ_bf, C["identJ_bf"]), (ident_f, C["ident_f"]),
                         (ones_f, C["ones_f"]), (ones_bf, C["ones_bf"]), (pp[0], pp_in[0]), (pp[1], pp_in[1])):
            S.load(tl_[:], src, W=[tl_])

        engs3 = ["dve", "act", "pool"]

        def cast(k, out_ap, in_ap, R, W):
            if k == "act":
                S.op("act", lambda e: e.copy(out=out_ap, in_=in_ap), R=R, W=W)
            else:
                S.op(k, lambda e: e.tensor_copy(out=out_ap, in_=in_ap), R=R, W=W)

        def bcast_row(es, dst, src_row_ap, n, scratch_row):
            S.load(scratch_row[0:1, 0:n], src_row_ap, W=[scratch_row])
            for c0 in range(0, n, 512):
                cw = min(512, n - c0)
                p = S.ps()
                S.op("pe", lambda e: e.matmul(p[:, 0:cw], lhsT=ones_f[0:1, :], rhs=scratch_row[0:1, c0:c0 + cw], start=True, stop=True),
                     R=[ones_f, scratch_row], W=[p])
                S.op("act", lambda e: e.copy(out=dst[:, c0:c0 + cw], in_=p[:, 0:cw]), R=[p], W=[dst])

        with ExitStack() as es:
            wf = [tile(es, f"pw_f{i}", [128, N_IN], F32) for i in range(2)]
            wb = [tile(es, f"pw_b{i}", [128, N_IN], BF16) for i in range(2)]
            it = 0
            for l in range(2):
                jobs = [(w_mod[l], Wb[("mod", l)], D, 6144), (w_in[l], Wb[("in", l)], D, N_IN),
                        (w_brs[0][l], Wb[("br", 0, l)], 512, D), (w_brs[1][l], Wb[("br", 1, l)], 512, D),
                        (w_brs[2][l], Wb[("br", 2, l)], 512, D), (w_out[l], Wb[("out", l)], D, D),
                        (w_up[l], Wb[("up", l)], D, 4096), (w_down[l], Wb[("down", l)], 4096, D),
                        (pool_w[l], Wb[("pw", l)], 512, 128)]
                for src, dst, rows, cols in jobs:
                    for r0 in range(0, rows, 128):
                        a, b_ = wf[it % 2], wb[it % 2]
                        S.load(a[:, 0:cols], src[r0:r0 + 128, :], W=[a])
                        cast(engs3[it % 3], b_[:, 0:cols], a[:, 0:cols], [a], [b_])
                        S.store(dst[r0:r0 + 128, :], b_[:, 0:cols], R=[b_])
                        it += 1
            S.barrier()
        if stop_after == "PW":
            return nc

        with ExitStack() as es:
            cT = tile(es, "cT", [128, 40], F32)
            sT = tile(es, "sT", [128, 40], BF16)
            srep = tile(es, "srep", [128, 40, 128], BF16)
            wm = tile(es, "wm", [128, 8, 6144], BF16)
            brow_f = tile(es, "brow_f", [1, 6144], F32)
            brow = tile(es, "brow", [1, 6144], BF16)
            gmix = tile(es, "gmix", [128, D], F32)
            gffn = tile(es, "gffn", [128, D], F32)
            mv = [tile(es, f"mv{i}", [128, 6144], F32) for i in range(2)]
            S.load(cT[:], cT_in, W=[cT])
            S.op("act", lambda e: e.activation(out=sT[:], in_=cT[:], func=AF.Silu), R=[cT], W=[sT])
            S.op("dve", lambda e: e.tensor_copy(out=srep[:], in_=sT[:].unsqueeze(2).to_broadcast([128, 40, 128])), R=[sT], W=[srep])
            for l in range(2):
                for k in range(8):
                    S.load(wm[:, k, :], Wb[("mod", l)][k * 128:(k + 1) * 128, :], W=[wm])
                S.load(brow_f[:], b_mod[l:l + 1, :], W=[brow_f])
                S.op("dve", lambda e: e.tensor_copy(out=brow[:], in_=brow_f[:]), R=[brow_f], W=[brow])
                bcast_row(es, gmix, norm_mix[l:l + 1, :], D, brow_f)
                bcast_row(es, gffn, norm_ffn[l:l + 1, :], D, brow_f)
                for s in range(5):
                    m = mv[s % 2]
                    for cb in range(12):
                        p = S.ps()
                        for k in range(8):
                            S.op("pe", lambda e: e.matmul(p[:], lhsT=srep[:, k * 5 + s, :], rhs=wm[:, k, cb * 512:(cb + 1) * 512], start=(k == 0), stop=False),
                                 R=[srep, wm], W=[p])
                        S.op("pe", lambda e: e.matmul(p[:], lhsT=ones_bf[0:1, :], rhs=brow[0:1, cb * 512:(cb + 1) * 512], start=False, stop=True),
                             R=[ones_bf, brow], W=[p])
                        sp_ = cb // 2
                        hs = (cb % 2) * 512
                        dst = m[:, cb * 512:(cb + 1) * 512]
                        if sp_ == 1:
                            S.op("dve", lambda e: e.scalar_tensor_tensor(out=dst, in0=p[:], scalar=1.0, in1=gmix[:, hs:hs + 512], op0=ALU.add, op1=ALU.mult),
                                 R=[p, gmix], W=[m])
                        elif sp_ == 4:
                            S.op("dve", lambda e: e.scalar_tensor_tensor(out=dst, in0=p[:], scalar=1.0, in1=gffn[:, hs:hs + 512], op0=ALU.add, op1=ALU.mult),
                                 R=[p, gffn], W=[m])
                        else:
                            S.op("act", lambda e: e.copy(out=dst, in_=p[:]), R=[p], W=[m])
                    S.store(MODV[l, s], m[:], R=[m])
            S.barrier()
        if stop_after == "P0":
            return nc
        MV_B1, MV_A1, MV_G1, MV_B2, MV_A2, MV_G2 = 0, 1, 2, 3, 4, 5

        def mv_ap(l, s, j):
            return MODV[l, s][:, j * D:(j + 1) * D]


        def evac(idx, out_ap, in_ap, R, W):
            if idx % 2 == 0:
                S.op("act", lambda e: e.copy(out=out_ap, in_=in_ap), R=R, W=W)
            else:
                S.op("dve", lambda e: e.tensor_copy(out=out_ap, in_=in_ap), R=R, W=W)

        def fft_stage1(src, srcR, F1t, A1v, PAb):
            for g_ in range(16):
                p = S.ps()
                for q in range(4):
                    n2 = 4 * g_ + q
                    S.op("pe", lambda e: e.matmul(p[:, q * 128:(q + 1) * 128], lhsT=F1t[:, n2, :], rhs=src[:, n2, :], start=True, stop=True), R=[F1t] + srcR, W=[p])
                S.op("act", lambda e: e.copy(out=A1v[:, 4 * g_:4 * g_ + 4, :], in_=p[:].rearrange("p (a b) -> p a b", a=4)), R=[p], W=[PAb[g_]])

        def fft_transA(blk, A1v, PAb, Q, Qb):
            c0 = blk * 8
            for hf in range(2):
                p = S.ps()
                for q in range(4):
                    c = c0 + 4 * hf + q
                    S.op("pe", lambda e: e.matmul(p[0:64, q * 128:(q + 1) * 128], lhsT=A1v[:, :, c], rhs=ident_bf[:], start=True, stop=True), R=PAb + [ident_bf], W=[p])
                evac(0, Q[:, c0 + 4 * hf:c0 + 4 * hf + 4, :], p[0:64, :].rearrange("p (a b) -> p a b", a=4), [p], [Qb[blk]])

        def fft_stage2(blk, Q, Qb, F2t, i0):
            c0 = blk * 8
            pa = S.ps()
            pb_ = S.ps()
            for pp_, ia in ((pa, i0), (pb_, i0 + 2)):
                S.op("pe", lambda e: e.matmul(pp_[:], lhsT=F2t[:, ia, :], rhs=Q[:, c0:c0 + 8, 0:64], start=True, stop=False), R=[F2t, Qb[blk]], W=[pp_])
                S.op("pe", lambda e: e.matmul(pp_[:], lhsT=F2t[:, ia + 1, :], rhs=Q[:, c0:c0 + 8, 64:128], start=False, stop=True), R=[F2t, Qb[blk]], W=[pp_])
            return pa, pb_

        def spec_gen(es2, l, a2, w3):
            F1k = tile(es2, "s_F1k", [64, 64, 128], BF16)
            F2t = tile(es2, "s_F2", [64, 8, 128], BF16)
            a2b = tile(es2, "s_a2b", [64, 4096], BF16)
            w3b = tile(es2, "s_w3b", [64, 2048], BF16)
            kc = tile(es2, "s_kc", [64, 64, 256], BF16)
            wt = [tile(es2, f"s_wt{i}", [64, 4, 256], F32) for i in range(2)]
            skr = tile(es2, "s_skr", [1, 1024], F32)
            A1 = tile(es2, "s_A1", [128, 64, 128], BF16)
            A1T = tile(es2, "s_A1T", [64, 128, 128], BF16)
            kab = [tile(es2, f"s_kab{i}", [128, 2, 512], BF16) for i in range(2)]
            PAb = [Buf() for _ in range(16)]
            Qb = [Buf() for _ in range(16)]
            kcb = [Buf() for _ in range(16)]
            S.load(F1k[:], C["F1k"], W=[F1k])
            S.load(F2t[:], C["F2"], W=[F2t])
            S.load(skr[:], hy_skip[l].rearrange("(x o) c -> x (o c)", x=1), W=[skr])
            S.op("act", lambda e: e.copy(out=a2b[:], in_=a2[:]), R=[a2], W=[a2b])
            S.op("pool", lambda e: e.tensor_copy(out=w3b[:], in_=w3[:]), R=[w3], W=[w3b])
            it = 0
            wi = 0
            for o in range(2):
                for ch in range(2):
                    colf = 0 * 1024 + o * 512 + ch * 256
                    colb = 1 * 1024 + o * 512 + ch * 256
                    for g_ in range(16):
                        w_ = wt[wi % 2]
                        wi += 1
                        S.load(w_[:], C["wtab"][:, 4 * g_:4 * g_ + 4, ch * 256:(ch + 1) * 256], W=[w_])
                        for q2 in range(2):
                            pf = S.ps()
                            for q in range(2):
                                n2 = 4 * g_ + 2 * q2 + q
                                S.op("pe", lambda e: e.matmul(pf[0:64, q * 256:(q + 1) * 256], lhsT=a2b[:, n2::64], rhs=w3b[:, colf:colf + 256], start=True, stop=True), R=[a2b, w3b], W=[pf])
                            pg = S.ps()
                            for q in range(2):
                                n2 = 4 * g_ + 2 * q2 + q
                                S.op("pe", lambda e: e.matmul(pg[0:64, q * 256:(q + 1) * 256], lhsT=a2b[:, n2::64], rhs=w3b[:, colb:colb + 256], start=True, stop=True), R=[a2b, w3b], W=[pg])
                            n2a = 4 * g_ + 2 * q2
                            S.op("dve", lambda e: e.tensor_tensor(out=kc[0:32, n2a:n2a + 2, :], in0=pf[0:32, :].rearrange("p (a b) -> p a b", a=2), in1=w_[0:32, 2 * q2:2 * q2 + 2, :], op=ALU.mult),
                                 R=[pf, w_], W=[kcb[g_]])
                            S.op("dve", lambda e: e.tensor_tensor(out=kc[32:64, n2a:n2a + 2, :], in0=pg[32:64, :].rearrange("p (a b) -> p a b", a=2), in1=w_[32:64, 2 * q2:2 * q2 + 2, :], op=ALU.mult),
                                 R=[pg, w_], W=[kcb[g_]])
                            if n2a == 0:
                                S.op("dve", lambda e: e.tensor_tensor(out=kc[0:1, 0, :], in0=kc[0:1, 0, :], in1=skr[0:1, o * 512 + ch * 256:o * 512 + (ch + 1) * 256], op=ALU.add),
                                     R=[kcb[0], skr], W=[kcb[0]])
                    for c2 in range(2):
                        cc = 2 * ch + c2
                        fft_stage1(kc[:, :, c2 * 128:(c2 + 1) * 128], kcb, F1k, A1, PAb)
                        for blk_ in range(17):
                            if blk_ < 16:
                                fft_transA(blk_, A1, PAb, A1T, Qb)
                            if blk_ >= 1:
                                blk = blk_ - 1
                                pa, pb_ = fft_stage2(blk, A1T, Qb, F2t, 4)
                                kb_ = kab[it % 2]
                                it += 1
                                S.op("act", lambda e: e.copy(out=kb_[:, 0, :], in_=pa[:]), R=[pa], W=[kb_])
                                S.op("dve", lambda e: e.tensor_copy(out=kb_[:, 1, :], in_=pb_[:]), R=[pb_], W=[kb_])
                                S.store(KF[o, cc][:, :, blk * 512:(blk + 1) * 512].rearrange("t p n -> p t n"), kb_[:], R=[kb_])

        def hyena_fft(l):
            with ExitStack() as es2:
                F1x = tile(es2, "x_F1x", [64, 64, 128], BF16)
                F2t = tile(es2, "x_F2", [64, 8, 128], BF16)
                F3t = tile(es2, "x_F3", [128, 128], BF16)
                F4t = tile(es2, "x_F4", [64, 64, 2, 64], BF16)
                for t_, nm in ((F1x, "F1x"), (F2t, "F2"), (F3t, "F3"), (F4t, "F4")):
                    S.load(t_[:], C[nm], W=[t_])
                VY = tile(es2, "x_VY", [64, 64, 128], BF16)
                X = tile(es2, "x_X", [64, 64, 128], BF16)
                Z = tile(es2, "x_Z", [64, 64, 128], BF16)
                A1 = tile(es2, "x_A1", [128, 64, 128], BF16)
                Yt = tile(es2, "x_Yt", [128, 128, 64], BF16)
                G1 = tile(es2, "x_G1", [128, 128, 64], BF16)
                Q = tile(es2, "x_Q", [64, 128, 128], BF16)
                yT = tile(es2, "x_yT", [128, 2, T], BF16)
                kab = [tile(es2, f"x_kab{i}", [128, 2, 512], BF16) for i in range(4)]
                t1 = [tile(es2, f"x_t1{i}", [128, 512], F32) for i in range(2)]
                t2 = [tile(es2, f"x_t2{i}", [128, 512], F32) for i in range(2)]
                PAb = [Buf() for _ in range(16)]
                Qb = [Buf() for _ in range(16)]
                Ytb = [Buf() for _ in range(16)]
                Gb = [Buf() for _ in range(16)]
                VYb = [Buf() for _ in range(16)]
                Zb = [Buf() for _ in range(16)]
                cnt = [0]

                def conv(src, srcb, o, cc, dst, dstb):
                    fft_stage1(src, srcb, F1x, A1, PAb)
                    st = {}

                    def sA(blk):
                        fft_transA(blk, A1, PAb, Q, Qb)

                    def s2(blk):
                        c0 = blk * 8
                        kb_ = kab[cnt[0] % 4]
                        ta, tb = t1[cnt[0] % 2], t2[cnt[0] % 2]
                        cnt[0] += 1
                        S.load(kb_[:], KF[o, cc][:, :, blk * 512:(blk + 1) * 512].rearrange("t p n -> p t n"), W=[kb_])
                        pa, pb_ = fft_stage2(blk, Q, Qb, F2t, 0)
                        S.op("dve", lambda e: e.tensor_tensor(out=ta[:], in0=pa[:], in1=kb_[:, 0, :], op=ALU.mult), R=[pa, kb_], W=[ta])
                        S.op("dve", lambda e: e.tensor_tensor(out=tb[:], in0=pb_[:], in1=kb_[:, 1, :], op=ALU.mult), R=[pb_, kb_], W=[tb])
                        S.op("pool", lambda e: e.tensor_tensor(out=Yt[:, c0:c0 + 8, :].rearrange("p a b -> p (a b)"), in0=ta[:], in1=tb[:], op=ALU.add), R=[ta, tb], W=[Ytb[blk]])

                    def s3(blk):
                        c0 = blk * 8
                        p = S.ps()
                        S.op("pe", lambda e: e.matmul(p[:], lhsT=F3t[:], rhs=Yt[:, c0:c0 + 8, :].rearrange("p a b -> p (a b)"), start=True, stop=True), R=[F3t, Ytb[blk]], W=[p])
                        S.op("act", lambda e: e.copy(out=G1[:, c0:c0 + 8, :].rearrange("p a b -> p (a b)"), in_=p[:]), R=[p], W=[Gb[blk]])

                    def sB(blk):
                        c0 = blk * 8
                        for hf in range(2):
                            p = S.ps()
                            for q in range(4):
                                c = c0 + 4 * hf + q
                                S.op("pe", lambda e: e.matmul(p[0:64, q * 128:(q + 1) * 128], lhsT=G1[:, c, :], rhs=ident_bf[:], start=True, stop=True), R=[Gb[blk], ident_bf], W=[p])
                            evac(hf, Q[:, c0 + 4 * hf:c0 + 4 * hf + 4, :], p[0:64, :].rearrange("p (a b) -> p a b", a=4), [p], [Qb[blk]])

                    for i_ in range(16 + 6):
                        if i_ < 16:
                            sA(i_)
                        if 0 <= i_ - 2 < 16:
                            s2(i_ - 2)
                        if 0 <= i_ - 4 < 16:
                            s3(i_ - 4)
                        if 0 <= i_ - 6 < 16:
                            sB(i_ - 6)
                    for g_ in range(16):
                        p = S.ps()
                        for q in range(4):
                            n2 = 4 * g_ + q
                            S.op("pe", lambda e: e.matmul(p[0:64, q * 128:(q + 1) * 128], lhsT=F4t[:, n2, 0, :], rhs=Q[:, :, n2], start=True, stop=False), R=[F4t] + Qb, W=[p])
                            S.op("pe", lambda e: e.matmul(p[0:64, q * 128:(q + 1) * 128], lhsT=F4t[:, n2, 1, :], rhs=Q[:, :, 64 + n2], start=False, stop=True), R=[F4t] + Qb, W=[p])
                        S.op("dve", lambda e: e.tensor_tensor(out=dst[:, 4 * g_:4 * g_ + 4, :], in0=X[:, 4 * g_:4 * g_ + 4, :], in1=p[0:64, :].rearrange("p (a b) -> p a b", a=4), op=ALU.mult),
                             R=[X, p], W=[dstb[g_]])

                for cc in range(4):
                    for pr in range(2):
                        for ri in range(2):
                            b = 2 * pr + ri
                            S.load(VY[32 * ri:32 * ri + 32], U[8 + cc, b].rearrange("(n1 n2) c -> n1 n2 c", n2=64), W=VYb)
                        for ri in range(2):
                            S.load(X[32 * ri:32 * ri + 32], U[0 + cc, 2 * pr + ri].rearrange("(n1 n2) c -> n1 n2 c", n2=64), W=[X])
                        conv(VY, VYb, 0, cc, Z, Zb)
                        for ri in range(2):
                            S.load(X[32 * ri:32 * ri + 32], U[4 + cc, 2 * pr + ri].rearrange("(n1 n2) c -> n1 n2 c", n2=64), W=[X])
                        conv(Z, Zb, 1, cc, VY, VYb)
                        for n2q in range(0, 64, 16):
                            pb = S.psb()
                            for q in range(16):
                                S.op("pe", lambda e: e.transpose(out=pb[:, q * 64:(q + 1) * 64], in_=VY[:, n2q + q, :], identity=ident_bf[0:64, 0:64]), R=VYb + [ident_bf], W=[pb])
                            S.op("act", lambda e: e.copy(out=yT[:].rearrange("c b (n1 n2) -> c b n1 n2", n2=64)[:, :, :, n2q:n2q + 16],
                                                         in_=pb[:].rearrange("c (n2 b n1) -> c b n1 n2", n2=16, b=2)), R=[pb], W=[yT])
                        for ri in range(2):
                            S.store(YT[2 * pr + ri, cc * 128:(cc + 1) * 128, :], yT[:, ri, :], R=[yT])
                S.barrier()

        def chk(name):
            if name == stop_after:
                S.barrier()
                S.dead = True

        for l in range(2):
            streams = []
            for b in range(NB):
                if l == 0:
                    streams.append(("c", b, TC, ctx_in[b], HXTC[b], 4))
                else:
                    streams.append(("c", b, TC, CXS[b], HXTC[b], 4))
                streams.append(("x", b, T, x_in[b] if l == 0 else XS[b], HXT[b], b))

            with ExitStack() as es:
                w1 = tile(es, "f_w1", [32, 64], F32)
                w2 = tile(es, "f_w2", [64, 64], F32)
                w3 = tile(es, "f_w3", [64, 2048], F32)
                fb = tile(es, "f_fb", [64, 2], F32)
                S.op("dve", lambda e: e.memset(w1[:], 0.0), W=[w1])
                S.load(w1[0:17, :], hy_w1[l], W=[w1])
                S.load(w2[:], hy_w2[l], W=[w2])
                S.load(w3[:], hy_w3[l], W=[w3])
                ppl = pp[l]
                S.op("dve", lambda e: e.tensor_tensor(out=fb[:, 0:1], in0=ppl[0:64, 94:95], in1=ppl[0:64, 95:96], op=ALU.mult), R=[ppl], W=[fb])
                S.op("dve", lambda e: e.tensor_tensor(out=fb[:, 1:2], in0=ppl[0:64, 94:95], in1=ppl[0:64, 96:97], op=ALU.mult), R=[ppl], W=[fb])
                for kind, L_, zsrc, psrc, kdst in (("x", T, C["zT_f"], C["posn_x"], KB), ("c", TC, C["zT_c"], C["posn_c"], KBC)):
                    if kind == "c" and l == 1:
                        continue
                    NP = 2 * L_
                    with ExitStack() as es2:
                        zT = tile(es2, "f_zT", [32, NP], F32)
                        a1 = tile(es2, "f_a1", [64, NP], F32)
                        a2 = tile(es2, "f_a2", [64, NP], F32)
                        tA = tile(es2, "f_tA", [64, 512], F32)
                        tB = tile(es2, "f_tB", [64, 512], F32)
                        S.load(zT[:], zsrc, W=[zT])
                        if kind == "c":
                            posn = tile(es2, "f_posn", [128, 2, NP], F32)
                            win = [tile(es2, f"f_win{i}", [128, 512], F32) for i in range(2)]
                            krow = [tile(es2, f"f_krow{i}", [128, NP], F32) for i in range(2)]
                            kbf = [tile(es2, f"f_kbf{i}", [128, NP], BF16) for i in range(2)]
                            S.load(posn[:], psrc, W=[posn])

                        def sin_layer(dst, lhsT_ap, lhs_tl, src, fbcol):
                            for c0 in range(0, NP, 512):
                                p = S.ps()
                                S.op("pe", lambda e: e.matmul(p[0:64, :], lhsT=lhsT_ap, rhs=src[:, c0:c0 + 512], start=True, stop=True),
                                     R=[lhs_tl, src], W=[p])
                                S.op("dve", lambda e: e.tensor_scalar(out=tA[:], in0=p[0:64, :], scalar1=ppl[0:64, 94:95], scalar2=fb[:, fbcol:fbcol + 1], op0=ALU.mult, op1=ALU.add),
                                     R=[p, ppl, fb], W=[tA])
                                S.op("dve", lambda e: e.tensor_scalar(out=tB[:], in0=tA[:], scalar1=PI, scalar2=2 * PI, op0=ALU.is_gt, op1=ALU.mult), R=[tA], W=[tB])
                                S.op("dve", lambda e: e.tensor_tensor(out=tA[:], in0=tA[:], in1=tB[:], op=ALU.subtract), R=[tA, tB], W=[tA])
                                S.op("dve", lambda e: e.tensor_scalar(out=tB[:], in0=tA[:], scalar1=-PI, scalar2=2 * PI, op0=ALU.is_lt, op1=ALU.mult), R=[tA], W=[tB])
                                S.op("dve", lambda e: e.tensor_tensor(out=tA[:], in0=tA[:], in1=tB[:], op=ALU.add), R=[tA, tB], W=[tA])
                                S.op("act", lambda e: e.activation(out=dst[:, c0:c0 + 512], in_=tA[:], func=AF.Sin), R=[tA], W=[dst])

                        sin_layer(a1, w1[:, :], w1, zT, 0)
                        sin_layer(a2, w2[:, :], w2, a1, 1)
                        if kind == "x":
                            spec_gen(es2, l, a2, w3)
                            S.barrier()
                            continue
                        it = 0
                        for o in range(2):
                            for cc in range(4):
                                kr, kb_ = krow[it % 2], kbf[it % 2]
                                it += 1
                                if o == 1:
                                    S.op("pool", lambda e: e.memset(kr[:, 0:1], 0.0), W=[kr])
                                BS = min(512, L_)
                                for blk in range(NP // BS):
                                    first_half = blk * BS < L_
                                    if o == 0:
                                        dr = 0 if first_half else 1
                                        s0, d0, n = blk * BS, blk * BS, BS
                                    else:
                                        dr = 1 if first_half else 0
                                        if blk == 0:
                                            s0, d0, n = 0, 1, BS - 1
                                        else:
                                            s0, d0, n = blk * BS - 1, blk * BS, BS
                                    col0 = dr * 1024 + o * 512 + cc * 128
                                    p = S.ps()
                                    S.op("pe", lambda e: e.matmul(p[:, 0:n], lhsT=w3[:, col0:col0 + 128], rhs=a2[:, s0:s0 + n], start=True, stop=True),
                                         R=[w3, a2], W=[p])
                                    wt = win[blk % 2]
                                    S.op("act", lambda e: e.activation(out=wt[:, 0:n], in_=posn[:, o, d0:d0 + n], func=AF.Exp, scale=ppl[:, 98 + cc:99 + cc]),
                                         R=[posn, ppl], W=[wt])
                                    S.op("dve", lambda e: e.tensor_tensor(out=kr[:, d0:d0 + n], in0=p[:, 0:n], in1=wt[:, 0:n], op=ALU.mult), R=[p, wt], W=[kr])
                                zc = (L_ - 1) if o == 0 else L_
                                S.op("dve", lambda e: e.tensor_scalar(out=kr[:, zc:zc + 1], in0=kr[:, zc:zc + 1], scalar1=ppl[:, 102 + 4 * o + cc:103 + 4 * o + cc], scalar2=None, op0=ALU.add),
                                     R=[kr, ppl], W=[kr])
                                S.op("pool", lambda e: e.tensor_copy(out=kb_[:], in_=kr[:]), R=[kr], W=[kb_])
                                S.store(kdst[o, cc * 128:(cc + 1) * 128, :], kb_[:], R=[kb_])
                        S.barrier()
                S.barrier()
            if stop_after == f"PF{l}":
                return nc

            with ExitStack() as es:
                why = tile(es, "a_why", [128, 8, 1536], BF16)
                for k in range(8):
                    S.load(why[:, k, :], Wb[("in", l)][k * 128:(k + 1) * 128, 0:1536], W=[why])
                ppl = pp[l]
                hxTs = [tile(es, f"a_hxT{i}", [128, 8, T], BF16) for i in range(2)]
                abs_ = [tile(es, f"a_ab{i}", [128, 2, D], F32) for i in range(2)]
                xt = [tile(es, f"a_xt{i}", [128, D], F32) for i in range(2)]
                junk = tile(es, "a_junk", [128, D], F32)
                ss = tile(es, "a_ss", [128, 2], F32)
                hx = [tile(es, f"a_hx{i}", [128, D], BF16) for i in range(2)]
                raw = [tile(es, f"a_raw{i}", [128, T + 2], F32) for i in range(2)]
                tmp = tile(es, "a_tmp", [128, T], F32)
                ub = [tile(es, f"a_u{i}", [128, T], BF16) for i in range(2)]
                stg = [tile(es, f"a_stg{i}", [128, 16, 128], BF16) for i in range(2)]
                for r_ in raw:
                    S.op("pool", lambda e: e.memset(r_[:], 0.0), W=[r_])

                def gen_ln(si):
                    (kind, b, Tn, xsrc, hdst, ms) = streams[si]
                    hxT = hxTs[si % 2]
                    ab = abs_[si % 2]
                    NT = Tn // 128
                    S.load(ab[:, 0, :], mv_ap(l, ms, MV_A1), W=[ab])
                    S.load(ab[:, 1, :], mv_ap(l, ms, MV_B1), W=[ab])
                    for i in range(NT):
                        x_ = xt[i % 2]
                        h_ = hx[i % 2]
                        S.load(x_[:], xsrc[i * 128:(i + 1) * 128, :], W=[x_])
                        S.op("act", lambda e: e.activation(out=junk[:], in_=x_[:], func=AF.Square, accum_out=ss[:, 0:1]), R=[x_], W=[junk, ss])
                        S.op("act", lambda e: e.activation(out=ss[:, 1:2], in_=ss[:, 0:1], func=AF.Sqrt, scale=1.0 / D, bias=EPS), R=[ss], W=[ss])
                        S.op("dve", lambda e: e.reciprocal(out=ss[:, 1:2], in_=ss[:, 1:2]), R=[ss], W=[ss])
                        S.op("dve", lambda e: e.scalar_tensor_tensor(out=junk[:], in0=x_[:], scalar=ss[:, 1:2], in1=ab[:, 0, :], op0=ALU.mult, op1=ALU.mult),
                             R=[x_, ss, ab], W=[junk])
                        S.op("dve", lambda e: e.tensor_tensor(out=h_[:], in0=junk[:], in1=ab[:, 1, :], op=ALU.add), R=[junk, ab], W=[h_])
                        pb = S.psb()
                        for k in range(8):
                            S.op("pe", lambda e: e.transpose(out=pb[:, k * 128:(k + 1) * 128], in_=h_[:, k * 128:(k + 1) * 128], identity=ident_bf[:]),
                                 R=[h_, ident_bf], W=[pb])
                        S.op("act", lambda e: e.copy(out=hxT[:, :, i * 128:(i + 1) * 128], in_=pb[:].rearrange("p (k t) -> p k t", k=8)), R=[pb], W=[hxT])
                        yield
                    for k in range(8):
                        S.store(hdst[:, k * Tn:(k + 1) * Tn], hxT[:, k, 0:Tn], R=[hxT])
                    yield

                def gen_proj(si):
                    (kind, b, Tn, xsrc, hdst, ms) = streams[si]
                    hxT = hxTs[si % 2]
                    NT = Tn // 128
                    if kind == "c" and l == 1:
                        return
                    udst = U if kind == "x" else UC
                    for j in range(12):
                        rw = raw[j % 2]
                        u_ = ub[j % 2]
                        for t0 in range(0, Tn, 512):
                            tw = min(512, Tn - t0)
                            p = S.ps()
                            for k in range(8):
                                S.op("pe", lambda e: e.matmul(p[:, 0:tw], lhsT=why[:, k, j * 128:(j + 1) * 128], rhs=hxT[:, k, t0:t0 + tw], start=(k == 0), stop=(k == 7)),
                                     R=[why, hxT], W=[p])
                            S.op("act", lambda e: e.activation(out=rw[:, 1 + t0:1 + t0 + tw], in_=p[:, 0:tw], func=AF.Identity, bias=ppl[:, j:j + 1]), R=[p, ppl], W=[rw])
                        if Tn < T:
                            S.op("pool", lambda e: e.memset(rw[:, Tn + 1:Tn + 2], 0.0), W=[rw])
                        yield
                        S.op("act", lambda e: e.activation(out=tmp[:, 0:Tn], in_=rw[:, 1:Tn + 1], func=AF.Identity, scale=ppl[:, 24 + j:25 + j], bias=ppl[:, 48 + j:49 + j]),
                             R=[rw, ppl], W=[tmp])
                        S.op("dve", lambda e: e.scalar_tensor_tensor(out=tmp[:, 0:Tn], in0=rw[:, 0:Tn], scalar=ppl[:, 12 + j:13 + j], in1=tmp[:, 0:Tn], op0=ALU.mult, op1=ALU.add),
                             R=[rw, ppl, tmp], W=[tmp])
                        S.op("dve", lambda e: e.scalar_tensor_tensor(out=u_[:, 0:Tn], in0=rw[:, 2:Tn + 2], scalar=ppl[:, 36 + j:37 + j], in1=tmp[:, 0:Tn], op0=ALU.mult, op1=ALU.add),
                             R=[rw, ppl, tmp], W=[u_])
                        yield
                        sg = stg[j % 2]
                        for i0 in range(0, NT, 8):
                            ni = min(8, NT - i0)
                            pb = S.psb()
                            for ii in range(ni):
                                S.op("pe", lambda e: e.transpose(out=pb[:, ii * 128:(ii + 1) * 128], in_=u_[:, (i0 + ii) * 128:(i0 + ii + 1) * 128], identity=ident_bf[:]),
                                     R=[u_, ident_bf], W=[pb])
                            evac(i0 // 8, sg[:, i0:i0 + ni, :], pb[:, 0:ni * 128].rearrange("p (i c) -> p i c", i=ni), [pb], [sg])
                        S.store(udst[j, b].rearrange("(i p) c -> p i c", p=128), sg[:, 0:NT, :], R=[sg])
                        yield

                def run_rr(gs):
                    alive = [g_ for g_ in gs if g_ is not None]
                    while alive:
                        for g_ in list(alive):
                            try:
                                next(g_)
                            except StopIteration:
                                alive.remove(g_)

                run_rr([gen_ln(0)])
                for si in range(len(streams)):
                    nxt = gen_ln(si + 1) if si + 1 < len(streams) else None
                    run_rr([gen_proj(si), nxt])
                S.barrier()
            if stop_after == f"PA{l}":
                return nc

            with ExitStack() as es:
                hyena_fft(l)
                for kind, Tn, usrc, ksrc, ydst in (("c", TC, UC, KBC, YTC),):
                    if kind == "c" and l == 1:
                        continue
                    NBk = Tn // 128
                    GW = (2 * NBk - 1) * 128 + 1
                    with ExitStack() as es2:
                        vt = tile(es2, "h_v", [128, NB, NBk, 128], BF16)
                        x1 = tile(es2, "h_x1", [128, NB, NBk, 128], BF16)
                        x1r = tile(es2, "h_x1r", [128, NB, NBk, 128], BF16)
                        x2 = tile(es2, "h_x2", [128, NB, NBk, 128], BF16)
                        zt = tile(es2, "h_z", [128, NB, NBk, 128], BF16)
                        yh = tile(es2, "h_y", [128, NB, NBk, 128], BF16)
                        yT = tile(es2, "h_yT", [128, NB, Tn], BF16)
                        g = [tile(es2, f"h_g{i}", [128, GW], BF16) for i in range(3)]
                        gi = 0
                        NCOL = NB * NBk
                        for cc in range(4):
                            for b in range(NB):
                                S.load(vt[:, b], usrc[8 + cc, b].rearrange("(i p) c -> p i c", p=128), W=[vt])
                                S.load(x1[:, b], usrc[0 + cc, b].rearrange("(i p) c -> p i c", p=128), W=[x1])
                                S.load(x2[:, b], usrc[4 + cc, b].rearrange("(i p) c -> p i c", p=128), W=[x2])
                            x1f = x1[:].rearrange("p b i c -> p (b i c)")
                            x1rf = x1r[:].rearrange("p b i c -> p (b i c)")
                            for c0 in range(0, NCOL * 128, 512):
                                p = S.ps()
                                S.op("pe", lambda e: e.matmul(p[:], lhsT=identJ_bf[:], rhs=x1f[:, c0:c0 + 512], start=True, stop=True), R=[identJ_bf, x1], W=[p])
                                S.op("act", lambda e: e.copy(out=x1rf[:, c0:c0 + 512], in_=p[:]), R=[p], W=[x1r])
                            for o in range(2):
                                src_t = vt if o == 0 else zt
                                gate_t = x1r if o == 0 else x2
                                dst_t = zt if o == 0 else yh
                                CPB = 512 // NCOL if NCOL <= 512 else 1
                                for c8 in range(0, 128, CPB):
                                    p = S.ps()
                                    for ci in range(CPB):
                                        c = c8 + ci
                                        gt = g[gi % 3]
                                        gi += 1
                                        row = ksrc[o, cc * 128 + c]
                                        S.load(gt[:], bass.AP(row.tensor, row.offset, [[1, 128], [1, GW]]), W=[gt])
                                        lags = [0] + [d for d in range(-(NBk - 1), NBk) if d != 0]
                                        for n_, d in enumerate(lags):
                                            i_lo, i_hi = max(0, d), min(NBk - 1, NBk - 1 + d)
                                            ni = i_hi - i_lo + 1
                                            j_lo = i_lo - d
                                            g0 = (NBk - 1 - d) * 128 if o == 0 else (d + NBk - 1) * 128 + 1
                                            oap = p[:, ci * NCOL:(ci + 1) * NCOL].rearrange("p (b i) -> p b i", b=NB)[:, :, i_lo:i_lo + ni]
                                            S.op("pe", lambda e: e.matmul(oap, lhsT=gt[:, g0:g0 + 128], rhs=src_t[:, :, j_lo:j_lo + ni, c],
                                                                          start=(n_ == 0), stop=(n_ == len(lags) - 1)),
                                                 R=[gt, src_t], W=[p])
                                    S.op("dve", lambda e: e.tensor_tensor(out=dst_t[:].rearrange("p b i c -> p (b i) c")[:, :, c8:c8 + CPB],
                                                                          in0=gate_t[:].rearrange("p b i c -> p (b i) c")[:, :, c8:c8 + CPB],
                                                                          in1=p[:, 0:CPB * NCOL].rearrange("p (c n) -> p n c", c=CPB), op=ALU.mult),
                                         R=[gate_t, p], W=[dst_t])
                            for b in range(NB):
                                for i0 in range(0, NBk, 8):
                                    ni = min(8, NBk - i0)
                                    pb = S.psb()
                                    for ii in range(ni):
                                        S.op("pe", lambda e: e.transpose(out=pb[:, ii * 128:(ii + 1) * 128], in_=yh[:, b, i0 + ii, :], identity=ident_bf[:]),
                                             R=[yh, ident_bf], W=[pb])
                                    S.op("act", lambda e: e.copy(out=yT[:, b, i0 * 128:(i0 + ni) * 128], in_=pb[:, 0:ni * 128]), R=[pb], W=[yT])
                                S.store(ydst[b, cc * 128:(cc + 1) * 128, :], yT[:, b, :], R=[yT])
                        S.barrier()
            if stop_after == f"PH{l}":
                return nc
            if stop_after is not None and stop_after.startswith('PH'):
                pass
            try:
                PC(nc, S, l, streams, locals())
            except StopBuild:
                S.barrier()
                return nc
            if stop_after == f"PC{l}":
                return nc
        S.barrier()
    return nc


def _host_inputs(inp, core, consts, pps):
    b0 = core * NB
    m = {}
    m["x"] = np.ascontiguousarray(inp["x"][b0:b0 + NB])
    m["ctx"] = np.ascontiguousarray(inp["ctx"][b0:b0 + NB])
    call = np.concatenate([inp["c"][b0:b0 + NB], inp["c_ctx"][None]], 0)
    m["cT"] = np.ascontiguousarray(call.T.reshape(8, 128, 5).transpose(1, 0, 2).reshape(128, 40))
    for k in ["w_mod", "b_mod", "norm_mix", "norm_ffn", "w_in", "b_in", "hy_f_w1", "hy_f_w2", "hy_f_w3", "gla_wa_f", "gla_ba_f",
              "gla_wa_b", "gla_ba_b", "gla_norm_w", "hy_skip", "w_br_hy", "w_br_gla", "w_br_pool", "w_out", "w_up", "w_down", "norm_final"]:
        m[k] = inp[k]
    m["pool_w"] = inp["pool_w"].reshape(2, 512, 128)
    m["pp"] = pps
    for k, v in consts.items():
        m["c_" + k] = v
    return m


def kernel(**inputs):
    inp = {k: np.ascontiguousarray(np.asarray(v, dtype=np.float32)) for k, v in inputs.items()}
    consts = _consts()
    pps = np.stack([_pp_pack(inp, 0), _pp_pack(inp, 1)])
    nc = build()
    in_maps = [_host_inputs(inp, c, consts, pps) for c in range(NCORES)]
    res = run_bass_kernel_spmd(nc, in_maps, core_ids=list(range(NCORES)))
    outs = [np.asarray(r["out"], dtype=np.float32) for r in res.results]
    return np.concatenate(outs, axis=0)
```
